# Optimizing a Trainium2 kernel written in Bass

```python
import jax, jax.numpy as jnp
from jax import lax
import numpy as np

D_MODEL = 1024
BATCH = 8
SEQ = 4096
DEPTH = 1

PLE_DIM = 256
EPS = 1e-6
CHUNK = 64
GLA_HEADS = 4
GLA_DK = D_MODEL // (2 * GLA_HEADS)
GLA_DV = D_MODEL // GLA_HEADS
GLA_QK = GLA_HEADS * GLA_DK
GLA_V = GLA_HEADS * GLA_DV
GLA_RANK = 16
GLA_GATE_NORM = 16.0
ML_HEADS = 4
ML_DK = D_MODEL // (2 * ML_HEADS)
ML_DV = D_MODEL // ML_HEADS
ML_QK = ML_HEADS * ML_DK
ML_V = ML_HEADS * ML_DV
ML_CONV = 3
D_FF = 4 * D_MODEL
FFN_CONV = 3
IN_SPLITS = (GLA_QK, GLA_QK, GLA_V, GLA_V, 2 * GLA_RANK, 2 * ML_QK, ML_V, ML_V, 2 * ML_HEADS, 2 * ML_HEADS, D_MODEL, D_MODEL)
D_IN = sum(IN_SPLITS)

kernel_name = 'hybrid_gla_mlstm_bidir_block'


def rms_norm(x, w):
    xf = x.astype(jnp.float32)
    y = xf * lax.rsqrt(jnp.mean(xf * xf, axis=-1, keepdims=True) + EPS)
    return (y * w.astype(jnp.float32)).astype(x.dtype)


def head_rms_norm(o, w, n_heads):
    B_, S_, W_ = o.shape
    oh = o.reshape(B_, S_, n_heads, W_ // n_heads)
    oh = oh * lax.rsqrt(jnp.mean(oh * oh, axis=-1, keepdims=True) + EPS)
    return oh.reshape(B_, S_, W_) * w.astype(jnp.float32)


def dwconv_centred(x, w, b):
    c = x.shape[-1]
    y = lax.conv_general_dilated(x, w[:, None, :].astype(x.dtype), window_strides=(1,), padding='SAME',
                                 dimension_numbers=('NWC', 'WIO', 'NWC'), feature_group_count=c)
    return y + b.astype(x.dtype)


def to_chunks(t):
    B_, S_, H, d = t.shape
    return t.reshape(B_, S_ // CHUNK, CHUNK, H, d).transpose(1, 0, 3, 2, 4)


def gate_chunks(t):
    B_, S_, H = t.shape
    return t.reshape(B_, S_ // CHUNK, CHUNK, H).transpose(1, 0, 3, 2)


def from_chunks(o):
    N_, B_, H, C_, d = o.shape
    return o.transpose(1, 0, 3, 2, 4).reshape(B_, N_ * C_, H * d)


def flip(t):
    return jnp.flip(t, axis=1)


def gla_scan(q, k, v, log_a):
    B_, S_, H, dk = q.shape
    dv = v.shape[-1]
    causal = jnp.tril(jnp.ones((CHUNK, CHUNK), dtype=bool))

    def step(s_prev, xs):
        qc, kc, vc, gc = xs
        b = jnp.cumsum(gc, axis=2)
        qe = qc * jnp.exp(b)
        ke = kc * jnp.exp(-b)
        a = jnp.where(causal, jnp.einsum('bhid,bhjd->bhij', qe, ke), 0.0)
        o = jnp.einsum('bhij,bhjv->bhiv', a, vc) + jnp.einsum('bhid,bhdv->bhiv', qe, s_prev)
        b_end = b[:, :, -1:, :]
        kd = kc * jnp.exp(b_end - b)
        s_new = s_prev * jnp.exp(b_end[:, :, 0, :])[..., None] + jnp.einsum('bhjd,bhjv->bhdv', kd, vc)
        return s_new, o

    s0 = jnp.zeros((B_, H, dk, dv), jnp.float32)
    _, o = lax.scan(step, s0, (to_chunks(q), to_chunks(k), to_chunks(v), to_chunks(log_a)))
    return from_chunks(o)


def mlstm_scan(q, k, v, log_i, log_f):
    B_, S_, H, dk = q.shape
    dv = v.shape[-1]
    causal = jnp.tril(jnp.ones((CHUNK, CHUNK), dtype=bool))

    def step(carry, xs):
        c_st, n_st, m_st = carry
        qc, kc, vc, ic, fc = xs
        F = jnp.cumsum(fc, axis=-1)
        d_log = F[..., :, None] - F[..., None, :] + ic[..., None, :]
        d_log = jnp.where(causal, d_log, -jnp.inf)
        inter_log = F + m_st[..., None]
        m_i = jnp.maximum(inter_log, jnp.max(d_log, axis=-1))
        scores = jnp.einsum('bhid,bhjd->bhij', qc, kc) * jnp.exp(d_log - m_i[..., None])
        inter = jnp.exp(inter_log - m_i)
        num = jnp.einsum('bhij,bhjv->bhiv', scores, vc) + inter[..., None] * jnp.einsum('bhid,bhdv->bhiv', qc, c_st)
        den = jnp.sum(scores, axis=-1) + inter * jnp.einsum('bhid,bhd->bhi', qc, n_st)
        h = num / jnp.maximum(jnp.abs(den), jnp.exp(-m_i))[..., None]
        F_end = F[..., -1]
        end_log = F_end[..., None] - F + ic
        m_new = jnp.maximum(F_end + m_st, jnp.max(end_log, axis=-1))
        decay = jnp.exp(F_end + m_st - m_new)
        kw = kc * jnp.exp(end_log - m_new[..., None])[..., None]
        c_new = decay[..., None, None] * c_st + jnp.einsum('bhjd,bhjv->bhdv', kw, vc)
        n_new = decay[..., None] * n_st + jnp.sum(kw, axis=2)
        return (c_new, n_new, m_new), h

    init = (jnp.zeros((B_, H, dk, dv), jnp.float32), jnp.zeros((B_, H, dk), jnp.float32),
            jnp.full((B_, H), -jnp.inf, jnp.float32))
    _, h = lax.scan(step, init, (to_chunks(q), to_chunks(k), to_chunks(v), gate_chunks(log_i), gate_chunks(log_f)))
    return from_chunks(h)


def hybrid_mixer(h, w_in, w_gla_decay_f, b_gla_decay_f, w_gla_decay_b, b_gla_decay_b, gla_norm,
                 ml_conv_w, ml_conv_b, ml_igate_b, ml_fgate_b, ml_norm, w_out):
    B_, S_, _ = h.shape
    f32 = jnp.float32
    z = h @ w_in
    split_points = np.cumsum(IN_SPLITS)[:-1].tolist()
    gq, gk, gv, gr, glr, mqk, mv, mo, mi, mf, ga, gb = jnp.split(z, split_points, axis=-1)

    def heads(t, n):
        return t.reshape(B_, S_, n, -1).astype(f32)

    q = heads(gq, GLA_HEADS) * (GLA_DK ** -0.5)
    k = heads(gk, GLA_HEADS)
    v = heads(gv, GLA_HEADS)
    lr_f, lr_b = jnp.split(glr, 2, axis=-1)
    la_f = jax.nn.log_sigmoid((lr_f @ w_gla_decay_f + b_gla_decay_f).astype(f32)) / GLA_GATE_NORM
    la_b = jax.nn.log_sigmoid((lr_b @ w_gla_decay_b + b_gla_decay_b).astype(f32)) / GLA_GATE_NORM
    la_f = la_f.reshape(B_, S_, GLA_HEADS, GLA_DK)
    la_b = la_b.reshape(B_, S_, GLA_HEADS, GLA_DK)
    o_a = gla_scan(q, k, v, la_f) + flip(gla_scan(flip(q), flip(k), flip(v), flip(la_b)))
    y_a = head_rms_norm(o_a, gla_norm, GLA_HEADS) * jax.nn.silu(gr.astype(f32))

    qk = jax.nn.silu(dwconv_centred(mqk, ml_conv_w, ml_conv_b))
    mq, mk = jnp.split(qk, 2, axis=-1)
    q = heads(mq, ML_HEADS)
    k = heads(mk, ML_HEADS) * (ML_DK ** -0.5)
    v = heads(mv, ML_HEADS)
    i_pre = (mi + ml_igate_b).astype(f32)
    lf = jax.nn.log_sigmoid((mf + ml_fgate_b).astype(f32))
    i_f, i_b = jnp.split(i_pre, 2, axis=-1)
    lf_f, lf_b = jnp.split(lf, 2, axis=-1)
    h_b = mlstm_scan(q, k, v, i_f, lf_f) + flip(mlstm_scan(flip(q), flip(k), flip(v), flip(i_b), flip(lf_b)))
    y_b = head_rms_norm(h_b, ml_norm, ML_HEADS) * jax.nn.sigmoid(mo.astype(f32))

    y = jax.nn.sigmoid(ga.astype(f32)) * y_a + jax.nn.sigmoid(gb.astype(f32)) * y_b
    return y.astype(h.dtype) @ w_out


def conv_ffn(h, w_up, ffn_conv_w, ffn_conv_b, w_down):
    u = dwconv_centred(h @ w_up, ffn_conv_w, ffn_conv_b)
    gate, val = jnp.split(u, 2, axis=-1)
    return (jax.nn.gelu(gate, approximate=True) * val) @ w_down


def setup_inputs(seed: int = 0) -> dict:
    key = jax.random.key(seed)
    ks = iter(jax.random.split(key, 32))
    L = DEPTH

    def nrm(shape, scale):
        return scale * jax.random.normal(next(ks), shape, jnp.float32)

    def gain(shape):
        return 1.0 + 0.05 * jax.random.normal(next(ks), shape, jnp.float32)

    fbias = jnp.tile(jnp.linspace(3.0, 6.0, ML_HEADS, dtype=jnp.float32), 2)
    return {
        'x': nrm((BATCH, SEQ, D_MODEL), 1.0),
        'p': nrm((DEPTH, BATCH, SEQ, PLE_DIM), 1.0),
        'norm_mix_pre': gain((L, D_MODEL)),
        'norm_mix_post': gain((L, D_MODEL)),
        'w_in': nrm((L, D_MODEL, D_IN), D_MODEL ** -0.5),
        'w_gla_decay_f': nrm((L, GLA_RANK, GLA_QK), GLA_RANK ** -0.5),
        'b_gla_decay_f': nrm((L, GLA_QK), 0.1),
        'w_gla_decay_b': nrm((L, GLA_RANK, GLA_QK), GLA_RANK ** -0.5),
        'b_gla_decay_b': nrm((L, GLA_QK), 0.1),
        'gla_norm': gain((L, GLA_V)),
        'ml_conv_w': nrm((L, ML_CONV, 2 * ML_QK), ML_CONV ** -0.5),
        'ml_conv_b': nrm((L, 2 * ML_QK), 0.02),
        'ml_igate_b': nrm((L, 2 * ML_HEADS), 0.1),
        'ml_fgate_b': fbias[None, :] + nrm((L, 2 * ML_HEADS), 0.1),
        'ml_norm': gain((L, ML_V)),
        'w_out': nrm((L, D_MODEL, D_MODEL), D_MODEL ** -0.5),
        'norm_ffn_pre': gain((L, D_MODEL)),
        'norm_ffn_post': gain((L, D_MODEL)),
        'w_up': nrm((L, D_MODEL, 2 * D_FF), D_MODEL ** -0.5),
        'ffn_conv_w': nrm((L, FFN_CONV, 2 * D_FF), FFN_CONV ** -0.5),
        'ffn_conv_b': nrm((L, 2 * D_FF), 0.02),
        'w_down': nrm((L, D_FF, D_MODEL), D_FF ** -0.5),
        'w_ple_gate': nrm((L, D_MODEL, D_MODEL), D_MODEL ** -0.5),
        'w_ple_proj': nrm((L, PLE_DIM, D_MODEL), PLE_DIM ** -0.5),
        'norm_ple_post': gain((L, D_MODEL)),
    }


def reference(x, p, norm_mix_pre, norm_mix_post, w_in, w_gla_decay_f, b_gla_decay_f, w_gla_decay_b, b_gla_decay_b,
              gla_norm, ml_conv_w, ml_conv_b, ml_igate_b, ml_fgate_b, ml_norm, w_out, norm_ffn_pre, norm_ffn_post,
              w_up, ffn_conv_w, ffn_conv_b, w_down, w_ple_gate, w_ple_proj, norm_ple_post):
    for i in range(DEPTH):
        h = rms_norm(x, norm_mix_pre[i])
        mix = hybrid_mixer(h, w_in[i], w_gla_decay_f[i], b_gla_decay_f[i], w_gla_decay_b[i], b_gla_decay_b[i],
                           gla_norm[i], ml_conv_w[i], ml_conv_b[i], ml_igate_b[i], ml_fgate_b[i], ml_norm[i], w_out[i])
        x = x + rms_norm(mix, norm_mix_post[i])
        h = rms_norm(x, norm_ffn_pre[i])
        x = x + rms_norm(conv_ffn(h, w_up[i], ffn_conv_w[i], ffn_conv_b[i], w_down[i]), norm_ffn_post[i])
        gate = jax.nn.sigmoid(x @ w_ple_gate[i])
        e = p[i].astype(x.dtype) @ w_ple_proj[i]
        x = x + rms_norm(gate * e, norm_ple_post[i])
    return x
```

```python
import math
import numpy as np
import ml_dtypes
from contextlib import ExitStack
import concourse.bass as bass
import concourse.mybir as mybir
from concourse.bass_utils import run_bass_kernel_spmd

F32 = mybir.dt.float32
BF16 = mybir.dt.bfloat16
AF = mybir.ActivationFunctionType
ALU = mybir.AluOpType

D = 1024
DIN = 8240
H = 4
DK = 128
DV = 256
DFF = 4096
PLE = 256
EPS = 1e-6
LN_QS = math.log(DK ** -0.5)
CONV_CH = 1024
DEFER = True
DEFER_MODE = 3


class Buf:
    __slots__ = ("name", "w", "r", "sem", "cnt")

    def __init__(self, name):
        self.name = name
        self.w = None
        self.r = {}
        self.sem = None
        self.cnt = 0


class KB:
    def __init__(self, nc, es):
        self.nc = nc
        self.es = es
        self.eng = {"pe": nc.tensor, "act": nc.scalar, "dve": nc.vector, "pool": nc.gpsimd, "sp": nc.sync}
        self.esem = {}
        for e in ("pe", "act", "dve", "pool"):
            self.esem[e] = es.enter_context(nc.semaphore("es_" + e))
        self.ecnt = {e: 0 for e in self.esem}
        self.waited = {e: {} for e in self.eng}
        self.free_sems = []
        self.all_dma = {}
        self.phase_bufs = []
        self.nsem = 0
        self.ps = es.enter_context(nc.psum_tensor("ps", [128, 8, 512], F32))
        self.pb = [Buf("bank%d" % i) for i in range(8)]
        self.bank_i = 0

    def buf(self, name):
        b = Buf(name)
        self.phase_bufs.append(b)
        return b

    def bank(self):
        i = self.bank_i
        self.bank_i = (i + 1) % 8
        return self.ps[:, i, :], self.pb[i]

    def _wait(self, e, tok):
        key, sem, val = tok
        if e == "pe" and key == "pe":
            return
        w = self.waited[e]
        if w.get(id(sem), 0) >= val:
            return
        self.eng[e].wait_ge(sem, val)
        w[id(sem)] = val

    def _deps(self, e, reads, writes):
        for b in reads:
            if b.w is not None:
                self._wait(e, b.w)
        for b in writes:
            if b.w is not None:
                self._wait(e, b.w)
            for t in b.r.values():
                self._wait(e, t)

    def _mark(self, tok, reads, writes):
        for b in reads:
            b.r[tok[0]] = tok
        for b in writes:
            b.w = tok
            b.r = {}

    def op(self, e, fn, reads=(), writes=()):
        self._deps(e, reads, writes)
        ins = fn(self.eng[e])
        self.ecnt[e] += 1
        ins.then_inc(self.esem[e], 1)
        tok = (e, self.esem[e], self.ecnt[e])
        self._mark(tok, reads, writes)
        return tok

    def pe_begin(self, reads=(), writes=()):
        self._deps("pe", reads, writes)

    def pe_end(self, ins, reads=(), writes=()):
        self.ecnt["pe"] += 1
        ins.then_inc(self.esem["pe"], 1)
        tok = ("pe", self.esem["pe"], self.ecnt["pe"])
        self._mark(tok, reads, writes)

    def dma(self, q, out, in_, reads=(), writes=(), owner=None, **kw):
        skip = ("d", id(owner.sem)) if owner.sem is not None else None
        for b in reads:
            if b.w is not None:
                self._wait(q, b.w)
        for b in writes:
            if b.w is not None and b.w[0] != skip:
                self._wait(q, b.w)
            for t in b.r.values():
                self._wait(q, t)
        if owner.sem is None:
            if self.free_sems:
                owner.sem, owner.cnt = self.free_sems.pop()
            else:
                self.nsem += 1
                owner.sem = self.es.enter_context(self.nc.semaphore("ds%d" % self.nsem))
                owner.cnt = 0
        ins = self.eng[q].dma_start(out=out, in_=in_, **kw)
        owner.cnt += 16
        ins.then_inc(owner.sem, 16)
        self.all_dma[id(owner.sem)] = (owner.sem, owner.cnt)
        tok = (("d", id(owner.sem)), owner.sem, owner.cnt)
        self._mark(tok, reads, writes)

    def barrier(self):
        for e in self.eng:
            for e2 in self.esem:
                if e2 != e and self.ecnt[e2] > 0:
                    self._wait(e, (e2 + "_b", self.esem[e2], self.ecnt[e2]))
            for sem, cnt in self.all_dma.values():
                self._wait(e, ("db", sem, cnt))

    def end_phase(self):
        self.barrier()
        for b in self.phase_bufs:
            if b.sem is not None:
                self.free_sems.append((b.sem, b.cnt))
                b.sem = None
        self.phase_bufs = []
        for b in self.pb:
            b.w = None
            b.r = {}


class Ring:
    def __init__(self, k, es, name, shape, dtype, n):
        self.t = [es.enter_context(k.nc.sbuf_tensor("sr_%s%d_%d" % (name, i, id(es)), shape, dtype)) for i in range(n)]
        self.b = [k.buf("%s%d" % (name, i)) for i in range(n)]
        self.i = 0
        self.n = n

    def next(self):
        i = self.i
        self.i = (i + 1) % self.n
        return self.t[i], self.b[i]


def build(S):
    NT = S // 128
    NB = S // 512
    nc = bass.Bass("TRN2", target_bir_lowering=False)

    def din(name, shape, dt=F32):
        return nc.dram_tensor(name, list(shape), dt, kind="ExternalInput").ap()

    def dscr(name, shape, dt):
        return nc.dram_tensor(name, list(shape), dt, kind="Internal").ap()

    x_d = din("x", [S, D])
    p_d = din("p", [S, PLE])
    w_in = din("w_in", [D, DIN])
    wd_aug = din("wd_aug", [2, 17, 512])
    mlcw_d = din("mlcw", [128, 8, 3])
    mlcb_d = din("mlcb", [128, 8])
    gbias_d = din("gbias", [4, 4])
    ffcw_d = din("ffcw", [128, 64, 3])
    ffcb_d = din("ffcb", [128, 64])
    norms_d = din("norms", [7, D])
    w_out_d = din("w_out", [D, D])
    w_up_d = din("w_up", [D, 2 * DFF])
    w_down_d = din("w_down", [DFF, D])
    w_pg_d = din("w_pg", [D, D])
    w_pp_d = din("w_pp", [PLE, D])
    tri_d = din("tri", [128, 4, 128])
    identb_d = din("identb", [128, 128], BF16)
    identf_d = din("identf", [128, 128])
    sel_d = din("sel", [4, 4, 128])
    out_d = nc.dram_tensor("out", [S, D], F32, kind="ExternalOutput").ap()

    s_hT = dscr("s_hT", [NB, 128, 8, 512], BF16)
    s_mqk = dscr("s_mqk", [1024, S], F32)
    s_gate = dscr("s_gate", [16, S], F32)
    s_qe = [dscr("s_qe%d" % d, [NT, 128, 512], BF16) for d in range(2)]
    s_ke = [dscr("s_ke%d" % d, [NT, 128, 512], BF16) for d in range(2)]
    s_kd = [dscr("s_kd%d" % d, [NT, 128, 512], BF16) for d in range(2)]
    s_v = dscr("s_v", [NT, 128, 1024], BF16)
    s_mv = dscr("s_mv", [NT, 128, 1024], BF16)
    s_GA = dscr("s_GA", [NT, 128, 1024], BF16)
    s_GB = dscr("s_GB", [NT, 128, 1024], BF16)
    s_mq = dscr("s_mq", [NT, 128, 4, 128], BF16)
    s_mk = dscr("s_mk", [NT, 128, 4, 128], BF16)
    s_mktm = dscr("s_mktm", [NT, 128, 512], BF16)
    s_obA = dscr("s_obA", [NT, 128, 1024], F32)
    s_obB = dscr("s_obB", [NT, 128, 1024], F32)
    s_x1 = dscr("s_x1", [NT, 128, 1024], F32)
    s_h2T = dscr("s_h2T", [128, 8, S + 2], BF16)
    s_wup = dscr("s_wup", [D, 2 * DFF], BF16)
    s_wdn = dscr("s_wdn", [DFF, D], BF16)

    with ExitStack() as ges:
        k = KB(nc, ges)

        def GT(name, shape, dt):
            return ges.enter_context(nc.sbuf_tensor("sb_" + name, shape, dt))

        tri = GT("tri", [128, 4, 128], F32)
        identb = GT("identb", [128, 128], BF16)
        identf = GT("identf", [128, 128], F32)
        sel = GT("sel", [4, 4, 128], F32)
        neghalf = GT("neghalf", [128, 8], F32)
        EB = GT("EB", [128, 2, NT, 4], F32)
        w_tm = GT("w_tm", [128, 2, NT, 4], F32)
        thr_tm = GT("thr_tm", [128, 2, NT, 4], F32)
        dec_bc = GT("dec_bc", [128, 2, 4, NT], F32)
        junk = GT("junk", [128, 1024], BF16)
        cst_b = Buf("cst")
        nh_b = Buf("neghalf")
        EB_b = Buf("EB")
        wtm_b = Buf("wtm")
        thr_b = Buf("thr")
        dec_b = Buf("dec")
        k.dma("sp", tri[:], tri_d[:, :, :], writes=[cst_b], owner=cst_b)
        k.dma("sp", identb[:], identb_d[:, :], writes=[cst_b], owner=cst_b)
        k.dma("sp", identf[:], identf_d[:, :], writes=[cst_b], owner=cst_b)
        k.dma("sp", sel[:], sel_d[:, :, :], writes=[cst_b], owner=cst_b)
        k.op("pool", lambda e: e.memset(neghalf[:], -0.5), writes=[nh_b])

        def rstd_ops(ss_ap, ms_ap, rs_ap, stb, scale, n=1):
            k.op("pool", lambda e: e.tensor_scalar(out=ms_ap, in0=ss_ap, scalar1=scale, scalar2=EPS,
                                                   op0=ALU.mult, op1=ALU.add), reads=[stb], writes=[stb])
            k.op("pool", lambda e: e.tensor_tensor(out=rs_ap, in0=ms_ap, in1=neghalf[:, 0:n], op=ALU.pow),
                 reads=[stb, nh_b], writes=[stb])

        def rstd_act(ss_ap, ms_ap, rs_ap, stb, scale, n=1):
            k.op("act", lambda e: e.activation(out=ms_ap, in_=ss_ap, func=AF.Ln, scale=scale, bias=EPS),
                 reads=[stb], writes=[stb])
            k.op("act", lambda e: e.activation(out=rs_ap, in_=ms_ap, func=AF.Exp, scale=-0.5),
                 reads=[stb], writes=[stb])

        def transpose8(src, src_b, dst_view, dst_b, evac_eng):
            bank, bank_b = k.bank()
            bankb = bank.bitcast(BF16)
            k.pe_begin(reads=[src_b, cst_b], writes=[bank_b])
            ins = None
            for kk in range(8):
                ins = nc.tensor.transpose(out=bankb[:, kk * 128:(kk + 1) * 128], in_=src[:, kk * 128:(kk + 1) * 128],
                                          identity=identb[:])
            k.pe_end(ins, reads=[src_b, cst_b], writes=[bank_b])
            srcv = bankb.rearrange("p (k t) -> p k t", k=8)
            if evac_eng == "act":
                k.op("act", lambda e: e.activation(out=dst_view, in_=srcv, func=AF.Copy), reads=[bank_b], writes=[dst_b])
            else:
                k.op("dve", lambda e: e.tensor_copy(out=dst_view, in_=srcv), reads=[bank_b], writes=[dst_b])

        with ExitStack() as es:
            def T(name, shape, dt):
                return es.enter_context(nc.sbuf_tensor("sb_" + name + "_%d" % id(es), shape, dt))

            wA = T("wA", [128, 8, 2096], BF16)
            wA_b = k.buf("wA")
            wpre = T("wpre", [128, D], F32)
            wdt = T("wdt", [17, 2, 512], F32)
            pa_b = k.buf("pa_c")
            k.dma("sp", wpre[:], norms_d[0:1, :].partition_broadcast(128), writes=[pa_b], owner=pa_b)
            for d in range(2):
                k.dma("sp", wdt[:, d, :], wd_aug[d, :, :], writes=[pa_b], owner=pa_b)
            for kk in range(8):
                rows = slice(kk * 128, (kk + 1) * 128)
                k.dma("pool", wA[:, kk, 0:1024], w_in[rows, 0:1024], writes=[wA_b], owner=wA_b)
                k.dma("pool", wA[:, kk, 1024:2080], w_in[rows, 3072:4128], writes=[wA_b], owner=wA_b)
                k.dma("pool", wA[:, kk, 2080:2096], w_in[rows, 6176:6192], writes=[wA_b], owner=wA_b)
            x_ring = Ring(k, es, "xa", [128, D], F32, 4)
            st_ring = Ring(k, es, "sta", [128, 4], F32, 4)
            hb_ring = Ring(k, es, "hba", [128, D], BF16, 2)
            hT_ring = Ring(k, es, "hTa", [128, 8, 512], BF16, 2)
            qraw_ring = Ring(k, es, "qraw", [128, 4, 512], F32, 2)
            kraw_ring = Ring(k, es, "kraw", [128, 4, 512], F32, 2)
            mstg_ring = Ring(k, es, "mstg", [128, 512], F32, 3)
            gstg_ring = Ring(k, es, "gstg", [16, 512], F32, 2)
            lrT_ring = [Ring(k, es, "lrT%d" % d, [17, 512], F32, 2) for d in range(2)]
            for d in range(2):
                for i in range(2):
                    t_, b_ = lrT_ring[d].t[i], lrT_ring[d].b[i]
                    k.op("pool", lambda e: e.memset(t_[:], 1.0), writes=[b_])
            sp_ring = [Ring(k, es, "sp%d" % d, [128, 512], F32, 5) for d in range(2)]
            eb_ring = [Ring(k, es, "eb%d" % d, [128, 512], F32, 2) for d in range(2)]
            enb_ring = [Ring(k, es, "enb%d" % d, [128, 512], F32, 2) for d in range(2)]
            ekd_ring = [Ring(k, es, "ekd%d" % d, [128, 512], F32, 2) for d in range(2)]
            qst_ring = Ring(k, es, "qst", [128, 512], BF16, 4)
            kst_ring = Ring(k, es, "kst", [128, 512], BF16, 4)
            kdst_ring = Ring(k, es, "kdst", [128, 512], BF16, 4)

            def make_hT(b):
                hT, hT_b = hT_ring.next()
                for s in range(4):
                    t = 4 * b + s
                    xs, xs_b = x_ring.next()
                    k.dma("sp", xs[:], x_d[t * 128:(t + 1) * 128, :], writes=[xs_b], owner=xs_b)
                    st, st_b = st_ring.next()
                    k.op("act", lambda e: e.activation(out=junk[:], in_=xs[:], func=AF.Square, accum_out=st[:, 0:1]),
                         reads=[xs_b], writes=[st_b])
                    rstd_ops(st[:, 0:1], st[:, 1:2], st[:, 2:3], st_b, 1.0 / D)
                    hb, hb_b = hb_ring.next()
                    k.op("dve", lambda e: e.scalar_tensor_tensor(out=hb[:], in0=xs[:], scalar=st[:, 2:3], in1=wpre[:],
                                                                 op0=ALU.mult, op1=ALU.mult),
                         reads=[xs_b, st_b, pa_b], writes=[hb_b])
                    transpose8(hb, hb_b, hT[:, :, s * 128:(s + 1) * 128], hT_b, "dve")
                k.dma("pool", s_hT[b], hT[:], reads=[hT_b], owner=hT_b)
                return hT, hT_b

            hT_next = make_hT(0)
            for b in range(NB):
                hT, hT_b = hT_next

                def fm_group(c0, M):
                    bank, bank_b = k.bank()
                    k.pe_begin(reads=[wA_b, hT_b], writes=[bank_b])
                    ins = None
                    for kk in range(8):
                        ins = nc.tensor.matmul(bank[0:M, :], lhsT=wA[:, kk, c0:c0 + M], rhs=hT[:, kk, :],
                                               start=(kk == 0), stop=(kk == 7))
                    k.pe_end(ins, reads=[wA_b, hT_b], writes=[bank_b])
                    return bank, bank_b

                lrT = []
                for d in range(2):
                    bank, bank_b = fm_group(1024 + d * 16, 16)
                    lt, lt_b = lrT_ring[d].next()
                    k.op("dve", lambda e: e.tensor_copy(out=lt[0:16, :], in_=bank[0:16, :]), reads=[bank_b], writes=[lt_b])
                    lrT.append((lt, lt_b))
                spts = []
                for s in range(4):
                    ts = slice(s * 128, (s + 1) * 128)
                    row = []
                    for d in range(2):
                        lt, lt_b = lrT[d]
                        bank, bank_b = k.bank()
                        k.pe_begin(reads=[lt_b, pa_b], writes=[bank_b])
                        ins = nc.tensor.matmul(bank, lhsT=lt[0:17, ts], rhs=wdt[0:17, d, :], start=True, stop=True)
                        k.pe_end(ins, reads=[lt_b, pa_b], writes=[bank_b])
                        spt, spt_b = sp_ring[d].next()
                        k.op("act", lambda e: e.activation(out=spt[:], in_=bank, func=AF.Exp, scale=-1.0),
                             reads=[bank_b], writes=[spt_b])
                        k.op("act", lambda e: e.activation(out=spt[:], in_=spt[:], func=AF.Ln, bias=1.0),
                             reads=[spt_b], writes=[spt_b])
                        row.append((spt, spt_b))
                    spts.append(row)
                qraw, qraw_b = qraw_ring.next()
                kraw, kraw_b = kraw_ring.next()
                for h in range(4):
                    bank, bank_b = fm_group(h * 128, 128)
                    k.op("act", lambda e: e.activation(out=qraw[:, h, :], in_=bank, func=AF.Copy),
                         reads=[bank_b], writes=[qraw_b])
                for h in range(4):
                    bank, bank_b = fm_group(512 + h * 128, 128)
                    k.op("dve", lambda e: e.tensor_copy(out=kraw[:, h, :], in_=bank), reads=[bank_b], writes=[kraw_b])
                for c in range(8):
                    bank, bank_b = fm_group(1056 + c * 128, 128)
                    ms_, ms_b = mstg_ring.next()
                    if c % 2 == 0:
                        k.op("act", lambda e: e.activation(out=ms_[:], in_=bank, func=AF.Copy), reads=[bank_b], writes=[ms_b])
                    else:
                        k.op("dve", lambda e: e.tensor_copy(out=ms_[:], in_=bank), reads=[bank_b], writes=[ms_b])
                    k.dma("pool", s_mqk[c * 128:(c + 1) * 128, b * 512:(b + 1) * 512], ms_[:], reads=[ms_b], owner=ms_b)
                bank, bank_b = fm_group(2080, 16)
                gs_, gs_b = gstg_ring.next()
                k.op("dve", lambda e: e.tensor_copy(out=gs_[:], in_=bank[0:16, :]), reads=[bank_b], writes=[gs_b])
                k.dma("pool", s_gate[:, b * 512:(b + 1) * 512], gs_[:], reads=[gs_b], owner=gs_b)

                if b + 1 < NB:
                    hT_next = make_hT(b + 1)

                for s in range(4):
                    t = 4 * b + s
                    ts = slice(s * 128, (s + 1) * 128)
                    bk, bk_b = k.bank()
                    k.pe_begin(reads=[wA_b, hT_b], writes=[bk_b])
                    ins = None
                    for kk in range(8):
                        ins = nc.tensor.matmul(bk, lhsT=hT[:, kk, ts], rhs=wA[:, kk, 512:1024], start=(kk == 0), stop=(kk == 7))
                    k.pe_end(ins, reads=[wA_b, hT_b], writes=[bk_b])
                    pb2 = []
                    for d in range(2):
                        spt, spt_b = spts[s][d]
                        bank2, bank2_b = k.bank()
                        k.pe_begin(reads=[spt_b, cst_b], writes=[bank2_b])
                        for h in range(4):
                            ins = nc.tensor.matmul(bank2[:, h * 128:(h + 1) * 128], lhsT=spt[:, h * 128:(h + 1) * 128],
                                                   rhs=tri[:, d, :], start=True, stop=True)
                        k.pe_end(ins, reads=[spt_b, cst_b], writes=[bank2_b])
                        bank3, bank3_b = k.bank()
                        k.pe_begin(reads=[spt_b, cst_b], writes=[bank3_b])
                        ins = nc.tensor.matmul(bank3, lhsT=tri[:, 2 + d, :], rhs=spt[:], start=True, stop=True)
                        k.pe_end(ins, reads=[spt_b, cst_b], writes=[bank3_b])
                        pb2.append((bank2, bank2_b, bank3, bank3_b))
                    for d in range(2):
                        bank2, bank2_b, bank3, bank3_b = pb2[d]
                        eb, eb_b = eb_ring[d].next()
                        enb, enb_b = enb_ring[d].next()
                        k.op("act", lambda e: e.activation(out=eb[:], in_=bank2, func=AF.Exp, scale=-1.0 / 16.0),
                             reads=[bank2_b], writes=[eb_b])
                        k.op("act", lambda e: e.activation(out=enb[:], in_=bank2, func=AF.Exp, scale=1.0 / 16.0),
                             reads=[bank2_b], writes=[enb_b])
                        ekd, ekd_b = ekd_ring[d].next()
                        k.op("act", lambda e: e.activation(out=ekd[:], in_=bank3, func=AF.Exp, scale=-1.0 / 16.0),
                             reads=[bank3_b], writes=[ekd_b])
                        col = 127 if d == 0 else 0
                        ebv = eb[:].rearrange("p (h t) -> p h t", h=4)
                        enbv = enb[:].rearrange("p (h t) -> p h t", h=4)
                        k.op("pool", lambda e: e.tensor_copy(out=EB[:, d, t, :], in_=ebv[:, :, col]),
                             reads=[eb_b], writes=[EB_b])
                        qst, qst_b = qst_ring.next()
                        k.op("dve", lambda e: e.scalar_tensor_tensor(
                            out=qst[:].rearrange("p (h t) -> p h t", h=4), in0=qraw[:, :, ts], scalar=DK ** -0.5,
                            in1=ebv, op0=ALU.mult, op1=ALU.mult), reads=[qraw_b, eb_b], writes=[qst_b])
                        k.dma("pool", s_qe[d][t], qst[:], reads=[qst_b], owner=qst_b)
                        kst, kst_b = kst_ring.next()
                        k.op("pool", lambda e: e.tensor_tensor(
                            out=kst[:].rearrange("p (h t) -> p h t", h=4), in0=kraw[:, :, ts], in1=enbv, op=ALU.mult),
                            reads=[kraw_b, enb_b], writes=[kst_b])
                        k.dma("pool", s_ke[d][t], kst[:], reads=[kst_b], owner=kst_b)
                        kdst, kdst_b = kdst_ring.next()
                        k.op("dve", lambda e: e.tensor_tensor(out=kdst[:], in0=bk, in1=ekd[:], op=ALU.mult),
                             reads=[bk_b, ekd_b], writes=[kdst_b])
                        k.dma("pool", s_kd[d][t], kdst[:], reads=[kdst_b], owner=kdst_b)
            k.end_phase()

        with ExitStack() as es:
            def T(name, shape, dt):
                return es.enter_context(nc.sbuf_tensor("sb_" + name + "_%d" % id(es), shape, dt))

            wB = T("wB", [128, 8, 6144], BF16)
            wB_b = k.buf("wB")
            gnorm = T("gnorm", [128, D], F32)
            mnorm = T("mnorm", [128, D], F32)
            pb_b = k.buf("pb_c")
            k.dma("sp", gnorm[:], norms_d[2:3, :].partition_broadcast(128), writes=[pb_b], owner=pb_b)
            k.dma("sp", mnorm[:], norms_d[3:4, :].partition_broadcast(128), writes=[pb_b], owner=pb_b)
            wB_bs = [k.buf("wB%d" % i) for i in range(3)]
            for i, (c0, s0) in enumerate(((0, 1024), (2048, 4128), (4096, 6192))):
                for kk in range(8):
                    rows = slice(kk * 128, (kk + 1) * 128)
                    k.dma("pool", wB[:, kk, c0:c0 + 2048], w_in[rows, s0:s0 + 2048], writes=[wB_bs[i]], owner=wB_bs[i])
            hT_ring = Ring(k, es, "hTb", [128, 8, 512], BF16, 2)
            vst_ring = Ring(k, es, "vst", [128, D], BF16, 2)
            mvst_ring = Ring(k, es, "mvst", [128, D], BF16, 2)
            gst_ring = Ring(k, es, "gst", [128, D], BF16, 3)
            t1_ring = Ring(k, es, "t1", [128, D], F32, 2)
            t2_ring = Ring(k, es, "t2", [128, D], F32, 2)

            CH = min(CONV_CH, S)
            NCH = S // CH
            TPC = CH // 128
            cw = T("cw", [128, 8, 3], F32)
            cb = T("cb", [128, 8], F32)
            cv_b = k.buf("cv_c")
            k.dma("sp", cw[:], mlcw_d[:, :, :], writes=[cv_b], owner=cv_b)
            k.dma("sp", cb[:], mlcb_d[:, :], writes=[cv_b], owner=cv_b)
            pad_ring = Ring(k, es, "pad", [128, CH + 2], F32, 2)
            y_ring = Ring(k, es, "ycv", [128, CH], F32, 2)
            qo_ring = Ring(k, es, "qo", [128, CH], BF16, 2)
            mkc = T("mkc", [128, 4, CH], BF16)
            mkc_b = k.buf("mkc")
            mkst_ring = Ring(k, es, "mkst", [128, 2, 512], BF16, 2)

            def conv_unit(u):
                tc, c = u // 8, u % 8
                rows = slice(c * 128, (c + 1) * 128)
                pd, pd_b = pad_ring.next()
                k.dma("sp", pd[:, 1:CH + 1], s_mqk[rows, tc * CH:(tc + 1) * CH], writes=[pd_b], owner=pd_b)
                if tc == 0:
                    k.op("pool", lambda e: e.memset(pd[:, 0:1], 0.0), writes=[pd_b])
                else:
                    k.dma("sp", pd[:, 0:1], s_mqk[rows, tc * CH - 1:tc * CH], writes=[pd_b], owner=pd_b, allow_slow_non_contiguous=True)
                if tc == NCH - 1:
                    k.op("pool", lambda e: e.memset(pd[:, CH + 1:CH + 2], 0.0), writes=[pd_b])
                else:
                    k.dma("sp", pd[:, CH + 1:CH + 2], s_mqk[rows, (tc + 1) * CH:(tc + 1) * CH + 1], writes=[pd_b], owner=pd_b, allow_slow_non_contiguous=True)
                y, y_b = y_ring.next()
                k.op("dve", lambda e: e.tensor_scalar(out=y[:], in0=pd[:, 1:CH + 1], scalar1=cw[:, c, 1:2], scalar2=cb[:, c:c + 1],
                                                      op0=ALU.mult, op1=ALU.add), reads=[pd_b, cv_b], writes=[y_b])
                k.op("dve", lambda e: e.scalar_tensor_tensor(out=y[:], in0=pd[:, 0:CH], scalar=cw[:, c, 0:1], in1=y[:],
                                                             op0=ALU.mult, op1=ALU.add), reads=[pd_b, cv_b, y_b], writes=[y_b])
                k.op("dve", lambda e: e.scalar_tensor_tensor(out=y[:], in0=pd[:, 2:CH + 2], scalar=cw[:, c, 2:3], in1=y[:],
                                                             op0=ALU.mult, op1=ALU.add), reads=[pd_b, cv_b, y_b], writes=[y_b])
                n0 = tc * TPC
                if c < 4:
                    qo, qo_b = qo_ring.next()
                    k.op("act", lambda e: e.activation(out=qo[:], in_=y[:], func=AF.Silu), reads=[y_b], writes=[qo_b])
                    k.dma("pool", s_mq[n0:n0 + TPC, :, c, :].rearrange("n p t -> p n t"),
                          qo[:].rearrange("p (n t) -> p n t", t=128), reads=[qo_b], owner=qo_b)
                else:
                    h = c - 4
                    k.op("act", lambda e: e.activation(out=mkc[:, h, :], in_=y[:], func=AF.Silu), reads=[y_b], writes=[mkc_b])
                    k.dma("pool", s_mk[n0:n0 + TPC, :, h, :].rearrange("n p t -> p n t"),
                          mkc[:, h, :].rearrange("p (n t) -> p n t", t=128), reads=[mkc_b], owner=mkc_b)
                if c == 7:
                    for n2 in range(0, TPC, 2):
                        nn = min(2, TPC - n2)
                        bank, bank_b = k.bank()
                        bankb = bank.bitcast(BF16)
                        k.pe_begin(reads=[mkc_b, cst_b], writes=[bank_b])
                        ins = None
                        for j in range(nn):
                            for h in range(4):
                                ins = nc.tensor.transpose(out=bankb[:, (j * 4 + h) * 128:(j * 4 + h + 1) * 128],
                                                          in_=mkc[:, h, (n2 + j) * 128:(n2 + j + 1) * 128], identity=identb[:])
                        k.pe_end(ins, reads=[mkc_b, cst_b], writes=[bank_b])
                        ms_, ms_b = mkst_ring.next()
                        k.op("act", lambda e: e.activation(out=ms_[:, 0:nn, :],
                                                           in_=bankb[:, 0:nn * 512].rearrange("p (j c) -> p j c", c=512), func=AF.Copy),
                             reads=[bank_b], writes=[ms_b])
                        k.dma("pool", s_mktm[n0 + n2:n0 + n2 + nn].rearrange("n p c -> p n c"), ms_[:, 0:nn, :],
                              reads=[ms_b], owner=ms_b)

            NUNIT = NCH * 8
            units_done = [0]

            for b in range(NB):
                hT, hT_b = hT_ring.next()
                k.dma("sp", hT[:], s_hT[b], writes=[hT_b], owner=hT_b)
                for s in range(4):
                    t = 4 * b + s
                    ts = slice(s * 128, (s + 1) * 128)
                    target = ((t + 1) * NUNIT + NT - 1) // NT
                    while units_done[0] < min(target, NUNIT):
                        conv_unit(units_done[0])
                        units_done[0] += 1

                    def tm_group(c0):
                        res = []
                        for half in range(2):
                            bank, bank_b = k.bank()
                            wbb = wB_bs[c0 // 2048]
                            k.pe_begin(reads=[wbb, hT_b], writes=[bank_b])
                            ins = None
                            for kk in range(8):
                                ins = nc.tensor.matmul(bank, lhsT=hT[:, kk, ts],
                                                       rhs=wB[:, kk, c0 + half * 512:c0 + (half + 1) * 512],
                                                       start=(kk == 0), stop=(kk == 7))
                            k.pe_end(ins, reads=[wbb, hT_b], writes=[bank_b])
                            res.append((bank, bank_b))
                        return res

                    def act_evac(banks, dst, dst_b, func, scale=1.0):
                        for half in range(2):
                            bank, bank_b = banks[half]
                            k.op("act", lambda e: e.activation(out=dst[:, half * 512:(half + 1) * 512], in_=bank,
                                                               func=func, scale=scale), reads=[bank_b], writes=[dst_b])

                    banks = tm_group(0)
                    vst, vst_b = vst_ring.next()
                    act_evac(banks, vst, vst_b, AF.Copy)
                    k.dma("pool", s_v[t], vst[:], reads=[vst_b], owner=vst_b)
                    banks = tm_group(2048)
                    mvst, mvst_b = mvst_ring.next()
                    for half in range(2):
                        bank, bank_b = banks[half]
                        k.op("dve", lambda e: e.tensor_copy(out=mvst[:, half * 512:(half + 1) * 512], in_=bank),
                             reads=[bank_b], writes=[mvst_b])
                    k.dma("pool", s_mv[t], mvst[:], reads=[mvst_b], owner=mvst_b)
                    banks = tm_group(1024)
                    t1, t1_b = t1_ring.next()
                    act_evac(banks, t1, t1_b, AF.Silu)
                    banks = tm_group(4096)
                    t2, t2_b = t2_ring.next()
                    act_evac(banks, t2, t2_b, AF.Tanh, 0.5)
                    k.op("dve", lambda e: e.scalar_tensor_tensor(out=t1[:], in0=t2[:], scalar=1.0, in1=t1[:],
                                                                 op0=ALU.add, op1=ALU.mult), reads=[t1_b, t2_b], writes=[t1_b])
                    gst, gst_b = gst_ring.next()
                    k.op("dve", lambda e: e.scalar_tensor_tensor(out=gst[:], in0=t1[:], scalar=0.5, in1=gnorm[:],
                                                                 op0=ALU.mult, op1=ALU.mult), reads=[t1_b, pb_b], writes=[gst_b])
                    k.dma("pool", s_GA[t], gst[:], reads=[gst_b], owner=gst_b)
                    banks = tm_group(3072)
                    t1, t1_b = t1_ring.next()
                    act_evac(banks, t1, t1_b, AF.Tanh, 0.5)
                    banks = tm_group(5120)
                    t2, t2_b = t2_ring.next()
                    act_evac(banks, t2, t2_b, AF.Tanh, 0.5)
                    k.op("pool", lambda e: e.tensor_scalar(out=t1[:], in0=t1[:], scalar1=1.0, scalar2=0.25,
                                                           op0=ALU.add, op1=ALU.mult), reads=[t1_b], writes=[t1_b])
                    k.op("dve", lambda e: e.scalar_tensor_tensor(out=t1[:], in0=t2[:], scalar=1.0, in1=t1[:],
                                                                 op0=ALU.add, op1=ALU.mult), reads=[t1_b, t2_b], writes=[t1_b])
                    gst, gst_b = gst_ring.next()
                    k.op("pool", lambda e: e.tensor_tensor(out=gst[:], in0=t1[:], in1=mnorm[:], op=ALU.mult),
                         reads=[t1_b, pb_b], writes=[gst_b])
                    k.dma("pool", s_GB[t], gst[:], reads=[gst_b], owner=gst_b)
            k.end_phase()

        with ExitStack() as es:
            def T(name, shape, dt):
                return es.enter_context(nc.sbuf_tensor("sb_" + name + "_%d" % id(es), shape, dt))

            wc_b = Buf("wcast")
            for kk in range(8):
                k.dma("pool", s_wup[kk * 128:(kk + 1) * 128, :], w_up_d[kk * 128:(kk + 1) * 128, :], owner=wc_b)
            for kk in range(32):
                k.dma("pool", s_wdn[kk * 128:(kk + 1) * 128, :], w_down_d[kk * 128:(kk + 1) * 128, :], owner=wc_b)
            G = T("G", [4, 4, S], F32)
            gb = T("gb", [4, 4], F32)
            ones4 = T("ones4", [4, S], F32)
            Ssp = [T("Ssp%d" % d, [4, S], F32) for d in range(2)]
            uu = [T("uu%d" % d, [4, S], F32) for d in range(2)]
            Mg = [T("Mg%d" % d, [4, S], F32) for d in range(2)]
            dec = [T("dec%d" % d, [4, NT], F32) for d in range(2)]
            Gk_b = [k.buf("Gk%d" % i) for i in range(4)]
            gb_b = k.buf("gb")
            on_b = k.buf("ones4")
            g_bd = [k.buf("gwork%d" % d) for d in range(2)]
            for i in range(4):
                k.dma("sp", G[:, i, :], s_gate[4 * i:4 * i + 4, :], writes=[Gk_b[i]], owner=Gk_b[i])
            k.dma("sp", gb[:], gbias_d[:, :], writes=[gb_b], owner=gb_b)
            k.op("pool", lambda e: e.memset(ones4[:], 1.0), writes=[on_b])

            def rvd(d, ap):
                return ap[:, ::-1] if d == 1 else ap

            for d in range(2):
                ipre = G[:, d, :]
                k.op("dve", lambda e: e.tensor_scalar(out=ipre, in0=ipre, scalar1=gb[:, d:d + 1], scalar2=None, op0=ALU.add),
                     reads=[Gk_b[d], gb_b], writes=[Gk_b[d]])
            for d in range(2):
                fpre = G[:, 2 + d, :]
                k.op("dve", lambda e: e.tensor_scalar(out=fpre, in0=fpre, scalar1=gb[:, 2 + d:3 + d], scalar2=None, op0=ALU.add),
                     reads=[Gk_b[2 + d], gb_b], writes=[Gk_b[2 + d]])
            for d in range(2):
                fpre = G[:, 2 + d, :]
                k.op("act", lambda e: e.activation(out=fpre, in_=fpre, func=AF.Exp, scale=-1.0),
                     reads=[Gk_b[2 + d]], writes=[Gk_b[2 + d]])
            for d in range(2):
                fpre = G[:, 2 + d, :]
                k.op("act", lambda e: e.activation(out=fpre, in_=fpre, func=AF.Ln, bias=1.0),
                     reads=[Gk_b[2 + d]], writes=[Gk_b[2 + d]])
            for d in range(2):
                fpre = G[:, 2 + d, :]
                k.op("dve", lambda e: e.tensor_tensor_scan(out=rvd(d, Ssp[d][:]), data0=rvd(d, ones4[:]), data1=rvd(d, fpre),
                                                           initial=0.0, op0=ALU.mult, op1=ALU.add),
                     reads=[Gk_b[2 + d], on_b], writes=[g_bd[d]])
            for d in range(2):
                ipre = G[:, d, :]
                k.op("dve", lambda e: e.tensor_tensor(out=uu[d][:], in0=ipre, in1=Ssp[d][:], op=ALU.add),
                     reads=[Gk_b[d], g_bd[d]], writes=[g_bd[d]])
            for d in range(2):
                k.op("dve", lambda e: e.tensor_tensor_scan(out=rvd(d, Mg[d][:]), data0=rvd(d, ones4[:]), data1=rvd(d, uu[d][:]),
                                                           initial=-1e30, op0=ALU.mult, op1=ALU.max),
                     reads=[g_bd[d], on_b], writes=[g_bd[d]])
            views = []
            for d in range(2):
                endc = 127 if d == 0 else 0
                Mgv = Mg[d][:].rearrange("p (n t) -> p n t", t=128)
                Mnb = Mgv[:, :, endc:endc + 1].to_broadcast([4, NT, 128])
                uv = uu[d][:].rearrange("p (n t) -> p n t", t=128)
                sv = Ssp[d][:].rearrange("p (n t) -> p n t", t=128)
                views.append((Mgv, Mnb, uv, sv, endc))
            for d in range(2):
                Mgv, Mnb, uv, sv, endc = views[d]
                k.op("dve", lambda e: e.tensor_tensor(out=uv, in0=uv, in1=Mnb, op=ALU.subtract), reads=[g_bd[d]], writes=[g_bd[d]])
            for d in range(2):
                k.op("act", lambda e: e.activation(out=uu[d][:], in_=uu[d][:], func=AF.Exp, bias=LN_QS),
                     reads=[g_bd[d]], writes=[g_bd[d]])
            for d in range(2):
                Mgv, Mnb, uv, sv, endc = views[d]
                k.op("dve", lambda e: e.tensor_tensor(out=sv, in0=sv, in1=Mnb, op=ALU.subtract), reads=[g_bd[d]], writes=[g_bd[d]])
            for d in range(2):
                k.op("act", lambda e: e.activation(out=Ssp[d][:], in_=Ssp[d][:], func=AF.Exp), reads=[g_bd[d]], writes=[g_bd[d]])
            for d in range(2):
                Mgv, Mnb, uv, sv, endc = views[d]
                k.op("dve", lambda e: e.memset(dec[d][:], 0.0), reads=[g_bd[d]], writes=[g_bd[d]])
                if NT > 1:
                    Mn2 = Mgv[:, :, endc]
                    if d == 0:
                        k.op("dve", lambda e: e.tensor_tensor(out=dec[d][:, 1:NT], in0=Mn2[:, 0:NT - 1], in1=Mn2[:, 1:NT],
                                                              op=ALU.subtract), reads=[g_bd[d]], writes=[g_bd[d]])
                        k.op("act", lambda e: e.activation(out=dec[d][:, 1:NT], in_=dec[d][:, 1:NT], func=AF.Exp),
                             reads=[g_bd[d]], writes=[g_bd[d]])
                    else:
                        k.op("dve", lambda e: e.tensor_tensor(out=dec[d][:, 0:NT - 1], in0=Mn2[:, 1:NT], in1=Mn2[:, 0:NT - 1],
                                                              op=ALU.subtract), reads=[g_bd[d]], writes=[g_bd[d]])
                        k.op("act", lambda e: e.activation(out=dec[d][:, 0:NT - 1], in_=dec[d][:, 0:NT - 1], func=AF.Exp),
                             reads=[g_bd[d]], writes=[g_bd[d]])
            for d in range(2):
                for src, dst, dst_b in ((uu[d], w_tm, wtm_b), (Ssp[d], thr_tm, thr_b)):
                    bank, bank_b = k.bank()
                    k.pe_begin(reads=[g_bd[d], cst_b], writes=[bank_b])
                    ins = None
                    for n in range(NT):
                        ins = nc.tensor.matmul(bank[:, n * 4:(n + 1) * 4], lhsT=src[:, n * 128:(n + 1) * 128],
                                               rhs=identf[0:4, 0:4], start=True, stop=True)
                    k.pe_end(ins, reads=[g_bd[d], cst_b], writes=[bank_b])
                    k.op("dve", lambda e: e.tensor_copy(out=dst[:, d, :, :],
                                                        in_=bank[:, 0:NT * 4].rearrange("p (n h) -> p n h", h=4)),
                         reads=[bank_b], writes=[dst_b])
                bank, bank_b = k.bank()
                k.pe_begin(reads=[g_bd[d], cst_b], writes=[bank_b])
                for h in range(4):
                    ins = nc.tensor.matmul(bank[:, h * NT:(h + 1) * NT], lhsT=sel[:, h, :], rhs=dec[d][:], start=True, stop=True)
                k.pe_end(ins, reads=[g_bd[d], cst_b], writes=[bank_b])
                k.op("dve", lambda e: e.tensor_copy(out=dec_bc[:, d, :, :],
                                                    in_=bank[:, 0:4 * NT].rearrange("p (h n) -> p h n", h=4)),
                     reads=[bank_b], writes=[dec_b])

            k.end_phase()

        with ExitStack() as es:
            def T(name, shape, dt):
                return es.enter_context(nc.sbuf_tensor("sb_" + name + "_%d" % id(es), shape, dt))

            Sst = T("Sst", [128, 4, 256], F32)
            Sbf = [T("Sbf%d" % i, [128, 4, 256], BF16) for i in range(2)]
            Cst = T("Cst", [128, 4, 257], F32)
            Cbf = [T("Cbf%d" % i, [128, 4, 257], BF16) for i in range(2)]
            S_b = [k.buf("S%d" % h) for h in range(4)]
            Sbf_b = [k.buf("Sbf%d" % i) for i in range(2)]
            C_b = [k.buf("C%d" % h) for h in range(4)]
            Cbf_b = [k.buf("Cbf%d" % i) for i in range(2)]
            NL = 3
            ld = {}
            for nm, shp in (("qe", [128, 512]), ("ke", [128, 512]), ("kd", [128, 512]), ("v", [128, 1024]),
                            ("mq", [128, 512]), ("mk", [128, 512]), ("mktm", [128, 512]), ("mv", [128, 1024])):
                ld[nm] = [T("ld_%s%d" % (nm, i), shp, BF16) for i in range(NL)]
            ld_b = [k.buf("ld%d" % i) for i in range(NL)]
            ld_i = [0]
            vw_ring = Ring(k, es, "vw", [128, 4, 257], BF16, 3)
            at_ring = Ring(k, es, "at", [128, 512], BF16, 4)
            qd_ring = Ring(k, es, "qd", [128, 128], BF16, 3)
            dn_ring = Ring(k, es, "dn", [128, 16], F32, 4)
            oA_ring = Ring(k, es, "oA", [128, D], F32, 2)
            hB_ring = Ring(k, es, "hB", [128, D], F32, 2)
            obA_t = [k.buf("obA_t%d" % n) for n in range(NT)]
            obB_t = [k.buf("obB_t%d" % n) for n in range(NT)]
            for d in (1, 0):
                order = list(range(NT)) if d == 0 else list(range(NT - 1, -1, -1))
                maskb = tri[:, d:d + 1, :].to_broadcast([128, 4, 128])
                first = True
                def issue_loads(si2):
                    n2 = order[si2]
                    li = ld_i[0]
                    ld_i[0] = (li + 1) % NL
                    lb2 = ld_b[li]
                    L2 = {nm: ld[nm][li] for nm in ld}
                    k.dma("sp", L2["qe"][:], s_qe[d][n2], writes=[lb2], owner=lb2)
                    k.dma("sp", L2["ke"][:], s_ke[d][n2], writes=[lb2], owner=lb2)
                    k.dma("sp", L2["kd"][:], s_kd[d][n2], writes=[lb2], owner=lb2)
                    k.dma("sp", L2["v"][:], s_v[n2], writes=[lb2], owner=lb2)
                    k.dma("sp", L2["mq"][:], s_mq[n2].rearrange("p h t -> p (h t)"), writes=[lb2], owner=lb2)
                    k.dma("sp", L2["mk"][:], s_mk[n2].rearrange("p h t -> p (h t)"), writes=[lb2], owner=lb2)
                    k.dma("sp", L2["mktm"][:], s_mktm[n2], writes=[lb2], owner=lb2)
                    k.dma("sp", L2["mv"][:], s_mv[n2], writes=[lb2], owner=lb2)
                    vw2, vw2_b = vw_ring.next()
                    k.op("pool", lambda e: e.tensor_tensor(
                        out=vw2[:, :, 0:256], in0=L2["mv"][:].rearrange("p (h c) -> p h c", h=4),
                        in1=w_tm[:, d, n2, :].unsqueeze(2).to_broadcast([128, 4, 256]), op=ALU.mult),
                        reads=[lb2, wtm_b], writes=[vw2_b])
                    k.op("pool", lambda e: e.tensor_copy(out=vw2[:, :, 256], in_=w_tm[:, d, n2, :]), reads=[wtm_b], writes=[vw2_b])
                    return (lb2, L2, vw2, vw2_b)

                pending = issue_loads(0)
                for si, n in enumerate(order):
                    nxt = order[si + 1] if si + 1 < NT else None
                    cur = si % 2
                    prv = 1 - cur
                    lb, L, vw, vw_b = pending
                    if nxt is not None:
                        pending = issue_loads(si + 1)
                    oA, oA_b = oA_ring.next()
                    hB, hB_b = hB_ring.next()
                    s1, s1_b = k.bank()
                    k.pe_begin(reads=[lb], writes=[s1_b])
                    for h in range(4):
                        hk = slice(h * 128, (h + 1) * 128)
                        ins = nc.tensor.matmul(s1[:, hk], lhsT=L["ke"][:, hk], rhs=L["qe"][:, hk], start=True, stop=True)
                    k.pe_end(ins, reads=[lb], writes=[s1_b])
                    s2, s2_b = k.bank()
                    k.pe_begin(reads=[lb], writes=[s2_b])
                    for h in range(4):
                        hk = slice(h * 128, (h + 1) * 128)
                        ins = nc.tensor.matmul(s2[:, hk], lhsT=L["mk"][:, hk], rhs=L["mq"][:, hk], start=True, stop=True)
                    k.pe_end(ins, reads=[lb], writes=[s2_b])
                    ub = []
                    for g2 in range(2):
                        bu, bu_b = k.bank()
                        k.pe_begin(reads=[lb], writes=[bu_b])
                        for hh in range(2):
                            h = 2 * g2 + hh
                            ins = nc.tensor.matmul(bu[:, hh * 256:(hh + 1) * 256], lhsT=L["kd"][:, h * 128:(h + 1) * 128],
                                                   rhs=L["v"][:, h * 256:(h + 1) * 256], start=True, stop=True)
                        k.pe_end(ins, reads=[lb], writes=[bu_b])
                        ub.append((bu, bu_b))
                    u2 = []
                    for h in range(4):
                        bu2, bu2_b = k.bank()
                        k.pe_begin(reads=[lb, vw_b], writes=[bu2_b])
                        ins = nc.tensor.matmul(bu2[:, 0:257], lhsT=L["mktm"][:, h * 128:(h + 1) * 128], rhs=vw[:, h, :],
                                               start=True, stop=True)
                        k.pe_end(ins, reads=[lb, vw_b], writes=[bu2_b])
                        u2.append((bu2, bu2_b))
                    at1, at1_b = at_ring.next()
                    k.op("dve", lambda e: e.tensor_tensor(out=at1[:].rearrange("p (h t) -> p h t", h=4),
                                                          in0=s1.rearrange("p (h t) -> p h t", h=4), in1=maskb, op=ALU.mult),
                         reads=[s1_b, cst_b], writes=[at1_b])
                    at2, at2_b = at_ring.next()
                    k.op("dve", lambda e: e.tensor_tensor(out=at2[:].rearrange("p (h t) -> p h t", h=4),
                                                          in0=s2.rearrange("p (h t) -> p h t", h=4), in1=maskb, op=ALU.mult),
                         reads=[s2_b, cst_b], writes=[at2_b])
                    for h in range(4):
                        bu, bu_b = ub[h // 2]
                        src = bu[:, (h % 2) * 256:(h % 2 + 1) * 256]
                        if first:
                            k.op("dve", lambda e: e.tensor_copy(out=Sst[:, h, :], in_=src), reads=[bu_b], writes=[S_b[h]])
                        else:
                            k.op("dve", lambda e: e.scalar_tensor_tensor(out=Sst[:, h, :], in0=Sst[:, h, :],
                                                                         scalar=EB[:, d, n, h:h + 1], in1=src,
                                                                         op0=ALU.mult, op1=ALU.add),
                                 reads=[bu_b, S_b[h], EB_b], writes=[S_b[h]])
                    for h in range(4):
                        bu2, bu2_b = u2[h]
                        if first:
                            k.op("dve", lambda e: e.tensor_copy(out=Cst[:, h, :], in_=bu2[:, 0:257]),
                                 reads=[bu2_b], writes=[C_b[h]])
                        else:
                            k.op("dve", lambda e: e.scalar_tensor_tensor(out=Cst[:, h, :], in0=Cst[:, h, :],
                                                                         scalar=dec_bc[:, d, h, n:n + 1], in1=bu2[:, 0:257],
                                                                         op0=ALU.mult, op1=ALU.add),
                                 reads=[bu2_b, C_b[h], dec_b], writes=[C_b[h]])
                    ob = []
                    for g2 in range(2):
                        bo, bo_b = k.bank()
                        rd = [at1_b, lb] + ([] if first else [Sbf_b[prv]])
                        k.pe_begin(reads=rd, writes=[bo_b])
                        for hh in range(2):
                            h = 2 * g2 + hh
                            hk = slice(h * 128, (h + 1) * 128)
                            dst = bo[:, hh * 256:(hh + 1) * 256]
                            ins = nc.tensor.matmul(dst, lhsT=at1[:, hk], rhs=L["v"][:, h * 256:(h + 1) * 256],
                                                   start=True, stop=first)
                            if not first:
                                ins = nc.tensor.matmul(dst, lhsT=L["qe"][:, hk], rhs=Sbf[prv][:, h, :], start=False, stop=True)
                        k.pe_end(ins, reads=rd, writes=[bo_b])
                        ob.append((bo, bo_b))
                    nb = []
                    for h in range(4):
                        hk = slice(h * 128, (h + 1) * 128)
                        bn, bn_b = k.bank()
                        rd = [at2_b, vw_b, lb] + ([] if first else [Cbf_b[prv]])
                        k.pe_begin(reads=rd, writes=[bn_b])
                        ins = nc.tensor.matmul(bn[:, 0:257], lhsT=at2[:, hk], rhs=vw[:, h, :], start=True, stop=first)
                        if not first:
                            ins = nc.tensor.matmul(bn[:, 0:257], lhsT=L["mq"][:, hk], rhs=Cbf[prv][:, h, :], start=False, stop=True)
                        k.pe_end(ins, reads=rd, writes=[bn_b])
                        nb.append((bn, bn_b))
                    if nxt is not None:
                        k.op("act", lambda e: e.activation(out=Sbf[cur][:], in_=Sst[:], func=AF.Copy), reads=S_b, writes=[Sbf_b[cur]])
                        k.op("pool", lambda e: e.tensor_tensor(
                            out=Cbf[cur][:], in0=Cst[:], in1=dec_bc[:, d, :, nxt:nxt + 1].to_broadcast([128, 4, 257]),
                            op=ALU.mult), reads=C_b + [dec_b], writes=[Cbf_b[cur]])
                    for g2 in range(2):
                        bo, bo_b = ob[g2]
                        k.op("act", lambda e: e.activation(out=oA[:, g2 * 512:(g2 + 1) * 512], in_=bo, func=AF.Copy),
                             reads=[bo_b], writes=[oA_b])
                    dn, dn_b = dn_ring.next()
                    for h in range(4):
                        bn, bn_b = nb[h]
                        k.op("act", lambda e: e.activation(out=dn[:, h:h + 1], in_=bn[:, 256:257], func=AF.Copy),
                             reads=[bn_b], writes=[dn_b])
                    for h in range(4):
                        bn, bn_b = nb[h]
                        k.op("act", lambda e: e.activation(out=hB[:, h * 256:(h + 1) * 256], in_=bn[:, 0:256], func=AF.Copy),
                             reads=[bn_b], writes=[hB_b])
                    k.op("dve", lambda e: e.scalar_tensor_tensor(out=dn[:, 4:8], in0=dn[:, 0:4], scalar=-1.0, in1=dn[:, 0:4],
                                                                 op0=ALU.mult, op1=ALU.max), reads=[dn_b], writes=[dn_b])
                    k.op("dve", lambda e: e.tensor_tensor(out=dn[:, 8:12], in0=dn[:, 4:8], in1=thr_tm[:, d, n, :], op=ALU.max),
                         reads=[dn_b, thr_b], writes=[dn_b])
                    k.op("dve", lambda e: e.reciprocal(out=dn[:, 12:16], in_=dn[:, 8:12]), reads=[dn_b], writes=[dn_b])
                    k.op("dve", lambda e: e.tensor_tensor(out=hB[:].rearrange("p (h c) -> p h c", h=4),
                                                          in0=hB[:].rearrange("p (h c) -> p h c", h=4),
                                                          in1=dn[:, 12:16].unsqueeze(2).to_broadcast([128, 4, 256]), op=ALU.mult),
                         reads=[hB_b, dn_b], writes=[hB_b])
                    if d == 1:
                        k.dma("pool", s_obA[n], oA[:], reads=[oA_b], writes=[obA_t[n]], owner=oA_b)
                        k.dma("pool", s_obB[n], hB[:], reads=[hB_b], writes=[obB_t[n]], owner=hB_b)
                    else:
                        k.dma("pool", s_obA[n], oA[:], reads=[oA_b], writes=[obA_t[n]], owner=oA_b, accum_op=ALU.add)
                        k.dma("pool", s_obB[n], hB[:], reads=[hB_b], writes=[obB_t[n]], owner=hB_b, accum_op=ALU.add)
                    first = False
            k.end_phase()

        with ExitStack() as es:
            def T(name, shape, dt):
                return es.enter_context(nc.sbuf_tensor("sb_" + name + "_%d" % id(es), shape, dt))

            wout = T("wout", [128, 8, D], BF16)
            wpost = T("wpost", [128, D], F32)
            wffn = T("wffn", [128, D], F32)
            zero2 = T("zero2", [128, 8, 1], BF16)
            p2_b = k.buf("p2_c")
            for kk in range(8):
                k.dma("pool", wout[:, kk, :], w_out_d[kk * 128:(kk + 1) * 128, :], writes=[p2_b], owner=p2_b)
            k.dma("sp", wpost[:], norms_d[1:2, :].partition_broadcast(128), writes=[p2_b], owner=p2_b)
            k.dma("sp", wffn[:], norms_d[4:5, :].partition_broadcast(128), writes=[p2_b], owner=p2_b)
            z_b = k.buf("zero2")
            k.op("pool", lambda e: e.memset(zero2[:], 0.0), writes=[z_b])
            k.dma("pool", s_h2T[:, :, 0:1], zero2[:], reads=[z_b], owner=z_b, allow_slow_non_contiguous=True)
            k.dma("pool", s_h2T[:, :, S + 1:S + 2], zero2[:], reads=[z_b], owner=z_b, allow_slow_non_contiguous=True)

            RD = 11
            A_ring = Ring(k, es, "cA", [128, D], F32, RD)
            B_ring = Ring(k, es, "cB", [128, D], F32, RD)
            GA_ring = Ring(k, es, "cGA", [128, D], BF16, 6)
            GB_ring = Ring(k, es, "cGB", [128, D], BF16, 6)
            X_ring = Ring(k, es, "cX", [128, D], F32, 4)
            ss_ring = Ring(k, es, "css", [128, 32], F32, RD)
            ybf_ring = Ring(k, es, "cybf", [128, D], BF16, 3)
            yT_ring = Ring(k, es, "cyT", [128, 8, 128], BF16, 3)
            h2_ring = Ring(k, es, "ch2", [128, D], BF16, 3)
            h2T_ring = Ring(k, es, "ch2T", [128, 8, 128], BF16, 3)
            ctxs = {}

            def f0(n):
                c = {"n": n}
                c["A"], c["A_b"] = A_ring.next()
                c["B"], c["B_b"] = B_ring.next()
                c["GA"], c["GA_b"] = GA_ring.next()
                c["GB"], c["GB_b"] = GB_ring.next()
                c["ss"], c["ss_b"] = ss_ring.next()
                k.dma("sp", c["A"][:], s_obA[n], writes=[c["A_b"]], owner=c["A_b"])
                k.dma("sp", c["B"][:], s_obB[n], writes=[c["B_b"]], owner=c["B_b"])
                k.dma("sp", c["GA"][:], s_GA[n], writes=[c["GA_b"]], owner=c["GA_b"])
                k.dma("sp", c["GB"][:], s_GB[n], writes=[c["GB_b"]], owner=c["GB_b"])
                ctxs[n] = c

            def f1(n):
                c = ctxs[n]
                A, A_b, B, B_b, ss, ss_b = c["A"], c["A_b"], c["B"], c["B_b"], c["ss"], c["ss_b"]
                for h in range(4):
                    k.op("act", lambda e: e.activation(out=junk[:, 0:256], in_=A[:, h * 256:(h + 1) * 256], func=AF.Square,
                                                       accum_out=ss[:, h:h + 1]), reads=[A_b], writes=[ss_b])
                for h in range(4):
                    k.op("act", lambda e: e.activation(out=junk[:, 0:256], in_=B[:, h * 256:(h + 1) * 256], func=AF.Square,
                                                       accum_out=ss[:, 4 + h:5 + h]), reads=[B_b], writes=[ss_b])

            def f2(n):
                c = ctxs[n]
                A, A_b, B, B_b, ss, ss_b = c["A"], c["A_b"], c["B"], c["B_b"], c["ss"], c["ss_b"]
                GA, GA_b, GBt, GB_b = c["GA"], c["GA_b"], c["GB"], c["GB_b"]
                rstd_act(ss[:, 0:8], ss[:, 8:16], ss[:, 16:24], ss_b, 1.0 / DV, n=8)
                k.op("pool", lambda e: e.tensor_tensor(out=A[:], in0=A[:], in1=GA[:], op=ALU.mult),
                     reads=[A_b, GA_b], writes=[A_b])
                k.op("pool", lambda e: e.tensor_tensor(out=B[:], in0=B[:], in1=GBt[:], op=ALU.mult),
                     reads=[B_b, GB_b], writes=[B_b])

            def f3(n):
                c = ctxs[n]
                A, A_b, B, B_b, ss, ss_b = c["A"], c["A_b"], c["B"], c["B_b"], c["ss"], c["ss_b"]
                c["ybf"], c["ybf_b"] = ybf_ring.next()
                ybf = c["ybf"]
                for h in range(4):
                    hs = slice(h * 256, (h + 1) * 256)
                    k.op("dve", lambda e: e.tensor_scalar(out=B[:, hs], in0=B[:, hs], scalar1=ss[:, 20 + h:21 + h],
                                                          scalar2=None, op0=ALU.mult),
                         reads=[B_b, ss_b], writes=[B_b])
                for h in range(4):
                    hs = slice(h * 256, (h + 1) * 256)
                    k.op("dve", lambda e: e.scalar_tensor_tensor(out=ybf[:, hs], in0=A[:, hs], scalar=ss[:, 16 + h:17 + h],
                                                                 in1=B[:, hs], op0=ALU.mult, op1=ALU.add),
                         reads=[A_b, B_b, ss_b], writes=[c["ybf_b"]])

            def f4(n):
                c = ctxs[n]
                c["yT"], c["yT_b"] = yT_ring.next()
                transpose8(c["ybf"], c["ybf_b"], c["yT"][:], c["yT_b"], "act")
                c["X"], c["X_b"] = X_ring.next()
                k.dma("sp", c["X"][:], x_d[n * 128:(n + 1) * 128, :], writes=[c["X_b"]], owner=c["X_b"])

            def f5(n):
                c = ctxs[n]
                ss, ss_b = c["ss"], c["ss_b"]
                yT, yT_b = c["yT"], c["yT_b"]
                banks = []
                for half in range(2):
                    bank, bank_b = k.bank()
                    k.pe_begin(reads=[yT_b, p2_b], writes=[bank_b])
                    ins = None
                    for kk in range(8):
                        ins = nc.tensor.matmul(bank, lhsT=yT[:, kk, :], rhs=wout[:, kk, half * 512:(half + 1) * 512],
                                               start=(kk == 0), stop=(kk == 7))
                    k.pe_end(ins, reads=[yT_b, p2_b], writes=[bank_b])
                    banks.append((bank, bank_b))
                c["banks"] = banks
                for half in range(2):
                    bank, bank_b = banks[half]
                    k.op("act", lambda e: e.activation(out=junk[:, 0:512], in_=bank, func=AF.Square,
                                                       accum_out=ss[:, 24 + half:25 + half]), reads=[bank_b], writes=[ss_b])

            def f6(n):
                c = ctxs[n]
                ss, ss_b = c["ss"], c["ss_b"]
                k.op("pool", lambda e: e.tensor_tensor(out=ss[:, 26:27], in0=ss[:, 24:25], in1=ss[:, 25:26], op=ALU.add),
                     reads=[ss_b], writes=[ss_b])
                rstd_act(ss[:, 26:27], ss[:, 27:28], ss[:, 28:29], ss_b, 1.0 / D)

            def f7(n):
                c = ctxs[n]
                A, A_b, B, B_b, ss, ss_b, X, X_b = c["A"], c["A_b"], c["B"], c["B_b"], c["ss"], c["ss_b"], c["X"], c["X_b"]
                for half in range(2):
                    bank, bank_b = c["banks"][half]
                    cs = slice(half * 512, (half + 1) * 512)
                    k.op("dve", lambda e: e.scalar_tensor_tensor(out=A[:, cs], in0=bank, scalar=ss[:, 28:29], in1=wpost[:, cs],
                                                                 op0=ALU.mult, op1=ALU.mult),
                         reads=[bank_b, ss_b, p2_b], writes=[A_b])
                k.op("pool", lambda e: e.tensor_tensor(out=B[:], in0=A[:], in1=X[:], op=ALU.add),
                     reads=[A_b, X_b], writes=[B_b])
                k.dma("pool", s_x1[n], B[:], reads=[B_b], owner=B_b)
                k.op("act", lambda e: e.activation(out=junk[:], in_=B[:], func=AF.Square, accum_out=ss[:, 29:30]),
                     reads=[B_b], writes=[ss_b])

            def f8(n):
                c = ctxs[n]
                ss, ss_b = c["ss"], c["ss_b"]
                rstd_act(ss[:, 29:30], ss[:, 30:31], ss[:, 31:32], ss_b, 1.0 / D)

            def f9(n):
                c = ctxs[n]
                B, B_b, ss, ss_b = c["B"], c["B_b"], c["ss"], c["ss_b"]
                h2, h2_b = h2_ring.next()
                k.op("dve", lambda e: e.scalar_tensor_tensor(out=h2[:], in0=B[:], scalar=ss[:, 31:32], in1=wffn[:],
                                                             op0=ALU.mult, op1=ALU.mult),
                     reads=[B_b, ss_b, p2_b], writes=[h2_b])
                h2T, h2T_b = h2T_ring.next()
                transpose8(h2, h2_b, h2T[:], h2T_b, "act")
                k.dma("pool", s_h2T[:, :, 1 + n * 128:1 + (n + 1) * 128], h2T[:], reads=[h2T_b], owner=h2T_b)
                del ctxs[n]

            stages = [f0, f1, f2, f3, f4, f5, f6, f7, f8, f9]
            NSTG = len(stages)
            GS = 1
            LAG = 1
            groups = [list(range(g0, min(NT, g0 + GS))) for g0 in range(0, NT, GS)]
            NG = len(groups)
            for tau in range((NG - 1) * LAG + NSTG):
                for g in range(NG):
                    st = tau - g * LAG
                    if 0 <= st < NSTG:
                        for t in groups[g]:
                            stages[st](t)
            k.end_phase()

        with ExitStack() as es:
            def T(name, shape, dt):
                return es.enter_context(nc.sbuf_tensor("sb_" + name + "_%d" % id(es), shape, dt))

            wg = T("wg", [128, 8, D], BF16)
            wp = T("wp", [128, 2, D], BF16)
            fcw = T("fcw", [128, 64, 3], F32)
            fcb = T("fcb", [128, 64], F32)
            wfpost = T("wfpost", [128, D], F32)
            wple = T("wple", [128, D], F32)
            p3_b = k.buf("p3_c")
            pw_b = k.buf("p3_w")
            k.dma("sp", fcw[:], ffcw_d[:, :, :], writes=[p3_b], owner=p3_b)
            k.dma("sp", fcb[:], ffcb_d[:, :], writes=[p3_b], owner=p3_b)
            k.dma("sp", wfpost[:], norms_d[5:6, :].partition_broadcast(128), writes=[p3_b], owner=p3_b)
            k.dma("sp", wple[:], norms_d[6:7, :].partition_broadcast(128), writes=[p3_b], owner=p3_b)
            h2_ring = Ring(k, es, "h2b", [128, 8, 514], BF16, 1)
            gT = T("gT", [128, 32, 512], BF16)
            gT_bs = [k.buf("gT%d" % i) for i in range(8)]
            wug_ring = Ring(k, es, "wug", [128, 2, 8, 512], BF16, 2)
            wd_ring = Ring(k, es, "wdn", [128, 4, D], BF16, 2)
            yg_ring = Ring(k, es, "yg", [128, 256], F32, 3)
            yv_ring = Ring(k, es, "yv", [128, 256], F32, 3)
            gl_ring = Ring(k, es, "gl", [128, 256], F32, 2)
            x1_ring = Ring(k, es, "x1f", [128, D], F32, 4)
            x2_ring = Ring(k, es, "x2f", [128, D], F32, 4)
            x2b_ring = Ring(k, es, "x2b", [128, D], BF16, 2)
            x2T_ring = Ring(k, es, "x2T", [128, 8, 128], BF16, 2)
            pt_ring = Ring(k, es, "ptf", [128, PLE], F32, 4)
            ptb_ring = Ring(k, es, "ptb", [128, PLE], BF16, 2)
            pT_ring = Ring(k, es, "pTf", [128, 2, 128], BF16, 2)
            tg_ring = Ring(k, es, "tgf", [128, D], F32, 2)
            ss_ring = Ring(k, es, "ssf", [128, 16], F32, 4)
            xo_ring = Ring(k, es, "xo", [128, D], F32, 1)

            def conv3(bank, bank_b, c, dst, dst_b):
                k.op("act", lambda e: e.activation(out=dst[:], in_=bank[:, 1:257], func=AF.Identity, scale=fcw[:, c, 1:2],
                                                   bias=fcb[:, c:c + 1]),
                     reads=[bank_b, p3_b], writes=[dst_b])
                k.op("dve", lambda e: e.scalar_tensor_tensor(out=dst[:], in0=bank[:, 0:256], scalar=fcw[:, c, 0:1], in1=dst[:],
                                                             op0=ALU.mult, op1=ALU.add), reads=[bank_b, p3_b, dst_b], writes=[dst_b])
                k.op("dve", lambda e: e.scalar_tensor_tensor(out=dst[:], in0=bank[:, 2:258], scalar=fcw[:, c, 2:3], in1=dst[:],
                                                             op0=ALU.mult, op1=ALU.add), reads=[bank_b, p3_b, dst_b], writes=[dst_b])

            print("phase3 sbuf remaining", nc.sbuf_bytes_remaining)
            raw_ring = Ring(k, es, "rawv", [128, 256], F32, 2)
            tp_ring = Ring(k, es, "tpv", [128, 256], F32, 2)

            def conv3p(bank, bank_b, c, dst, dst_b):
                k.op("act", lambda e: e.activation(out=dst[:], in_=bank[:, 1:257], func=AF.Identity, scale=fcw[:, c, 1:2],
                                                   bias=fcb[:, c:c + 1]),
                     reads=[bank_b, p3_b], writes=[dst_b])
                raw, raw_b = raw_ring.next()
                k.op("act", lambda e: e.activation(out=raw[:], in_=bank[:, 0:256], func=AF.Copy),
                     reads=[bank_b], writes=[raw_b])
                k.op("dve", lambda e: e.scalar_tensor_tensor(out=dst[:], in0=bank[:, 2:258], scalar=fcw[:, c, 2:3], in1=dst[:],
                                                             op0=ALU.mult, op1=ALU.add), reads=[bank_b, p3_b, dst_b], writes=[dst_b])
                tp, tp_b = tp_ring.next()
                k.op("pool", lambda e: e.tensor_scalar(out=tp[:], in0=raw[:], scalar1=fcw[:, c, 0:1], scalar2=0.0,
                                                       op0=ALU.mult, op1=ALU.add), reads=[raw_b, p3_b], writes=[tp_b])
                k.op("pool", lambda e: e.tensor_tensor(out=dst[:], in0=dst[:], in1=tp[:], op=ALU.add),
                     reads=[dst_b, tp_b], writes=[dst_b])

            eS_ring = Ring(k, es, "eSf", [128, D], F32, 2)
            from collections import deque
            wug_jobs = deque((bb, gg) for bb in range(NB) for gg in range(8))
            wug_pend = deque()

            def wug_prefetch():
                while len(wug_pend) < 2 and wug_jobs:
                    bb, gg = wug_jobs.popleft()
                    wug, wug_b = wug_ring.next()
                    for gv in range(2):
                        c0 = gv * DFF + gg * 512
                        k.dma("sp", wug[:, gv, :, :], s_wup[:, c0:c0 + 512].rearrange("(kk p) c -> p kk c", p=128),
                              writes=[wug_b], owner=wug_b)
                    wug_pend.append((wug, wug_b))

            wd_jobs = deque((bb, pp) for bb in range(NB) for pp in range(8))
            wd_pend = deque()

            def wd_prefetch():
                while len(wd_pend) < 2 and wd_jobs:
                    bb, piece = wd_jobs.popleft()
                    wd, wd_b = wd_ring.next()
                    k.dma("sp", wd[:], s_wdn[piece * 512:(piece + 1) * 512, :].rearrange("(i p) c -> p i c", p=128),
                          writes=[wd_b], owner=wd_b)
                    wd_pend.append((wd, wd_b))

            def load_h2(bb):
                h2, h2_b = h2_ring.next()
                k.dma("sp", h2[:], s_h2T[:, :, bb * 512:bb * 512 + 514], writes=[h2_b], owner=h2_b)
                return h2, h2_b

            deferred = deque()

            def emit_deferred():
                if deferred:
                    for fn in deferred.popleft():
                        fn()

            def make_tail(b, x2s, pts):
                tctx = [dict() for _ in range(4)]

                def A1(s):
                    c = tctx[s]
                    x2, x2_b, ss, ss_b = x2s[s]
                    pt, pt_b = pts[s]
                    x2b, x2b_b = x2b_ring.next()
                    k.op("act", lambda e: e.activation(out=x2b[:], in_=x2[:], func=AF.Copy), reads=[x2_b], writes=[x2b_b])
                    ptb, ptb_b = ptb_ring.next()
                    k.op("pool", lambda e: e.tensor_copy(out=ptb[:], in_=pt[:]), reads=[pt_b], writes=[ptb_b])
                    c["x2b"], c["x2b_b"], c["ptb"], c["ptb_b"] = x2b, x2b_b, ptb, ptb_b

                def A2(s):
                    c = tctx[s]
                    x2b, x2b_b, ptb, ptb_b = c["x2b"], c["x2b_b"], c["ptb"], c["ptb_b"]
                    bank, bank_b = k.bank()
                    bankb = bank.bitcast(BF16)
                    k.pe_begin(reads=[x2b_b, cst_b], writes=[bank_b])
                    ins = None
                    for kk in range(8):
                        ins = nc.tensor.transpose(out=bankb[:, kk * 128:(kk + 1) * 128], in_=x2b[:, kk * 128:(kk + 1) * 128],
                                                  identity=identb[:])
                    k.pe_end(ins, reads=[x2b_b, cst_b], writes=[bank_b])
                    c["bk1"] = (bankb, bank_b)
                    bank, bank_b = k.bank()
                    bankb = bank.bitcast(BF16)
                    k.pe_begin(reads=[ptb_b, cst_b], writes=[bank_b])
                    for kk in range(2):
                        ins = nc.tensor.transpose(out=bankb[:, kk * 128:(kk + 1) * 128], in_=ptb[:, kk * 128:(kk + 1) * 128],
                                                  identity=identb[:])
                    k.pe_end(ins, reads=[ptb_b, cst_b], writes=[bank_b])
                    c["bk2"] = (bankb, bank_b)

                def A3(s):
                    c = tctx[s]
                    x2T, x2T_b = x2T_ring.next()
                    bankb, bank_b = c["bk1"]
                    k.op("act", lambda e: e.activation(out=x2T[:], in_=bankb.rearrange("p (k t) -> p k t", k=8), func=AF.Copy),
                         reads=[bank_b], writes=[x2T_b])
                    pT, pT_b = pT_ring.next()
                    bankb, bank_b = c["bk2"]
                    k.op("act", lambda e: e.activation(out=pT[:], in_=bankb[:, 0:256].rearrange("p (k t) -> p k t", k=2),
                                                       func=AF.Copy), reads=[bank_b], writes=[pT_b])
                    c["x2T"], c["x2T_b"], c["pT"], c["pT_b"] = x2T, x2T_b, pT, pT_b

                def A4(s):
                    c = tctx[s]
                    x2T, x2T_b, pT, pT_b = c["x2T"], c["x2T_b"], c["pT"], c["pT_b"]
                    gbanks = []
                    ebanks = []
                    for half in range(2):
                        cs = slice(half * 512, (half + 1) * 512)
                        bank, bank_b = k.bank()
                        k.pe_begin(reads=[x2T_b, pw_b], writes=[bank_b])
                        ins = None
                        for kk in range(8):
                            ins = nc.tensor.matmul(bank, lhsT=x2T[:, kk, :], rhs=wg[:, kk, cs], start=(kk == 0), stop=(kk == 7))
                        k.pe_end(ins, reads=[x2T_b, pw_b], writes=[bank_b])
                        gbanks.append((bank, bank_b))
                        bank, bank_b = k.bank()
                        k.pe_begin(reads=[pT_b, pw_b], writes=[bank_b])
                        for kk in range(2):
                            ins = nc.tensor.matmul(bank, lhsT=pT[:, kk, :], rhs=wp[:, kk, cs], start=(kk == 0), stop=(kk == 1))
                        k.pe_end(ins, reads=[pT_b, pw_b], writes=[bank_b])
                        ebanks.append((bank, bank_b))
                    c["gbanks"], c["ebanks"] = gbanks, ebanks

                def A5(s):
                    c = tctx[s]
                    tg, tg_b = tg_ring.next()
                    eS, eS_b = eS_ring.next()
                    for half in range(2):
                        cs = slice(half * 512, (half + 1) * 512)
                        bank, bank_b = c["gbanks"][half]
                        k.op("act", lambda e: e.activation(out=tg[:, cs], in_=bank, func=AF.Tanh, scale=0.5),
                             reads=[bank_b], writes=[tg_b])
                        bank, bank_b = c["ebanks"][half]
                        k.op("act", lambda e: e.activation(out=eS[:, cs], in_=bank, func=AF.Copy),
                             reads=[bank_b], writes=[eS_b])
                    c["tg"], c["tg_b"], c["eS"], c["eS_b"] = tg, tg_b, eS, eS_b

                def A6(s):
                    c = tctx[s]
                    tg, tg_b, eS, eS_b = c["tg"], c["tg_b"], c["eS"], c["eS_b"]
                    k.op("dve", lambda e: e.scalar_tensor_tensor(out=tg[:], in0=tg[:], scalar=1.0, in1=eS[:],
                                                                 op0=ALU.add, op1=ALU.mult),
                         reads=[eS_b, tg_b], writes=[tg_b])

                def A7(s):
                    c = tctx[s]
                    x2, x2_b, ss, ss_b = x2s[s]
                    tg, tg_b = c["tg"], c["tg_b"]
                    k.op("act", lambda e: e.activation(out=junk[:], in_=tg[:], func=AF.Square, accum_out=ss[:, 5:6]),
                         reads=[tg_b], writes=[ss_b])

                def A8(s):
                    x2, x2_b, ss, ss_b = x2s[s]
                    rstd_ops(ss[:, 5:6], ss[:, 6:7], ss[:, 7:8], ss_b, 0.25 / D)

                def A9(s):
                    n = 4 * b + s
                    c = tctx[s]
                    x2, x2_b, ss, ss_b = x2s[s]
                    tg, tg_b = c["tg"], c["tg_b"]
                    k.op("dve", lambda e: e.scalar_tensor_tensor(out=tg[:], in0=tg[:], scalar=ss[:, 7:8], in1=wple[:],
                                                                 op0=ALU.mult, op1=ALU.mult),
                         reads=[tg_b, ss_b, p3_b], writes=[tg_b])
                    xo, xo_b = xo_ring.next()
                    k.op("dve", lambda e: e.scalar_tensor_tensor(out=xo[:], in0=tg[:], scalar=0.5, in1=x2[:],
                                                                 op0=ALU.mult, op1=ALU.add),
                         reads=[tg_b, x2_b], writes=[xo_b])
                    k.dma("pool", out_d[n * 128:(n + 1) * 128, :], xo[:], reads=[xo_b], owner=xo_b)

                def U(fn, s):
                    return lambda: fn(s)

                A1(0)
                if DEFER_MODE == 3:
                    units = []
                    for j in range(8):
                        u = []
                        s = j - 3
                        if 0 <= s < 4:
                            u += [U(A8, s), U(A9, s)]
                        s = j - 2
                        if 0 <= s < 4:
                            u += [U(A6, s), U(A7, s)]
                        s = j - 1
                        if 0 <= s < 4:
                            u += [U(A4, s), U(A5, s)]
                        s = j
                        if 0 <= s < 4:
                            u += [U(A2, s), U(A3, s)]
                        if 0 <= j + 1 < 4:
                            u += [U(A1, j + 1)]
                        units.append(u)
                    return units
                units = []
                for s in range(4):
                    if DEFER_MODE == 2:
                        units.append([U(A2, s), U(A3, s)])
                        units.append([U(A4, s), U(A5, s)])
                        units.append([U(A6, s)])
                        units.append([U(A7, s)])
                        units.append([U(A8, s)])
                        units.append([])
                        units.append([])
                    else:
                        for fn in (A2, A3, A4, A5, A6, A7, A8):
                            units.append([U(fn, s)])
                    if s < 3:
                        units.append([U(A9, s), U(A1, s + 1)])
                    else:
                        units.append([U(A9, s)])
                return units

            wug_prefetch()
            h2_next = load_h2(0)
            for b in range(NB):
                h2, h2_b = h2_next
                for g in range(8):
                    wug, wug_b = wug_pend.popleft()
                    for j in range(4):
                        c = 4 * g + j
                        for seg in range(2):
                            res = []
                            for gv in range(2):
                                bank, bank_b = k.bank()
                                k.pe_begin(reads=[wug_b, h2_b], writes=[bank_b])
                                ins = None
                                for kk in range(8):
                                    ins = nc.tensor.matmul(bank[:, 0:258], lhsT=wug[:, gv, kk, j * 128:(j + 1) * 128],
                                                           rhs=h2[:, kk, seg * 256:seg * 256 + 258], start=(kk == 0), stop=(kk == 7))
                                k.pe_end(ins, reads=[wug_b, h2_b], writes=[bank_b])
                                res.append((bank, bank_b))
                            yg, yg_b = yg_ring.next()
                            yv, yv_b = yv_ring.next()
                            conv3(res[0][0], res[0][1], c, yg, yg_b)
                            conv3p(res[1][0], res[1][1], 32 + c, yv, yv_b)
                            gl, gl_b = gl_ring.next()
                            k.op("act", lambda e: e.activation(out=gl[:], in_=yg[:], func=AF.Gelu_apprx_tanh),
                                 reads=[yg_b], writes=[gl_b])
                            k.op("pool", lambda e: e.tensor_tensor(out=gT[:, c, seg * 256:(seg + 1) * 256], in0=gl[:], in1=yv[:],
                                                                   op=ALU.mult), reads=[gl_b, yv_b], writes=[gT_bs[c // 4]])
                        if DEFER_MODE in (0, 2):
                            emit_deferred()
                    if DEFER_MODE == 1:
                        for _ in range(4):
                            emit_deferred()
                    if DEFER_MODE == 3:
                        emit_deferred()
                    wug_prefetch()
                    if g == 5:
                        wd_prefetch()
                    if b == 0 and g == 1:
                        for kk in range(8):
                            k.dma("pool", wg[:, kk, :], w_pg_d[kk * 128:(kk + 1) * 128, :], writes=[pw_b], owner=pw_b)
                        for kk in range(2):
                            k.dma("pool", wp[:, kk, :], w_pp_d[kk * 128:(kk + 1) * 128, :], writes=[pw_b], owner=pw_b)
                while deferred:
                    emit_deferred()
                if b + 1 < NB:
                    h2_next = load_h2(b + 1)
                dbanks = [[k.bank() for half in range(2)] for s in range(4)]
                allb = [dbanks[s][half][1] for s in range(4) for half in range(2)]
                for piece in range(8):
                    wd_prefetch()
                    wd, wd_b = wd_pend.popleft()
                    k.pe_begin(reads=[wd_b, gT_bs[piece]], writes=allb)
                    ins = None
                    for s in range(4):
                        for half in range(2):
                            for i in range(4):
                                ins = nc.tensor.matmul(dbanks[s][half][0], lhsT=gT[:, piece * 4 + i, s * 128:(s + 1) * 128],
                                                       rhs=wd[:, i, half * 512:(half + 1) * 512],
                                                       start=(piece == 0 and i == 0), stop=(piece == 7 and i == 3),
                                                       skip_group_check=True)
                    k.pe_end(ins, reads=[wd_b, gT_bs[piece]], writes=allb)
                    if piece < 6:
                        wd_prefetch()
                x2s = []
                pts = []
                x1s = []
                for s in range(4):
                    n = 4 * b + s
                    x1, x1_b = x1_ring.next()
                    k.dma("sp", x1[:], s_x1[n], writes=[x1_b], owner=x1_b)
                    pt, pt_b = pt_ring.next()
                    k.dma("sp", pt[:], p_d[n * 128:(n + 1) * 128, :], writes=[pt_b], owner=pt_b)
                    x2, x2_b = x2_ring.next()
                    ss, ss_b = ss_ring.next()
                    x1s.append((x1, x1_b))
                    pts.append((pt, pt_b))
                    x2s.append((x2, x2_b, ss, ss_b))
                for s in range(4):
                    x2, x2_b, ss, ss_b = x2s[s]
                    for half in range(2):
                        bank, bank_b = dbanks[s][half]
                        k.op("act", lambda e: e.activation(out=junk[:, 0:512], in_=bank, func=AF.Square,
                                                           accum_out=ss[:, half:half + 1]), reads=[bank_b], writes=[ss_b])
                for s in range(4):
                    x2, x2_b, ss, ss_b = x2s[s]
                    k.op("pool", lambda e: e.tensor_tensor(out=ss[:, 2:3], in0=ss[:, 0:1], in1=ss[:, 1:2], op=ALU.add),
                         reads=[ss_b], writes=[ss_b])
                    rstd_ops(ss[:, 2:3], ss[:, 3:4], ss[:, 4:5], ss_b, 1.0 / D)
                for s in range(4):
                    x2, x2_b, ss, ss_b = x2s[s]
                    for half in range(2):
                        bank, bank_b = dbanks[s][half]
                        cs = slice(half * 512, (half + 1) * 512)
                        k.op("dve", lambda e: e.scalar_tensor_tensor(out=x2[:, cs], in0=bank, scalar=ss[:, 4:5], in1=wfpost[:, cs],
                                                                     op0=ALU.mult, op1=ALU.mult),
                             reads=[bank_b, ss_b, p3_b], writes=[x2_b])
                for s in range(4):
                    x2, x2_b, ss, ss_b = x2s[s]
                    x1, x1_b = x1s[s]
                    k.op("pool", lambda e: e.tensor_tensor(out=x2[:], in0=x2[:], in1=x1[:], op=ALU.add),
                         reads=[x2_b, x1_b], writes=[x2_b])
                for u in make_tail(b, x2s, pts):
                    if DEFER:
                        deferred.append(u)
                    else:
                        for fn in u:
                            fn()
            while deferred:
                emit_deferred()
            k.end_phase()
    return nc


def make_consts():
    j = np.arange(128)[:, None]
    i = np.arange(128)[None, :]
    tri = np.zeros((128, 4, 128), np.float32)
    tri[:, 0, :] = (j <= i)
    tri[:, 1, :] = (j >= i)
    tri[:, 2, :] = (j > i)
    tri[:, 3, :] = (j < i)
    sel = np.zeros((4, 4, 128), np.float32)
    for h in range(4):
        sel[h, h, :] = 1.0
    return {
        "tri": tri,
        "identb": np.eye(128, dtype=np.float32).astype(ml_dtypes.bfloat16),
        "identf": np.eye(128, dtype=np.float32),
        "sel": sel,
    }


def make_shared(inp):
    f = lambda a: np.ascontiguousarray(np.asarray(a, dtype=np.float32))
    sh = dict(make_consts())
    sh["w_in"] = f(inp["w_in"][0])
    sh["wd_aug"] = f(np.stack([
        np.concatenate([inp["w_gla_decay_f"][0], inp["b_gla_decay_f"][0][None, :]], axis=0),
        np.concatenate([inp["w_gla_decay_b"][0], inp["b_gla_decay_b"][0][None, :]], axis=0)], axis=0))
    sh["mlcw"] = f(np.asarray(inp["ml_conv_w"][0]).T.reshape(8, 128, 3).transpose(1, 0, 2))
    sh["mlcb"] = f(np.asarray(inp["ml_conv_b"][0]).reshape(8, 128).T)
    ib = np.asarray(inp["ml_igate_b"][0])
    fb = np.asarray(inp["ml_fgate_b"][0])
    sh["gbias"] = f(np.stack([ib[0:4], ib[4:8], fb[0:4], fb[4:8]], axis=1))
    sh["ffcw"] = f(np.asarray(inp["ffn_conv_w"][0]).T.reshape(64, 128, 3).transpose(1, 0, 2))
    sh["ffcb"] = f(np.asarray(inp["ffn_conv_b"][0]).reshape(64, 128).T)
    sh["norms"] = f(np.stack([inp["norm_mix_pre"][0], inp["norm_mix_post"][0], inp["gla_norm"][0], inp["ml_norm"][0],
                              inp["norm_ffn_pre"][0], inp["norm_ffn_post"][0], inp["norm_ple_post"][0]], axis=0))
    sh["w_out"] = f(inp["w_out"][0])
    sh["w_up"] = f(inp["w_up"][0])
    sh["w_down"] = f(inp["w_down"][0])
    sh["w_pg"] = f(inp["w_ple_gate"][0])
    sh["w_pp"] = f(inp["w_ple_proj"][0])
    return sh


def kernel(**inputs):
    x = np.asarray(inputs["x"], dtype=np.float32)
    p = np.asarray(inputs["p"], dtype=np.float32)
    B, S, _ = x.shape
    sh = make_shared(inputs)
    nc = build(S)
    in_maps = []
    for b in range(B):
        m = dict(sh)
        m["x"] = np.ascontiguousarray(x[b])
        m["p"] = np.ascontiguousarray(p[0, b])
        in_maps.append(m)
    res = run_bass_kernel_spmd(nc, in_maps, core_ids=list(range(B)))
    return np.stack([np.asarray(r["out"], dtype=np.float32) for r in res.results], axis=0)
```

```python
import math
import numpy as np
import ml_dtypes
from contextlib import ExitStack
import concourse.bass as bass
import concourse.mybir as mybir
from concourse.bass_utils import run_bass_kernel_spmd

F32 = mybir.dt.float32
BF16 = mybir.dt.bfloat16
AF = mybir.ActivationFunctionType
ALU = mybir.AluOpType

D = 1024
DIN = 8240
H = 4
DK = 128
DV = 256
DFF = 4096
PLE = 256
EPS = 1e-6
LN_QS = math.log(DK ** -0.5)
CONV_CH = 1024
DEFER = True
DEFER_MODE = 3


class Buf:
    __slots__ = ("name", "w", "r", "sem", "cnt")

    def __init__(self, name):
        self.name = name
        self.w = None
        self.r = {}
        self.sem = None
        self.cnt = 0


class KB:
    def __init__(self, nc, es):
        self.nc = nc
        self.es = es
        self.eng = {"pe": nc.tensor, "act": nc.scalar, "dve": nc.vector, "pool": nc.gpsimd, "sp": nc.sync}
        self.esem = {}
        for e in ("pe", "act", "dve", "pool"):
            self.esem[e] = es.enter_context(nc.semaphore("es_" + e))
        self.ecnt = {e: 0 for e in self.esem}
        self.waited = {e: {} for e in self.eng}
        self.free_sems = []
        self.all_dma = {}
        self.phase_bufs = []
        self.nsem = 0
        self.ps = es.enter_context(nc.psum_tensor("ps", [128, 8, 512], F32))
        self.pb = [Buf("bank%d" % i) for i in range(8)]
        self.bank_i = 0

    def buf(self, name):
        b = Buf(name)
        self.phase_bufs.append(b)
        return b

    def bank(self):
        i = self.bank_i
        self.bank_i = (i + 1) % 8
        return self.ps[:, i, :], self.pb[i]

    def _wait(self, e, tok):
        key, sem, val = tok
        if e == "pe" and key == "pe":
            return
        w = self.waited[e]
        if w.get(id(sem), 0) >= val:
            return
        self.eng[e].wait_ge(sem, val)
        w[id(sem)] = val

    def _deps(self, e, reads, writes):
        for b in reads:
            if b.w is not None:
                self._wait(e, b.w)
        for b in writes:
            if b.w is not None:
                self._wait(e, b.w)
            for t in b.r.values():
                self._wait(e, t)

    def _mark(self, tok, reads, writes):
        for b in reads:
            b.r[tok[0]] = tok
        for b in writes:
            b.w = tok
            b.r = {}

    def op(self, e, fn, reads=(), writes=()):
        self._deps(e, reads, writes)
        ins = fn(self.eng[e])
        self.ecnt[e] += 1
        ins.then_inc(self.esem[e], 1)
        tok = (e, self.esem[e], self.ecnt[e])
        self._mark(tok, reads, writes)
        return tok

    def pe_begin(self, reads=(), writes=()):
        self._deps("pe", reads, writes)

    def pe_end(self, ins, reads=(), writes=()):
        self.ecnt["pe"] += 1
        ins.then_inc(self.esem["pe"], 1)
        tok = ("pe", self.esem["pe"], self.ecnt["pe"])
        self._mark(tok, reads, writes)

    def dma(self, q, out, in_, reads=(), writes=(), owner=None, **kw):
        skip = ("d", id(owner.sem)) if owner.sem is not None else None
        for b in reads:
            if b.w is not None:
                self._wait(q, b.w)
        for b in writes:
            if b.w is not None and b.w[0] != skip:
                self._wait(q, b.w)
            for t in b.r.values():
                self._wait(q, t)
        if owner.sem is None:
            if self.free_sems:
                owner.sem, owner.cnt = self.free_sems.pop()
            else:
                self.nsem += 1
                owner.sem = self.es.enter_context(self.nc.semaphore("ds%d" % self.nsem))
                owner.cnt = 0
        ins = self.eng[q].dma_start(out=out, in_=in_, **kw)
        owner.cnt += 16
        ins.then_inc(owner.sem, 16)
        self.all_dma[id(owner.sem)] = (owner.sem, owner.cnt)
        tok = (("d", id(owner.sem)), owner.sem, owner.cnt)
        self._mark(tok, reads, writes)

    def barrier(self):
        for e in self.eng:
            for e2 in self.esem:
                if e2 != e and self.ecnt[e2] > 0:
                    self._wait(e, (e2 + "_b", self.esem[e2], self.ecnt[e2]))
            for sem, cnt in self.all_dma.values():
                self._wait(e, ("db", sem, cnt))

    def end_phase(self):
        self.barrier()
        for b in self.phase_bufs:
            if b.sem is not None:
                self.free_sems.append((b.sem, b.cnt))
                b.sem = None
        self.phase_bufs = []
        for b in self.pb:
            b.w = None
            b.r = {}


class Ring:
    def __init__(self, k, es, name, shape, dtype, n):
        self.t = [es.enter_context(k.nc.sbuf_tensor("sr_%s%d_%d" % (name, i, id(es)), shape, dtype)) for i in range(n)]
        self.b = [k.buf("%s%d" % (name, i)) for i in range(n)]
        self.i = 0
        self.n = n

    def next(self):
        i = self.i
        self.i = (i + 1) % self.n
        return self.t[i], self.b[i]


def build(S):
    NT = S // 128
    NB = S // 512
    nc = bass.Bass("TRN2", target_bir_lowering=False)

    def din(name, shape, dt=F32):
        return nc.dram_tensor(name, list(shape), dt, kind="ExternalInput").ap()

    def dscr(name, shape, dt):
        return nc.dram_tensor(name, list(shape), dt, kind="Internal").ap()

    x_d = din("x", [S, D])
    p_d = din("p", [S, PLE])
    w_in = din("w_in", [D, DIN])
    wd_aug = din("wd_aug", [2, 17, 512])
    mlcw_d = din("mlcw", [128, 8, 3])
    mlcb_d = din("mlcb", [128, 8])
    gbias_d = din("gbias", [4, 4])
    ffcw_d = din("ffcw", [128, 64, 3])
    ffcb_d = din("ffcb", [128, 64])
    norms_d = din("norms", [7, D])
    w_out_d = din("w_out", [D, D])
    w_up_d = din("w_up", [D, 2 * DFF])
    w_down_d = din("w_down", [DFF, D])
    w_pg_d = din("w_pg", [D, D])
    w_pp_d = din("w_pp", [PLE, D])
    tri_d = din("tri", [128, 4, 128])
    identb_d = din("identb", [128, 128], BF16)
    identf_d = din("identf", [128, 128])
    sel_d = din("sel", [4, 4, 128])
    out_d = nc.dram_tensor("out", [S, D], F32, kind="ExternalOutput").ap()

    s_hT = dscr("s_hT", [NB, 128, 8, 512], BF16)
    s_mqk = dscr("s_mqk", [1024, S], F32)
    s_gate = dscr("s_gate", [16, S], F32)
    s_qe = [dscr("s_qe%d" % d, [NT, 128, 512], BF16) for d in range(2)]
    s_ke = [dscr("s_ke%d" % d, [NT, 128, 512], BF16) for d in range(2)]
    s_kd = [dscr("s_kd%d" % d, [NT, 128, 512], BF16) for d in range(2)]
    s_v = dscr("s_v", [NT, 128, 1024], BF16)
    s_mv = dscr("s_mv", [NT, 128, 1024], BF16)
    s_GA = dscr("s_GA", [NT, 128, 1024], BF16)
    s_GB = dscr("s_GB", [NT, 128, 1024], BF16)
    s_mq = dscr("s_mq", [NT, 128, 4, 128], BF16)
    s_mk = dscr("s_mk", [NT, 128, 4, 128], BF16)
    s_mktm = dscr("s_mktm", [NT, 128, 512], BF16)
    s_obA = dscr("s_obA", [NT, 128, 1024], F32)
    s_obB = dscr("s_obB", [NT, 128, 1024], F32)
    s_x1 = dscr("s_x1", [NT, 128, 1024], F32)
    s_h2T = dscr("s_h2T", [128, 8, S + 2], BF16)
    s_wup = dscr("s_wup", [D, 2 * DFF], BF16)
    s_wdn = dscr("s_wdn", [DFF, D], BF16)

    with ExitStack() as ges:
        k = KB(nc, ges)

        def GT(name, shape, dt):
            return ges.enter_context(nc.sbuf_tensor("sb_" + name, shape, dt))

        tri = GT("tri", [128, 4, 128], F32)
        identb = GT("identb", [128, 128], BF16)
        identf = GT("identf", [128, 128], F32)
        sel = GT("sel", [4, 4, 128], F32)
        neghalf = GT("neghalf", [128, 8], F32)
        EB = GT("EB", [128, 2, NT, 4], F32)
        w_tm = GT("w_tm", [128, 2, NT, 4], F32)
        thr_tm = GT("thr_tm", [128, 2, NT, 4], F32)
        dec_bc = GT("dec_bc", [128, 2, 4, NT], F32)
        junk = GT("junk", [128, 1024], BF16)
        cst_b = Buf("cst")
        nh_b = Buf("neghalf")
        EB_b = Buf("EB")
        wtm_b = Buf("wtm")
        thr_b = Buf("thr")
        dec_b = Buf("dec")
        k.dma("sp", tri[:], tri_d[:, :, :], writes=[cst_b], owner=cst_b)
        k.dma("sp", identb[:], identb_d[:, :], writes=[cst_b], owner=cst_b)
        k.dma("sp", identf[:], identf_d[:, :], writes=[cst_b], owner=cst_b)
        k.dma("sp", sel[:], sel_d[:, :, :], writes=[cst_b], owner=cst_b)
        k.op("pool", lambda e: e.memset(neghalf[:], -0.5), writes=[nh_b])

        def rstd_ops(ss_ap, ms_ap, rs_ap, stb, scale, n=1):
            k.op("pool", lambda e: e.tensor_scalar(out=ms_ap, in0=ss_ap, scalar1=scale, scalar2=EPS,
                                                   op0=ALU.mult, op1=ALU.add), reads=[stb], writes=[stb])
            k.op("pool", lambda e: e.tensor_tensor(out=rs_ap, in0=ms_ap, in1=neghalf[:, 0:n], op=ALU.pow),
                 reads=[stb, nh_b], writes=[stb])

        def rstd_act(ss_ap, ms_ap, rs_ap, stb, scale, n=1):
            k.op("act", lambda e: e.activation(out=ms_ap, in_=ss_ap, func=AF.Ln, scale=scale, bias=EPS),
                 reads=[stb], writes=[stb])
            k.op("act", lambda e: e.activation(out=rs_ap, in_=ms_ap, func=AF.Exp, scale=-0.5),
                 reads=[stb], writes=[stb])

        def transpose8(src, src_b, dst_view, dst_b, evac_eng):
            bank, bank_b = k.bank()
            bankb = bank.bitcast(BF16)
            k.pe_begin(reads=[src_b, cst_b], writes=[bank_b])
            ins = None
            for kk in range(8):
                ins = nc.tensor.transpose(out=bankb[:, kk * 128:(kk + 1) * 128], in_=src[:, kk * 128:(kk + 1) * 128],
                                          identity=identb[:])
            k.pe_end(ins, reads=[src_b, cst_b], writes=[bank_b])
            srcv = bankb.rearrange("p (k t) -> p k t", k=8)
            if evac_eng == "act":
                k.op("act", lambda e: e.activation(out=dst_view, in_=srcv, func=AF.Copy), reads=[bank_b], writes=[dst_b])
            else:
                k.op("dve", lambda e: e.tensor_copy(out=dst_view, in_=srcv), reads=[bank_b], writes=[dst_b])

        with ExitStack() as es:
            def T(name, shape, dt):
                return es.enter_context(nc.sbuf_tensor("sb_" + name + "_%d" % id(es), shape, dt))

            wA = T("wA", [128, 8, 2096], BF16)
            wA_b = k.buf("wA")
            wpre = T("wpre", [128, D], F32)
            wdt = T("wdt", [17, 2, 512], BF16)
            pa_b = k.buf("pa_c")
            wd_b0 = k.buf("pa_wd")
            k.dma("sp", wpre[:], norms_d[0:1, :].partition_broadcast(128), writes=[pa_b], owner=pa_b)
            for d in range(2):
                k.dma("pool", wdt[:, d, :], wd_aug[d, :, :], writes=[wd_b0], owner=wd_b0)
            for kk in range(8):
                rows = slice(kk * 128, (kk + 1) * 128)
                k.dma("pool", wA[:, kk, 0:1024], w_in[rows, 0:1024], writes=[wA_b], owner=wA_b)
                k.dma("pool", wA[:, kk, 1024:2080], w_in[rows, 3072:4128], writes=[wA_b], owner=wA_b)
                k.dma("pool", wA[:, kk, 2080:2096], w_in[rows, 6176:6192], writes=[wA_b], owner=wA_b)
            x_ring = Ring(k, es, "xa", [128, D], F32, 4)
            st_ring = Ring(k, es, "sta", [128, 4], F32, 4)
            hb_ring = Ring(k, es, "hba", [128, D], BF16, 2)
            hT_ring = Ring(k, es, "hTa", [128, 8, 512], BF16, 2)
            qraw_ring = Ring(k, es, "qraw", [128, 4, 512], F32, 2)
            kraw_ring = Ring(k, es, "kraw", [128, 4, 512], F32, 2)
            mstg_ring = Ring(k, es, "mstg", [128, 512], F32, 3)
            gstg_ring = Ring(k, es, "gstg", [16, 512], F32, 2)
            lrT_ring = [Ring(k, es, "lrT%d" % d, [17, 512], BF16, 2) for d in range(2)]
            for d in range(2):
                for i in range(2):
                    t_, b_ = lrT_ring[d].t[i], lrT_ring[d].b[i]
                    k.op("pool", lambda e: e.memset(t_[:], 1.0), writes=[b_])
            sp_ring = [Ring(k, es, "sp%d" % d, [128, 512], F32, 5) for d in range(2)]
            eb_ring = [Ring(k, es, "eb%d" % d, [128, 512], F32, 2) for d in range(2)]
            enb_ring = [Ring(k, es, "enb%d" % d, [128, 512], F32, 2) for d in range(2)]
            ekd_ring = [Ring(k, es, "ekd%d" % d, [128, 512], F32, 2) for d in range(2)]
            qst_ring = Ring(k, es, "qst", [128, 512], BF16, 4)
            kst_ring = Ring(k, es, "kst", [128, 512], BF16, 4)
            kdst_ring = Ring(k, es, "kdst", [128, 512], BF16, 4)

            def make_hT(b):
                hT, hT_b = hT_ring.next()
                for s in range(4):
                    t = 4 * b + s
                    xs, xs_b = x_ring.next()
                    k.dma("sp", xs[:], x_d[t * 128:(t + 1) * 128, :], writes=[xs_b], owner=xs_b)
                    st, st_b = st_ring.next()
                    k.op("act", lambda e: e.activation(out=junk[:], in_=xs[:], func=AF.Square, accum_out=st[:, 0:1]),
                         reads=[xs_b], writes=[st_b])
                    rstd_ops(st[:, 0:1], st[:, 1:2], st[:, 2:3], st_b, 1.0 / D)
                    hb, hb_b = hb_ring.next()
                    k.op("dve", lambda e: e.scalar_tensor_tensor(out=hb[:], in0=xs[:], scalar=st[:, 2:3], in1=wpre[:],
                                                                 op0=ALU.mult, op1=ALU.mult),
                         reads=[xs_b, st_b, pa_b], writes=[hb_b])
                    transpose8(hb, hb_b, hT[:, :, s * 128:(s + 1) * 128], hT_b, "dve")
                k.dma("pool", s_hT[b], hT[:], reads=[hT_b], owner=hT_b)
                return hT, hT_b

            hT_next = make_hT(0)
            for b in range(NB):
                hT, hT_b = hT_next

                def fm_group(c0, M):
                    bank, bank_b = k.bank()
                    k.pe_begin(reads=[wA_b, hT_b], writes=[bank_b])
                    ins = None
                    for kk in range(8):
                        ins = nc.tensor.matmul(bank[0:M, :], lhsT=wA[:, kk, c0:c0 + M], rhs=hT[:, kk, :],
                                               start=(kk == 0), stop=(kk == 7))
                    k.pe_end(ins, reads=[wA_b, hT_b], writes=[bank_b])
                    return bank, bank_b

                lrT = []
                for d in range(2):
                    bank, bank_b = fm_group(1024 + d * 16, 16)
                    lt, lt_b = lrT_ring[d].next()
                    k.op("dve", lambda e: e.tensor_copy(out=lt[0:16, :], in_=bank[0:16, :]), reads=[bank_b], writes=[lt_b])
                    lrT.append((lt, lt_b))
                spts = []
                for s in range(4):
                    ts = slice(s * 128, (s + 1) * 128)
                    row = []
                    for d in range(2):
                        lt, lt_b = lrT[d]
                        bank, bank_b = k.bank()
                        k.pe_begin(reads=[lt_b, wd_b0], writes=[bank_b])
                        ins = nc.tensor.matmul(bank, lhsT=lt[0:17, ts], rhs=wdt[0:17, d, :], start=True, stop=True)
                        k.pe_end(ins, reads=[lt_b, wd_b0], writes=[bank_b])
                        spt, spt_b = sp_ring[d].next()
                        k.op("act", lambda e: e.activation(out=spt[:], in_=bank, func=AF.Exp, scale=-1.0),
                             reads=[bank_b], writes=[spt_b])
                        k.op("act", lambda e: e.activation(out=spt[:], in_=spt[:], func=AF.Ln, bias=1.0),
                             reads=[spt_b], writes=[spt_b])
                        row.append((spt, spt_b))
                    spts.append(row)
                qraw, qraw_b = qraw_ring.next()
                kraw, kraw_b = kraw_ring.next()
                for h in range(4):
                    bank, bank_b = fm_group(h * 128, 128)
                    k.op("act", lambda e: e.activation(out=qraw[:, h, :], in_=bank, func=AF.Copy),
                         reads=[bank_b], writes=[qraw_b])
                for h in range(4):
                    bank, bank_b = fm_group(512 + h * 128, 128)
                    k.op("dve", lambda e: e.tensor_copy(out=kraw[:, h, :], in_=bank), reads=[bank_b], writes=[kraw_b])
                for c in range(8):
                    bank, bank_b = fm_group(1056 + c * 128, 128)
                    ms_, ms_b = mstg_ring.next()
                    if c % 2 == 0:
                        k.op("act", lambda e: e.activation(out=ms_[:], in_=bank, func=AF.Copy), reads=[bank_b], writes=[ms_b])
                    else:
                        k.op("dve", lambda e: e.tensor_copy(out=ms_[:], in_=bank), reads=[bank_b], writes=[ms_b])
                    k.dma("pool", s_mqk[c * 128:(c + 1) * 128, b * 512:(b + 1) * 512], ms_[:], reads=[ms_b], owner=ms_b)
                bank, bank_b = fm_group(2080, 16)
                gs_, gs_b = gstg_ring.next()
                k.op("dve", lambda e: e.tensor_copy(out=gs_[:], in_=bank[0:16, :]), reads=[bank_b], writes=[gs_b])
                k.dma("pool", s_gate[:, b * 512:(b + 1) * 512], gs_[:], reads=[gs_b], owner=gs_b)

                if b + 1 < NB:
                    hT_next = make_hT(b + 1)

                for s in range(4):
                    t = 4 * b + s
                    ts = slice(s * 128, (s + 1) * 128)
                    bk, bk_b = k.bank()
                    k.pe_begin(reads=[wA_b, hT_b], writes=[bk_b])
                    ins = None
                    for kk in range(8):
                        ins = nc.tensor.matmul(bk, lhsT=hT[:, kk, ts], rhs=wA[:, kk, 512:1024], start=(kk == 0), stop=(kk == 7))
                    k.pe_end(ins, reads=[wA_b, hT_b], writes=[bk_b])
                    pb2 = []
                    for d in range(2):
                        spt, spt_b = spts[s][d]
                        bank2, bank2_b = k.bank()
                        k.pe_begin(reads=[spt_b, cst_b], writes=[bank2_b])
                        for h in range(4):
                            ins = nc.tensor.matmul(bank2[:, h * 128:(h + 1) * 128], lhsT=spt[:, h * 128:(h + 1) * 128],
                                                   rhs=tri[:, d, :], start=True, stop=True)
                        k.pe_end(ins, reads=[spt_b, cst_b], writes=[bank2_b])
                        bank3, bank3_b = k.bank()
                        k.pe_begin(reads=[spt_b, cst_b], writes=[bank3_b])
                        ins = nc.tensor.matmul(bank3, lhsT=tri[:, 2 + d, :], rhs=spt[:], start=True, stop=True)
                        k.pe_end(ins, reads=[spt_b, cst_b], writes=[bank3_b])
                        pb2.append((bank2, bank2_b, bank3, bank3_b))
                    for d in range(2):
                        bank2, bank2_b, bank3, bank3_b = pb2[d]
                        eb, eb_b = eb_ring[d].next()
                        enb, enb_b = enb_ring[d].next()
                        k.op("act", lambda e: e.activation(out=eb[:], in_=bank2, func=AF.Exp, scale=-1.0 / 16.0),
                             reads=[bank2_b], writes=[eb_b])
                        k.op("act", lambda e: e.activation(out=enb[:], in_=bank2, func=AF.Exp, scale=1.0 / 16.0),
                             reads=[bank2_b], writes=[enb_b])
                        ekd, ekd_b = ekd_ring[d].next()
                        k.op("act", lambda e: e.activation(out=ekd[:], in_=bank3, func=AF.Exp, scale=-1.0 / 16.0),
                             reads=[bank3_b], writes=[ekd_b])
                        col = 127 if d == 0 else 0
                        ebv = eb[:].rearrange("p (h t) -> p h t", h=4)
                        enbv = enb[:].rearrange("p (h t) -> p h t", h=4)
                        k.op("pool", lambda e: e.tensor_copy(out=EB[:, d, t, :], in_=ebv[:, :, col]),
                             reads=[eb_b], writes=[EB_b])
                        qst, qst_b = qst_ring.next()
                        k.op("dve", lambda e: e.scalar_tensor_tensor(
                            out=qst[:].rearrange("p (h t) -> p h t", h=4), in0=qraw[:, :, ts], scalar=DK ** -0.5,
                            in1=ebv, op0=ALU.mult, op1=ALU.mult), reads=[qraw_b, eb_b], writes=[qst_b])
                        k.dma("pool", s_qe[d][t], qst[:], reads=[qst_b], owner=qst_b)
                        kst, kst_b = kst_ring.next()
                        k.op("pool", lambda e: e.tensor_tensor(
                            out=kst[:].rearrange("p (h t) -> p h t", h=4), in0=kraw[:, :, ts], in1=enbv, op=ALU.mult),
                            reads=[kraw_b, enb_b], writes=[kst_b])
                        k.dma("pool", s_ke[d][t], kst[:], reads=[kst_b], owner=kst_b)
                        kdst, kdst_b = kdst_ring.next()
                        k.op("dve", lambda e: e.tensor_tensor(out=kdst[:], in0=bk, in1=ekd[:], op=ALU.mult),
                             reads=[bk_b, ekd_b], writes=[kdst_b])
                        k.dma("pool", s_kd[d][t], kdst[:], reads=[kdst_b], owner=kdst_b)
            k.end_phase()

        with ExitStack() as es:
            def T(name, shape, dt):
                return es.enter_context(nc.sbuf_tensor("sb_" + name + "_%d" % id(es), shape, dt))

            wB = T("wB", [128, 8, 6144], BF16)
            wB_b = k.buf("wB")
            gnorm = T("gnorm", [128, D], F32)
            mnorm = T("mnorm", [128, D], F32)
            pb_b = k.buf("pb_c")
            k.dma("sp", gnorm[:], norms_d[2:3, :].partition_broadcast(128), writes=[pb_b], owner=pb_b)
            k.dma("sp", mnorm[:], norms_d[3:4, :].partition_broadcast(128), writes=[pb_b], owner=pb_b)
            wB_bs = [k.buf("wB%d" % i) for i in range(3)]
            for i, (c0, s0) in enumerate(((0, 1024), (2048, 4128), (4096, 6192))):
                for kk in range(8):
                    rows = slice(kk * 128, (kk + 1) * 128)
                    k.dma("pool", wB[:, kk, c0:c0 + 2048], w_in[rows, s0:s0 + 2048], writes=[wB_bs[i]], owner=wB_bs[i])
            hT_ring = Ring(k, es, "hTb", [128, 8, 512], BF16, 2)
            vst_ring = Ring(k, es, "vst", [128, D], BF16, 2)
            mvst_ring = Ring(k, es, "mvst", [128, D], BF16, 2)
            gst_ring = Ring(k, es, "gst", [128, D], BF16, 3)
            t1_ring = Ring(k, es, "t1", [128, D], F32, 2)
            t2_ring = Ring(k, es, "t2", [128, D], F32, 2)

            CH = min(CONV_CH, S)
            NCH = S // CH
            TPC = CH // 128
            cw = T("cw", [128, 8, 3], F32)
            cb = T("cb", [128, 8], F32)
            cv_b = k.buf("cv_c")
            k.dma("sp", cw[:], mlcw_d[:, :, :], writes=[cv_b], owner=cv_b)
            k.dma("sp", cb[:], mlcb_d[:, :], writes=[cv_b], owner=cv_b)
            pad_ring = Ring(k, es, "pad", [128, CH + 2], F32, 2)
            y_ring = Ring(k, es, "ycv", [128, CH], F32, 2)
            qo_ring = Ring(k, es, "qo", [128, CH], BF16, 2)
            mkc = T("mkc", [128, 4, CH], BF16)
            mkc_b = k.buf("mkc")
            mkst_ring = Ring(k, es, "mkst", [128, 2, 512], BF16, 2)

            def conv_unit(u):
                tc, c = u // 8, u % 8
                rows = slice(c * 128, (c + 1) * 128)
                pd, pd_b = pad_ring.next()
                k.dma("sp", pd[:, 1:CH + 1], s_mqk[rows, tc * CH:(tc + 1) * CH], writes=[pd_b], owner=pd_b)
                if tc == 0:
                    k.op("pool", lambda e: e.memset(pd[:, 0:1], 0.0), writes=[pd_b])
                else:
                    k.dma("sp", pd[:, 0:1], s_mqk[rows, tc * CH - 1:tc * CH], writes=[pd_b], owner=pd_b, allow_slow_non_contiguous=True)
                if tc == NCH - 1:
                    k.op("pool", lambda e: e.memset(pd[:, CH + 1:CH + 2], 0.0), writes=[pd_b])
                else:
                    k.dma("sp", pd[:, CH + 1:CH + 2], s_mqk[rows, (tc + 1) * CH:(tc + 1) * CH + 1], writes=[pd_b], owner=pd_b, allow_slow_non_contiguous=True)
                y, y_b = y_ring.next()
                k.op("dve", lambda e: e.tensor_scalar(out=y[:], in0=pd[:, 1:CH + 1], scalar1=cw[:, c, 1:2], scalar2=cb[:, c:c + 1],
                                                      op0=ALU.mult, op1=ALU.add), reads=[pd_b, cv_b], writes=[y_b])
                k.op("dve", lambda e: e.scalar_tensor_tensor(out=y[:], in0=pd[:, 0:CH], scalar=cw[:, c, 0:1], in1=y[:],
                                                             op0=ALU.mult, op1=ALU.add), reads=[pd_b, cv_b, y_b], writes=[y_b])
                k.op("dve", lambda e: e.scalar_tensor_tensor(out=y[:], in0=pd[:, 2:CH + 2], scalar=cw[:, c, 2:3], in1=y[:],
                                                             op0=ALU.mult, op1=ALU.add), reads=[pd_b, cv_b, y_b], writes=[y_b])
                n0 = tc * TPC
                if c < 4:
                    qo, qo_b = qo_ring.next()
                    k.op("act", lambda e: e.activation(out=qo[:], in_=y[:], func=AF.Silu), reads=[y_b], writes=[qo_b])
                    k.dma("pool", s_mq[n0:n0 + TPC, :, c, :].rearrange("n p t -> p n t"),
                          qo[:].rearrange("p (n t) -> p n t", t=128), reads=[qo_b], owner=qo_b)
                else:
                    h = c - 4
                    k.op("act", lambda e: e.activation(out=mkc[:, h, :], in_=y[:], func=AF.Silu), reads=[y_b], writes=[mkc_b])
                    k.dma("pool", s_mk[n0:n0 + TPC, :, h, :].rearrange("n p t -> p n t"),
                          mkc[:, h, :].rearrange("p (n t) -> p n t", t=128), reads=[mkc_b], owner=mkc_b)
                if c == 7:
                    for n2 in range(0, TPC, 2):
                        nn = min(2, TPC - n2)
                        bank, bank_b = k.bank()
                        bankb = bank.bitcast(BF16)
                        k.pe_begin(reads=[mkc_b, cst_b], writes=[bank_b])
                        ins = None
                        for j in range(nn):
                            for h in range(4):
                                ins = nc.tensor.transpose(out=bankb[:, (j * 4 + h) * 128:(j * 4 + h + 1) * 128],
                                                          in_=mkc[:, h, (n2 + j) * 128:(n2 + j + 1) * 128], identity=identb[:])
                        k.pe_end(ins, reads=[mkc_b, cst_b], writes=[bank_b])
                        ms_, ms_b = mkst_ring.next()
                        k.op("act", lambda e: e.activation(out=ms_[:, 0:nn, :],
                                                           in_=bankb[:, 0:nn * 512].rearrange("p (j c) -> p j c", c=512), func=AF.Copy),
                             reads=[bank_b], writes=[ms_b])
                        k.dma("pool", s_mktm[n0 + n2:n0 + n2 + nn].rearrange("n p c -> p n c"), ms_[:, 0:nn, :],
                              reads=[ms_b], owner=ms_b)

            NUNIT = NCH * 8
            units_done = [0]

            for b in range(NB):
                hT, hT_b = hT_ring.next()
                k.dma("sp", hT[:], s_hT[b], writes=[hT_b], owner=hT_b)
                for s in range(4):
                    t = 4 * b + s
                    ts = slice(s * 128, (s + 1) * 128)
                    target = ((t + 1) * NUNIT + NT - 1) // NT
                    while units_done[0] < min(target, NUNIT):
                        conv_unit(units_done[0])
                        units_done[0] += 1

                    def tm_group(c0):
                        res = []
                        for half in range(2):
                            bank, bank_b = k.bank()
                            wbb = wB_bs[c0 // 2048]
                            k.pe_begin(reads=[wbb, hT_b], writes=[bank_b])
                            ins = None
                            for kk in range(8):
                                ins = nc.tensor.matmul(bank, lhsT=hT[:, kk, ts],
                                                       rhs=wB[:, kk, c0 + half * 512:c0 + (half + 1) * 512],
                                                       start=(kk == 0), stop=(kk == 7))
                            k.pe_end(ins, reads=[wbb, hT_b], writes=[bank_b])
                            res.append((bank, bank_b))
                        return res

                    def act_evac(banks, dst, dst_b, func, scale=1.0):
                        for half in range(2):
                            bank, bank_b = banks[half]
                            k.op("act", lambda e: e.activation(out=dst[:, half * 512:(half + 1) * 512], in_=bank,
                                                               func=func, scale=scale), reads=[bank_b], writes=[dst_b])

                    banks = tm_group(0)
                    vst, vst_b = vst_ring.next()
                    act_evac(banks, vst, vst_b, AF.Copy)
                    k.dma("pool", s_v[t], vst[:], reads=[vst_b], owner=vst_b)
                    banks = tm_group(2048)
                    mvst, mvst_b = mvst_ring.next()
                    for half in range(2):
                        bank, bank_b = banks[half]
                        k.op("dve", lambda e: e.tensor_copy(out=mvst[:, half * 512:(half + 1) * 512], in_=bank),
                             reads=[bank_b], writes=[mvst_b])
                    k.dma("pool", s_mv[t], mvst[:], reads=[mvst_b], owner=mvst_b)
                    banks = tm_group(1024)
                    t1, t1_b = t1_ring.next()
                    act_evac(banks, t1, t1_b, AF.Silu)
                    banks = tm_group(4096)
                    t2, t2_b = t2_ring.next()
                    act_evac(banks, t2, t2_b, AF.Tanh, 0.5)
                    k.op("dve", lambda e: e.scalar_tensor_tensor(out=t1[:], in0=t2[:], scalar=1.0, in1=t1[:],
                                                                 op0=ALU.add, op1=ALU.mult), reads=[t1_b, t2_b], writes=[t1_b])
                    gst, gst_b = gst_ring.next()
                    k.op("dve", lambda e: e.scalar_tensor_tensor(out=gst[:], in0=t1[:], scalar=0.5, in1=gnorm[:],
                                                                 op0=ALU.mult, op1=ALU.mult), reads=[t1_b, pb_b], writes=[gst_b])
                    k.dma("pool", s_GA[t], gst[:], reads=[gst_b], owner=gst_b)
                    banks = tm_group(3072)
                    t1, t1_b = t1_ring.next()
                    act_evac(banks, t1, t1_b, AF.Tanh, 0.5)
                    banks = tm_group(5120)
                    t2, t2_b = t2_ring.next()
                    act_evac(banks, t2, t2_b, AF.Tanh, 0.5)
                    k.op("pool", lambda e: e.tensor_scalar(out=t1[:], in0=t1[:], scalar1=1.0, scalar2=0.25,
                                                           op0=ALU.add, op1=ALU.mult), reads=[t1_b], writes=[t1_b])
                    k.op("dve", lambda e: e.scalar_tensor_tensor(out=t1[:], in0=t2[:], scalar=1.0, in1=t1[:],
                                                                 op0=ALU.add, op1=ALU.mult), reads=[t1_b, t2_b], writes=[t1_b])
                    gst, gst_b = gst_ring.next()
                    k.op("pool", lambda e: e.tensor_tensor(out=gst[:], in0=t1[:], in1=mnorm[:], op=ALU.mult),
                         reads=[t1_b, pb_b], writes=[gst_b])
                    k.dma("pool", s_GB[t], gst[:], reads=[gst_b], owner=gst_b)
            k.end_phase()

        with ExitStack() as es:
            def T(name, shape, dt):
                return es.enter_context(nc.sbuf_tensor("sb_" + name + "_%d" % id(es), shape, dt))

            wc_b = Buf("wcast")
            for kk in range(8):
                k.dma("pool", s_wup[kk * 128:(kk + 1) * 128, :], w_up_d[kk * 128:(kk + 1) * 128, :], owner=wc_b)
            for kk in range(32):
                k.dma("pool", s_wdn[kk * 128:(kk + 1) * 128, :], w_down_d[kk * 128:(kk + 1) * 128, :], owner=wc_b)
            G = T("G", [4, 4, S], F32)
            gb = T("gb", [4, 4], F32)
            ones4 = T("ones4", [4, S], F32)
            Ssp = [T("Ssp%d" % d, [4, S], F32) for d in range(2)]
            uu = [T("uu%d" % d, [4, S], F32) for d in range(2)]
            Mg = [T("Mg%d" % d, [4, S], F32) for d in range(2)]
            dec = [T("dec%d" % d, [4, NT], F32) for d in range(2)]
            Gk_b = [k.buf("Gk%d" % i) for i in range(4)]
            gb_b = k.buf("gb")
            on_b = k.buf("ones4")
            g_bd = [k.buf("gwork%d" % d) for d in range(2)]
            for i in range(4):
                k.dma("sp", G[:, i, :], s_gate[4 * i:4 * i + 4, :], writes=[Gk_b[i]], owner=Gk_b[i])
            k.dma("sp", gb[:], gbias_d[:, :], writes=[gb_b], owner=gb_b)
            k.op("pool", lambda e: e.memset(ones4[:], 1.0), writes=[on_b])

            def rvd(d, ap):
                return ap[:, ::-1] if d == 1 else ap

            for d in range(2):
                ipre = G[:, d, :]
                k.op("dve", lambda e: e.tensor_scalar(out=ipre, in0=ipre, scalar1=gb[:, d:d + 1], scalar2=None, op0=ALU.add),
                     reads=[Gk_b[d], gb_b], writes=[Gk_b[d]])
            for d in range(2):
                fpre = G[:, 2 + d, :]
                k.op("dve", lambda e: e.tensor_scalar(out=fpre, in0=fpre, scalar1=gb[:, 2 + d:3 + d], scalar2=None, op0=ALU.add),
                     reads=[Gk_b[2 + d], gb_b], writes=[Gk_b[2 + d]])
            for d in range(2):
                fpre = G[:, 2 + d, :]
                k.op("act", lambda e: e.activation(out=fpre, in_=fpre, func=AF.Exp, scale=-1.0),
                     reads=[Gk_b[2 + d]], writes=[Gk_b[2 + d]])
            for d in range(2):
                fpre = G[:, 2 + d, :]
                k.op("act", lambda e: e.activation(out=fpre, in_=fpre, func=AF.Ln, bias=1.0),
                     reads=[Gk_b[2 + d]], writes=[Gk_b[2 + d]])
            for d in range(2):
                fpre = G[:, 2 + d, :]
                k.op("dve", lambda e: e.tensor_tensor_scan(out=rvd(d, Ssp[d][:]), data0=rvd(d, ones4[:]), data1=rvd(d, fpre),
                                                           initial=0.0, op0=ALU.mult, op1=ALU.add),
                     reads=[Gk_b[2 + d], on_b], writes=[g_bd[d]])
            for d in range(2):
                ipre = G[:, d, :]
                k.op("dve", lambda e: e.tensor_tensor(out=uu[d][:], in0=ipre, in1=Ssp[d][:], op=ALU.add),
                     reads=[Gk_b[d], g_bd[d]], writes=[g_bd[d]])
            for d in range(2):
                k.op("dve", lambda e: e.tensor_tensor_scan(out=rvd(d, Mg[d][:]), data0=rvd(d, ones4[:]), data1=rvd(d, uu[d][:]),
                                                           initial=-1e30, op0=ALU.mult, op1=ALU.max),
                     reads=[g_bd[d], on_b], writes=[g_bd[d]])
            views = []
            for d in range(2):
                endc = 127 if d == 0 else 0
                Mgv = Mg[d][:].rearrange("p (n t) -> p n t", t=128)
                Mnb = Mgv[:, :, endc:endc + 1].to_broadcast([4, NT, 128])
                uv = uu[d][:].rearrange("p (n t) -> p n t", t=128)
                sv = Ssp[d][:].rearrange("p (n t) -> p n t", t=128)
                views.append((Mgv, Mnb, uv, sv, endc))
            for d in range(2):
                Mgv, Mnb, uv, sv, endc = views[d]
                k.op("dve", lambda e: e.tensor_tensor(out=uv, in0=uv, in1=Mnb, op=ALU.subtract), reads=[g_bd[d]], writes=[g_bd[d]])
            for d in range(2):
                k.op("act", lambda e: e.activation(out=uu[d][:], in_=uu[d][:], func=AF.Exp, bias=LN_QS),
                     reads=[g_bd[d]], writes=[g_bd[d]])
            for d in range(2):
                Mgv, Mnb, uv, sv, endc = views[d]
                k.op("dve", lambda e: e.tensor_tensor(out=sv, in0=sv, in1=Mnb, op=ALU.subtract), reads=[g_bd[d]], writes=[g_bd[d]])
            for d in range(2):
                k.op("act", lambda e: e.activation(out=Ssp[d][:], in_=Ssp[d][:], func=AF.Exp), reads=[g_bd[d]], writes=[g_bd[d]])
            for d in range(2):
                Mgv, Mnb, uv, sv, endc = views[d]
                k.op("dve", lambda e: e.memset(dec[d][:], 0.0), reads=[g_bd[d]], writes=[g_bd[d]])
                if NT > 1:
                    Mn2 = Mgv[:, :, endc]
                    if d == 0:
                        k.op("dve", lambda e: e.tensor_tensor(out=dec[d][:, 1:NT], in0=Mn2[:, 0:NT - 1], in1=Mn2[:, 1:NT],
                                                              op=ALU.subtract), reads=[g_bd[d]], writes=[g_bd[d]])
                        k.op("act", lambda e: e.activation(out=dec[d][:, 1:NT], in_=dec[d][:, 1:NT], func=AF.Exp),
                             reads=[g_bd[d]], writes=[g_bd[d]])
                    else:
                        k.op("dve", lambda e: e.tensor_tensor(out=dec[d][:, 0:NT - 1], in0=Mn2[:, 1:NT], in1=Mn2[:, 0:NT - 1],
                                                              op=ALU.subtract), reads=[g_bd[d]], writes=[g_bd[d]])
                        k.op("act", lambda e: e.activation(out=dec[d][:, 0:NT - 1], in_=dec[d][:, 0:NT - 1], func=AF.Exp),
                             reads=[g_bd[d]], writes=[g_bd[d]])
            for d in range(2):
                for src, dst, dst_b in ((uu[d], w_tm, wtm_b), (Ssp[d], thr_tm, thr_b)):
                    bank, bank_b = k.bank()
                    k.pe_begin(reads=[g_bd[d], cst_b], writes=[bank_b])
                    ins = None
                    for n in range(NT):
                        ins = nc.tensor.matmul(bank[:, n * 4:(n + 1) * 4], lhsT=src[:, n * 128:(n + 1) * 128],
                                               rhs=identf[0:4, 0:4], start=True, stop=True)
                    k.pe_end(ins, reads=[g_bd[d], cst_b], writes=[bank_b])
                    k.op("dve", lambda e: e.tensor_copy(out=dst[:, d, :, :],
                                                        in_=bank[:, 0:NT * 4].rearrange("p (n h) -> p n h", h=4)),
                         reads=[bank_b], writes=[dst_b])
                bank, bank_b = k.bank()
                k.pe_begin(reads=[g_bd[d], cst_b], writes=[bank_b])
                for h in range(4):
                    ins = nc.tensor.matmul(bank[:, h * NT:(h + 1) * NT], lhsT=sel[:, h, :], rhs=dec[d][:], start=True, stop=True)
                k.pe_end(ins, reads=[g_bd[d], cst_b], writes=[bank_b])
                k.op("dve", lambda e: e.tensor_copy(out=dec_bc[:, d, :, :],
                                                    in_=bank[:, 0:4 * NT].rearrange("p (h n) -> p h n", h=4)),
                     reads=[bank_b], writes=[dec_b])

            k.end_phase()

        with ExitStack() as es:
            def T(name, shape, dt):
                return es.enter_context(nc.sbuf_tensor("sb_" + name + "_%d" % id(es), shape, dt))

            Sst = T("Sst", [128, 4, 256], F32)
            Sbf = [T("Sbf%d" % i, [128, 4, 256], BF16) for i in range(2)]
            Cst = T("Cst", [128, 4, 257], F32)
            Cbf = [T("Cbf%d" % i, [128, 4, 257], BF16) for i in range(2)]
            S_b = [k.buf("S%d" % h) for h in range(4)]
            Sbf_b = [k.buf("Sbf%d" % i) for i in range(2)]
            C_b = [k.buf("C%d" % h) for h in range(4)]
            Cbf_b = [k.buf("Cbf%d" % i) for i in range(2)]
            NL = 3
            ld = {}
            for nm, shp in (("qe", [128, 512]), ("ke", [128, 512]), ("kd", [128, 512]), ("v", [128, 1024]),
                            ("mq", [128, 512]), ("mk", [128, 512]), ("mktm", [128, 512]), ("mv", [128, 1024])):
                ld[nm] = [T("ld_%s%d" % (nm, i), shp, BF16) for i in range(NL)]
            ld_b = [k.buf("ld%d" % i) for i in range(NL)]
            ld_i = [0]
            vw_ring = Ring(k, es, "vw", [128, 4, 257], BF16, 3)
            at_ring = Ring(k, es, "at", [128, 512], BF16, 4)
            qd_ring = Ring(k, es, "qd", [128, 128], BF16, 3)
            dn_ring = Ring(k, es, "dn", [128, 16], F32, 4)
            oA_ring = Ring(k, es, "oA", [128, D], F32, 2)
            hB_ring = Ring(k, es, "hB", [128, D], F32, 2)
            obA_t = [k.buf("obA_t%d" % n) for n in range(NT)]
            obB_t = [k.buf("obB_t%d" % n) for n in range(NT)]
            for d in (1, 0):
                order = list(range(NT)) if d == 0 else list(range(NT - 1, -1, -1))
                maskb = tri[:, d:d + 1, :].to_broadcast([128, 4, 128])
                first = True
                def issue_loads(si2):
                    n2 = order[si2]
                    li = ld_i[0]
                    ld_i[0] = (li + 1) % NL
                    lb2 = ld_b[li]
                    L2 = {nm: ld[nm][li] for nm in ld}
                    k.dma("sp", L2["qe"][:], s_qe[d][n2], writes=[lb2], owner=lb2)
                    k.dma("sp", L2["ke"][:], s_ke[d][n2], writes=[lb2], owner=lb2)
                    k.dma("sp", L2["kd"][:], s_kd[d][n2], writes=[lb2], owner=lb2)
                    k.dma("sp", L2["v"][:], s_v[n2], writes=[lb2], owner=lb2)
                    k.dma("sp", L2["mq"][:], s_mq[n2].rearrange("p h t -> p (h t)"), writes=[lb2], owner=lb2)
                    k.dma("sp", L2["mk"][:], s_mk[n2].rearrange("p h t -> p (h t)"), writes=[lb2], owner=lb2)
                    k.dma("sp", L2["mktm"][:], s_mktm[n2], writes=[lb2], owner=lb2)
                    k.dma("sp", L2["mv"][:], s_mv[n2], writes=[lb2], owner=lb2)
                    vw2, vw2_b = vw_ring.next()
                    k.op("pool", lambda e: e.tensor_tensor(
                        out=vw2[:, :, 0:256], in0=L2["mv"][:].rearrange("p (h c) -> p h c", h=4),
                        in1=w_tm[:, d, n2, :].unsqueeze(2).to_broadcast([128, 4, 256]), op=ALU.mult),
                        reads=[lb2, wtm_b], writes=[vw2_b])
                    k.op("pool", lambda e: e.tensor_copy(out=vw2[:, :, 256], in_=w_tm[:, d, n2, :]), reads=[wtm_b], writes=[vw2_b])
                    return (lb2, L2, vw2, vw2_b)

                pending = issue_loads(0)
                for si, n in enumerate(order):
                    nxt = order[si + 1] if si + 1 < NT else None
                    cur = si % 2
                    prv = 1 - cur
                    lb, L, vw, vw_b = pending
                    if nxt is not None:
                        pending = issue_loads(si + 1)
                    oA, oA_b = oA_ring.next()
                    hB, hB_b = hB_ring.next()
                    s1, s1_b = k.bank()
                    k.pe_begin(reads=[lb], writes=[s1_b])
                    for h in range(4):
                        hk = slice(h * 128, (h + 1) * 128)
                        ins = nc.tensor.matmul(s1[:, hk], lhsT=L["ke"][:, hk], rhs=L["qe"][:, hk], start=True, stop=True)
                    k.pe_end(ins, reads=[lb], writes=[s1_b])
                    s2, s2_b = k.bank()
                    k.pe_begin(reads=[lb], writes=[s2_b])
                    for h in range(4):
                        hk = slice(h * 128, (h + 1) * 128)
                        ins = nc.tensor.matmul(s2[:, hk], lhsT=L["mk"][:, hk], rhs=L["mq"][:, hk], start=True, stop=True)
                    k.pe_end(ins, reads=[lb], writes=[s2_b])
                    ub = []
                    for g2 in range(2):
                        bu, bu_b = k.bank()
                        k.pe_begin(reads=[lb], writes=[bu_b])
                        for hh in range(2):
                            h = 2 * g2 + hh
                            ins = nc.tensor.matmul(bu[:, hh * 256:(hh + 1) * 256], lhsT=L["kd"][:, h * 128:(h + 1) * 128],
                                                   rhs=L["v"][:, h * 256:(h + 1) * 256], start=True, stop=True)
                        k.pe_end(ins, reads=[lb], writes=[bu_b])
                        ub.append((bu, bu_b))
                    u2 = []
                    for h in range(4):
                        bu2, bu2_b = k.bank()
                        k.pe_begin(reads=[lb, vw_b], writes=[bu2_b])
                        ins = nc.tensor.matmul(bu2[:, 0:257], lhsT=L["mktm"][:, h * 128:(h + 1) * 128], rhs=vw[:, h, :],
                                               start=True, stop=True)
                        k.pe_end(ins, reads=[lb, vw_b], writes=[bu2_b])
                        u2.append((bu2, bu2_b))
                    at1, at1_b = at_ring.next()
                    k.op("dve", lambda e: e.tensor_tensor(out=at1[:].rearrange("p (h t) -> p h t", h=4),
                                                          in0=s1.rearrange("p (h t) -> p h t", h=4), in1=maskb, op=ALU.mult),
                         reads=[s1_b, cst_b], writes=[at1_b])
                    at2, at2_b = at_ring.next()
                    k.op("dve", lambda e: e.tensor_tensor(out=at2[:].rearrange("p (h t) -> p h t", h=4),
                                                          in0=s2.rearrange("p (h t) -> p h t", h=4), in1=maskb, op=ALU.mult),
                         reads=[s2_b, cst_b], writes=[at2_b])
                    for h in range(4):
                        bu, bu_b = ub[h // 2]
                        src = bu[:, (h % 2) * 256:(h % 2 + 1) * 256]
                        if first:
                            k.op("dve", lambda e: e.tensor_copy(out=Sst[:, h, :], in_=src), reads=[bu_b], writes=[S_b[h]])
                        else:
                            k.op("dve", lambda e: e.scalar_tensor_tensor(out=Sst[:, h, :], in0=Sst[:, h, :],
                                                                         scalar=EB[:, d, n, h:h + 1], in1=src,
                                                                         op0=ALU.mult, op1=ALU.add),
                                 reads=[bu_b, S_b[h], EB_b], writes=[S_b[h]])
                    for h in range(4):
                        bu2, bu2_b = u2[h]
                        if first:
                            k.op("dve", lambda e: e.tensor_copy(out=Cst[:, h, :], in_=bu2[:, 0:257]),
                                 reads=[bu2_b], writes=[C_b[h]])
                        else:
                            k.op("dve", lambda e: e.scalar_tensor_tensor(out=Cst[:, h, :], in0=Cst[:, h, :],
                                                                         scalar=dec_bc[:, d, h, n:n + 1], in1=bu2[:, 0:257],
                                                                         op0=ALU.mult, op1=ALU.add),
                                 reads=[bu2_b, C_b[h], dec_b], writes=[C_b[h]])
                    ob = []
                    for g2 in range(2):
                        bo, bo_b = k.bank()
                        rd = [at1_b, lb] + ([] if first else [Sbf_b[prv]])
                        k.pe_begin(reads=rd, writes=[bo_b])
                        for hh in range(2):
                            h = 2 * g2 + hh
                            hk = slice(h * 128, (h + 1) * 128)
                            dst = bo[:, hh * 256:(hh + 1) * 256]
                            ins = nc.tensor.matmul(dst, lhsT=at1[:, hk], rhs=L["v"][:, h * 256:(h + 1) * 256],
                                                   start=True, stop=first)
                            if not first:
                                ins = nc.tensor.matmul(dst, lhsT=L["qe"][:, hk], rhs=Sbf[prv][:, h, :], start=False, stop=True)
                        k.pe_end(ins, reads=rd, writes=[bo_b])
                        ob.append((bo, bo_b))
                    nb = []
                    for h in range(4):
                        hk = slice(h * 128, (h + 1) * 128)
                        bn, bn_b = k.bank()
                        rd = [at2_b, vw_b, lb] + ([] if first else [Cbf_b[prv]])
                        k.pe_begin(reads=rd, writes=[bn_b])
                        ins = nc.tensor.matmul(bn[:, 0:257], lhsT=at2[:, hk], rhs=vw[:, h, :], start=True, stop=first)
                        if not first:
                            ins = nc.tensor.matmul(bn[:, 0:257], lhsT=L["mq"][:, hk], rhs=Cbf[prv][:, h, :], start=False, stop=True)
                        k.pe_end(ins, reads=rd, writes=[bn_b])
                        nb.append((bn, bn_b))
                    if nxt is not None:
                        k.op("act", lambda e: e.activation(out=Sbf[cur][:], in_=Sst[:], func=AF.Copy), reads=S_b, writes=[Sbf_b[cur]])
                        k.op("pool", lambda e: e.tensor_tensor(
                            out=Cbf[cur][:], in0=Cst[:], in1=dec_bc[:, d, :, nxt:nxt + 1].to_broadcast([128, 4, 257]),
                            op=ALU.mult), reads=C_b + [dec_b], writes=[Cbf_b[cur]])
                    for g2 in range(2):
                        bo, bo_b = ob[g2]
                        k.op("act", lambda e: e.activation(out=oA[:, g2 * 512:(g2 + 1) * 512], in_=bo, func=AF.Copy),
                             reads=[bo_b], writes=[oA_b])
                    dn, dn_b = dn_ring.next()
                    for h in range(4):
                        bn, bn_b = nb[h]
                        k.op("act", lambda e: e.activation(out=dn[:, h:h + 1], in_=bn[:, 256:257], func=AF.Copy),
                             reads=[bn_b], writes=[dn_b])
                    k.op("dve", lambda e: e.scalar_tensor_tensor(out=dn[:, 4:8], in0=dn[:, 0:4], scalar=-1.0, in1=dn[:, 0:4],
                                                                 op0=ALU.mult, op1=ALU.max), reads=[dn_b], writes=[dn_b])
                    k.op("dve", lambda e: e.tensor_tensor(out=dn[:, 8:12], in0=dn[:, 4:8], in1=thr_tm[:, d, n, :], op=ALU.max),
                         reads=[dn_b, thr_b], writes=[dn_b])
                    k.op("dve", lambda e: e.reciprocal(out=dn[:, 12:16], in_=dn[:, 8:12]), reads=[dn_b], writes=[dn_b])
                    for h in range(4):
                        bn, bn_b = nb[h]
                        k.op("act", lambda e: e.activation(out=hB[:, h * 256:(h + 1) * 256], in_=bn[:, 0:256], func=AF.Copy,
                                                           scale=dn[:, 12 + h:13 + h]),
                             reads=[bn_b, dn_b], writes=[hB_b])
                    if d == 1:
                        k.dma("pool", s_obA[n], oA[:], reads=[oA_b], writes=[obA_t[n]], owner=oA_b)
                        k.dma("pool", s_obB[n], hB[:], reads=[hB_b], writes=[obB_t[n]], owner=hB_b)
                    else:
                        k.dma("pool", s_obA[n], oA[:], reads=[oA_b], writes=[obA_t[n]], owner=oA_b, accum_op=ALU.add)
                        k.dma("pool", s_obB[n], hB[:], reads=[hB_b], writes=[obB_t[n]], owner=hB_b, accum_op=ALU.add)
                    first = False
            k.end_phase()

        with ExitStack() as es:
            def T(name, shape, dt):
                return es.enter_context(nc.sbuf_tensor("sb_" + name + "_%d" % id(es), shape, dt))

            wout = T("wout", [128, 8, D], BF16)
            wpost = T("wpost", [128, D], F32)
            wffn = T("wffn", [128, D], F32)
            zero2 = T("zero2", [128, 8, 1], BF16)
            p2_b = k.buf("p2_c")
            for kk in range(8):
                k.dma("pool", wout[:, kk, :], w_out_d[kk * 128:(kk + 1) * 128, :], writes=[p2_b], owner=p2_b)
            k.dma("sp", wpost[:], norms_d[1:2, :].partition_broadcast(128), writes=[p2_b], owner=p2_b)
            k.dma("sp", wffn[:], norms_d[4:5, :].partition_broadcast(128), writes=[p2_b], owner=p2_b)
            z_b = k.buf("zero2")
            k.op("pool", lambda e: e.memset(zero2[:], 0.0), writes=[z_b])
            k.dma("pool", s_h2T[:, :, 0:1], zero2[:], reads=[z_b], owner=z_b, allow_slow_non_contiguous=True)
            k.dma("pool", s_h2T[:, :, S + 1:S + 2], zero2[:], reads=[z_b], owner=z_b, allow_slow_non_contiguous=True)

            RD = 11
            A_ring = Ring(k, es, "cA", [128, D], F32, RD)
            B_ring = Ring(k, es, "cB", [128, D], F32, RD)
            GA_ring = Ring(k, es, "cGA", [128, D], BF16, 6)
            GB_ring = Ring(k, es, "cGB", [128, D], BF16, 6)
            X_ring = Ring(k, es, "cX", [128, D], F32, 4)
            ss_ring = Ring(k, es, "css", [128, 32], F32, RD)
            ybf_ring = Ring(k, es, "cybf", [128, D], BF16, 3)
            yT_ring = Ring(k, es, "cyT", [128, 8, 128], BF16, 3)
            h2_ring = Ring(k, es, "ch2", [128, D], BF16, 3)
            h2T_ring = Ring(k, es, "ch2T", [128, 8, 128], BF16, 3)
            ctxs = {}

            def f0(n):
                c = {"n": n}
                c["A"], c["A_b"] = A_ring.next()
                c["B"], c["B_b"] = B_ring.next()
                c["GA"], c["GA_b"] = GA_ring.next()
                c["GB"], c["GB_b"] = GB_ring.next()
                c["ss"], c["ss_b"] = ss_ring.next()
                k.dma("sp", c["A"][:], s_obA[n], writes=[c["A_b"]], owner=c["A_b"])
                k.dma("sp", c["B"][:], s_obB[n], writes=[c["B_b"]], owner=c["B_b"])
                k.dma("sp", c["GA"][:], s_GA[n], writes=[c["GA_b"]], owner=c["GA_b"])
                k.dma("sp", c["GB"][:], s_GB[n], writes=[c["GB_b"]], owner=c["GB_b"])
                ctxs[n] = c

            def f1(n):
                c = ctxs[n]
                A, A_b, B, B_b, ss, ss_b = c["A"], c["A_b"], c["B"], c["B_b"], c["ss"], c["ss_b"]
                for h in range(4):
                    k.op("act", lambda e: e.activation(out=junk[:, 0:256], in_=A[:, h * 256:(h + 1) * 256], func=AF.Square,
                                                       accum_out=ss[:, h:h + 1]), reads=[A_b], writes=[ss_b])
                for h in range(4):
                    k.op("act", lambda e: e.activation(out=junk[:, 0:256], in_=B[:, h * 256:(h + 1) * 256], func=AF.Square,
                                                       accum_out=ss[:, 4 + h:5 + h]), reads=[B_b], writes=[ss_b])

            def f2(n):
                c = ctxs[n]
                A, A_b, B, B_b, ss, ss_b = c["A"], c["A_b"], c["B"], c["B_b"], c["ss"], c["ss_b"]
                GA, GA_b, GBt, GB_b = c["GA"], c["GA_b"], c["GB"], c["GB_b"]
                rstd_act(ss[:, 0:8], ss[:, 8:16], ss[:, 16:24], ss_b, 1.0 / DV, n=8)
                k.op("pool", lambda e: e.tensor_tensor(out=A[:], in0=A[:], in1=GA[:], op=ALU.mult),
                     reads=[A_b, GA_b], writes=[A_b])
                k.op("pool", lambda e: e.tensor_tensor(out=B[:], in0=B[:], in1=GBt[:], op=ALU.mult),
                     reads=[B_b, GB_b], writes=[B_b])

            def f3(n):
                c = ctxs[n]
                A, A_b, B, B_b, ss, ss_b = c["A"], c["A_b"], c["B"], c["B_b"], c["ss"], c["ss_b"]
                c["ybf"], c["ybf_b"] = ybf_ring.next()
                ybf = c["ybf"]
                for h in range(4):
                    hs = slice(h * 256, (h + 1) * 256)
                    k.op("dve", lambda e: e.tensor_scalar(out=B[:, hs], in0=B[:, hs], scalar1=ss[:, 20 + h:21 + h],
                                                          scalar2=None, op0=ALU.mult),
                         reads=[B_b, ss_b], writes=[B_b])
                for h in range(4):
                    hs = slice(h * 256, (h + 1) * 256)
                    k.op("dve", lambda e: e.scalar_tensor_tensor(out=ybf[:, hs], in0=A[:, hs], scalar=ss[:, 16 + h:17 + h],
                                                                 in1=B[:, hs], op0=ALU.mult, op1=ALU.add),
                         reads=[A_b, B_b, ss_b], writes=[c["ybf_b"]])

            def f4(n):
                c = ctxs[n]
                c["yT"], c["yT_b"] = yT_ring.next()
                transpose8(c["ybf"], c["ybf_b"], c["yT"][:], c["yT_b"], "act")
                c["X"], c["X_b"] = X_ring.next()
                k.dma("sp", c["X"][:], x_d[n * 128:(n + 1) * 128, :], writes=[c["X_b"]], owner=c["X_b"])

            def f5(n):
                c = ctxs[n]
                ss, ss_b = c["ss"], c["ss_b"]
                yT, yT_b = c["yT"], c["yT_b"]
                banks = []
                for half in range(2):
                    bank, bank_b = k.bank()
                    k.pe_begin(reads=[yT_b, p2_b], writes=[bank_b])
                    ins = None
                    for kk in range(8):
                        ins = nc.tensor.matmul(bank, lhsT=yT[:, kk, :], rhs=wout[:, kk, half * 512:(half + 1) * 512],
                                               start=(kk == 0), stop=(kk == 7))
                    k.pe_end(ins, reads=[yT_b, p2_b], writes=[bank_b])
                    banks.append((bank, bank_b))
                c["banks"] = banks
                for half in range(2):
                    bank, bank_b = banks[half]
                    k.op("act", lambda e: e.activation(out=junk[:, 0:512], in_=bank, func=AF.Square,
                                                       accum_out=ss[:, 24 + half:25 + half]), reads=[bank_b], writes=[ss_b])

            def f6(n):
                c = ctxs[n]
                ss, ss_b = c["ss"], c["ss_b"]
                k.op("pool", lambda e: e.tensor_tensor(out=ss[:, 26:27], in0=ss[:, 24:25], in1=ss[:, 25:26], op=ALU.add),
                     reads=[ss_b], writes=[ss_b])
                rstd_act(ss[:, 26:27], ss[:, 27:28], ss[:, 28:29], ss_b, 1.0 / D)

            def f7(n):
                c = ctxs[n]
                A, A_b, B, B_b, ss, ss_b, X, X_b = c["A"], c["A_b"], c["B"], c["B_b"], c["ss"], c["ss_b"], c["X"], c["X_b"]
                for half in range(2):
                    bank, bank_b = c["banks"][half]
                    cs = slice(half * 512, (half + 1) * 512)
                    k.op("dve", lambda e: e.scalar_tensor_tensor(out=A[:, cs], in0=bank, scalar=ss[:, 28:29], in1=wpost[:, cs],
                                                                 op0=ALU.mult, op1=ALU.mult),
                         reads=[bank_b, ss_b, p2_b], writes=[A_b])
                k.op("pool", lambda e: e.tensor_tensor(out=B[:], in0=A[:], in1=X[:], op=ALU.add),
                     reads=[A_b, X_b], writes=[B_b])
                k.dma("pool", s_x1[n], B[:], reads=[B_b], owner=B_b)
                k.op("act", lambda e: e.activation(out=junk[:], in_=B[:], func=AF.Square, accum_out=ss[:, 29:30]),
                     reads=[B_b], writes=[ss_b])

            def f8(n):
                c = ctxs[n]
                ss, ss_b = c["ss"], c["ss_b"]
                rstd_act(ss[:, 29:30], ss[:, 30:31], ss[:, 31:32], ss_b, 1.0 / D)

            def f9(n):
                c = ctxs[n]
                B, B_b, ss, ss_b = c["B"], c["B_b"], c["ss"], c["ss_b"]
                h2, h2_b = h2_ring.next()
                k.op("dve", lambda e: e.scalar_tensor_tensor(out=h2[:], in0=B[:], scalar=ss[:, 31:32], in1=wffn[:],
                                                             op0=ALU.mult, op1=ALU.mult),
                     reads=[B_b, ss_b, p2_b], writes=[h2_b])
                h2T, h2T_b = h2T_ring.next()
                transpose8(h2, h2_b, h2T[:], h2T_b, "act")
                k.dma("pool", s_h2T[:, :, 1 + n * 128:1 + (n + 1) * 128], h2T[:], reads=[h2T_b], owner=h2T_b)
                del ctxs[n]

            stages = [f0, f1, f2, f3, f4, f5, f6, f7, f8, f9]
            NSTG = len(stages)
            GS = 1
            LAG = 1
            groups = [list(range(g0, min(NT, g0 + GS))) for g0 in range(0, NT, GS)]
            NG = len(groups)
            for tau in range((NG - 1) * LAG + NSTG):
                for g in range(NG):
                    st = tau - g * LAG
                    if 0 <= st < NSTG:
                        for t in groups[g]:
                            stages[st](t)
            k.end_phase()

        with ExitStack() as es:
            def T(name, shape, dt):
                return es.enter_context(nc.sbuf_tensor("sb_" + name + "_%d" % id(es), shape, dt))

            wg = T("wg", [128, 8, D], BF16)
            wp = T("wp", [128, 2, D], BF16)
            fcw = T("fcw", [128, 64, 3], F32)
            fcb = T("fcb", [128, 64], F32)
            wfpost = T("wfpost", [128, D], F32)
            wple = T("wple", [128, D], F32)
            p3_b = k.buf("p3_c")
            pw_b = k.buf("p3_w")
            k.dma("sp", fcw[:], ffcw_d[:, :, :], writes=[p3_b], owner=p3_b)
            k.dma("sp", fcb[:], ffcb_d[:, :], writes=[p3_b], owner=p3_b)
            k.dma("sp", wfpost[:], norms_d[5:6, :].partition_broadcast(128), writes=[p3_b], owner=p3_b)
            k.dma("sp", wple[:], norms_d[6:7, :].partition_broadcast(128), writes=[p3_b], owner=p3_b)
            h2_ring = Ring(k, es, "h2b", [128, 8, 514], BF16, 1)
            gT = T("gT", [128, 32, 512], BF16)
            gT_bs = [k.buf("gT%d" % i) for i in range(8)]
            wug_ring = Ring(k, es, "wug", [128, 2, 8, 512], BF16, 2)
            wd_ring = Ring(k, es, "wdn", [128, 4, D], BF16, 2)
            yg_ring = Ring(k, es, "yg", [128, 256], F32, 3)
            yv_ring = Ring(k, es, "yv", [128, 256], F32, 3)
            gl_ring = Ring(k, es, "gl", [128, 256], F32, 2)
            x1_ring = Ring(k, es, "x1f", [128, D], F32, 4)
            x2_ring = Ring(k, es, "x2f", [128, D], F32, 4)
            x2b_ring = Ring(k, es, "x2b", [128, D], BF16, 2)
            x2T_ring = Ring(k, es, "x2T", [128, 8, 128], BF16, 2)
            pt_ring = Ring(k, es, "ptf", [128, PLE], F32, 4)
            ptb_ring = Ring(k, es, "ptb", [128, PLE], BF16, 2)
            pT_ring = Ring(k, es, "pTf", [128, 2, 128], BF16, 2)
            tg_ring = Ring(k, es, "tgf", [128, D], F32, 2)
            ss_ring = Ring(k, es, "ssf", [128, 16], F32, 4)
            xo_ring = Ring(k, es, "xo", [128, D], F32, 1)

            def conv3(bank, bank_b, c, dst, dst_b):
                k.op("act", lambda e: e.activation(out=dst[:], in_=bank[:, 1:257], func=AF.Identity, scale=fcw[:, c, 1:2],
                                                   bias=fcb[:, c:c + 1]),
                     reads=[bank_b, p3_b], writes=[dst_b])
                k.op("dve", lambda e: e.scalar_tensor_tensor(out=dst[:], in0=bank[:, 0:256], scalar=fcw[:, c, 0:1], in1=dst[:],
                                                             op0=ALU.mult, op1=ALU.add), reads=[bank_b, p3_b, dst_b], writes=[dst_b])
                k.op("dve", lambda e: e.scalar_tensor_tensor(out=dst[:], in0=bank[:, 2:258], scalar=fcw[:, c, 2:3], in1=dst[:],
                                                             op0=ALU.mult, op1=ALU.add), reads=[bank_b, p3_b, dst_b], writes=[dst_b])

            print("phase3 sbuf remaining", nc.sbuf_bytes_remaining)
            raw_ring = Ring(k, es, "rawv", [128, 256], F32, 2)
            tp_ring = Ring(k, es, "tpv", [128, 256], F32, 2)

            def conv3p(bank, bank_b, c, dst, dst_b):
                k.op("act", lambda e: e.activation(out=dst[:], in_=bank[:, 1:257], func=AF.Identity, scale=fcw[:, c, 1:2],
                                                   bias=fcb[:, c:c + 1]),
                     reads=[bank_b, p3_b], writes=[dst_b])
                raw, raw_b = raw_ring.next()
                k.op("act", lambda e: e.activation(out=raw[:], in_=bank[:, 0:256], func=AF.Copy),
                     reads=[bank_b], writes=[raw_b])
                k.op("dve", lambda e: e.scalar_tensor_tensor(out=dst[:], in0=bank[:, 2:258], scalar=fcw[:, c, 2:3], in1=dst[:],
                                                             op0=ALU.mult, op1=ALU.add), reads=[bank_b, p3_b, dst_b], writes=[dst_b])
                tp, tp_b = tp_ring.next()
                k.op("pool", lambda e: e.tensor_scalar(out=tp[:], in0=raw[:], scalar1=fcw[:, c, 0:1], scalar2=0.0,
                                                       op0=ALU.mult, op1=ALU.add), reads=[raw_b, p3_b], writes=[tp_b])
                k.op("pool", lambda e: e.tensor_tensor(out=dst[:], in0=dst[:], in1=tp[:], op=ALU.add),
                     reads=[dst_b, tp_b], writes=[dst_b])

            eS_ring = Ring(k, es, "eSf", [128, D], F32, 2)
            from collections import deque
            wug_jobs = deque((bb, gg) for bb in range(NB) for gg in range(8))
            wug_pend = deque()

            def wug_prefetch():
                while len(wug_pend) < 2 and wug_jobs:
                    bb, gg = wug_jobs.popleft()
                    wug, wug_b = wug_ring.next()
                    for gv in range(2):
                        c0 = gv * DFF + gg * 512
                        k.dma("sp", wug[:, gv, :, :], s_wup[:, c0:c0 + 512].rearrange("(kk p) c -> p kk c", p=128),
                              writes=[wug_b], owner=wug_b)
                    wug_pend.append((wug, wug_b))

            wd_jobs = deque((bb, pp) for bb in range(NB) for pp in range(8))
            wd_pend = deque()

            def wd_prefetch():
                while len(wd_pend) < 2 and wd_jobs:
                    bb, piece = wd_jobs.popleft()
                    wd, wd_b = wd_ring.next()
                    k.dma("sp", wd[:], s_wdn[piece * 512:(piece + 1) * 512, :].rearrange("(i p) c -> p i c", p=128),
                          writes=[wd_b], owner=wd_b)
                    wd_pend.append((wd, wd_b))

            def load_h2(bb):
                h2, h2_b = h2_ring.next()
                k.dma("sp", h2[:], s_h2T[:, :, bb * 512:bb * 512 + 514], writes=[h2_b], owner=h2_b)
                return h2, h2_b

            deferred = deque()

            def emit_deferred():
                if deferred:
                    for fn in deferred.popleft():
                        fn()

            def make_tail(b, x2s, pts):
                tctx = [dict() for _ in range(4)]

                def A1(s):
                    c = tctx[s]
                    x2, x2_b, ss, ss_b = x2s[s]
                    pt, pt_b = pts[s]
                    x2b, x2b_b = x2b_ring.next()
                    k.op("act", lambda e: e.activation(out=x2b[:], in_=x2[:], func=AF.Copy), reads=[x2_b], writes=[x2b_b])
                    ptb, ptb_b = ptb_ring.next()
                    k.op("pool", lambda e: e.tensor_copy(out=ptb[:], in_=pt[:]), reads=[pt_b], writes=[ptb_b])
                    c["x2b"], c["x2b_b"], c["ptb"], c["ptb_b"] = x2b, x2b_b, ptb, ptb_b

                def A2(s):
                    c = tctx[s]
                    x2b, x2b_b, ptb, ptb_b = c["x2b"], c["x2b_b"], c["ptb"], c["ptb_b"]
                    bank, bank_b = k.bank()
                    bankb = bank.bitcast(BF16)
                    k.pe_begin(reads=[x2b_b, cst_b], writes=[bank_b])
                    ins = None
                    for kk in range(8):
                        ins = nc.tensor.transpose(out=bankb[:, kk * 128:(kk + 1) * 128], in_=x2b[:, kk * 128:(kk + 1) * 128],
                                                  identity=identb[:])
                    k.pe_end(ins, reads=[x2b_b, cst_b], writes=[bank_b])
                    c["bk1"] = (bankb, bank_b)
                    bank, bank_b = k.bank()
                    bankb = bank.bitcast(BF16)
                    k.pe_begin(reads=[ptb_b, cst_b], writes=[bank_b])
                    for kk in range(2):
                        ins = nc.tensor.transpose(out=bankb[:, kk * 128:(kk + 1) * 128], in_=ptb[:, kk * 128:(kk + 1) * 128],
                                                  identity=identb[:])
                    k.pe_end(ins, reads=[ptb_b, cst_b], writes=[bank_b])
                    c["bk2"] = (bankb, bank_b)

                def A3(s):
                    c = tctx[s]
                    x2T, x2T_b = x2T_ring.next()
                    bankb, bank_b = c["bk1"]
                    k.op("act", lambda e: e.activation(out=x2T[:], in_=bankb.rearrange("p (k t) -> p k t", k=8), func=AF.Copy),
                         reads=[bank_b], writes=[x2T_b])
                    pT, pT_b = pT_ring.next()
                    bankb, bank_b = c["bk2"]
                    k.op("act", lambda e: e.activation(out=pT[:], in_=bankb[:, 0:256].rearrange("p (k t) -> p k t", k=2),
                                                       func=AF.Copy), reads=[bank_b], writes=[pT_b])
                    c["x2T"], c["x2T_b"], c["pT"], c["pT_b"] = x2T, x2T_b, pT, pT_b

                def A4(s):
                    c = tctx[s]
                    x2T, x2T_b, pT, pT_b = c["x2T"], c["x2T_b"], c["pT"], c["pT_b"]
                    gbanks = []
                    ebanks = []
                    for half in range(2):
                        cs = slice(half * 512, (half + 1) * 512)
                        bank, bank_b = k.bank()
                        k.pe_begin(reads=[x2T_b, pw_b], writes=[bank_b])
                        ins = None
                        for kk in range(8):
                            ins = nc.tensor.matmul(bank, lhsT=x2T[:, kk, :], rhs=wg[:, kk, cs], start=(kk == 0), stop=(kk == 7))
                        k.pe_end(ins, reads=[x2T_b, pw_b], writes=[bank_b])
                        gbanks.append((bank, bank_b))
                        bank, bank_b = k.bank()
                        k.pe_begin(reads=[pT_b, pw_b], writes=[bank_b])
                        for kk in range(2):
                            ins = nc.tensor.matmul(bank, lhsT=pT[:, kk, :], rhs=wp[:, kk, cs], start=(kk == 0), stop=(kk == 1))
                        k.pe_end(ins, reads=[pT_b, pw_b], writes=[bank_b])
                        ebanks.append((bank, bank_b))
                    c["gbanks"], c["ebanks"] = gbanks, ebanks

                def A5(s):
                    c = tctx[s]
                    tg, tg_b = tg_ring.next()
                    eS, eS_b = eS_ring.next()
                    for half in range(2):
                        cs = slice(half * 512, (half + 1) * 512)
                        bank, bank_b = c["gbanks"][half]
                        k.op("act", lambda e: e.activation(out=tg[:, cs], in_=bank, func=AF.Tanh, scale=0.5),
                             reads=[bank_b], writes=[tg_b])
                        bank, bank_b = c["ebanks"][half]
                        k.op("act", lambda e: e.activation(out=eS[:, cs], in_=bank, func=AF.Copy),
                             reads=[bank_b], writes=[eS_b])
                    c["tg"], c["tg_b"], c["eS"], c["eS_b"] = tg, tg_b, eS, eS_b

                def A6(s):
                    c = tctx[s]
                    tg, tg_b, eS, eS_b = c["tg"], c["tg_b"], c["eS"], c["eS_b"]
                    k.op("dve", lambda e: e.scalar_tensor_tensor(out=tg[:], in0=tg[:], scalar=1.0, in1=eS[:],
                                                                 op0=ALU.add, op1=ALU.mult),
                         reads=[eS_b, tg_b], writes=[tg_b])

                def A7(s):
                    c = tctx[s]
                    x2, x2_b, ss, ss_b = x2s[s]
                    tg, tg_b = c["tg"], c["tg_b"]
                    k.op("act", lambda e: e.activation(out=junk[:], in_=tg[:], func=AF.Square, accum_out=ss[:, 5:6]),
                         reads=[tg_b], writes=[ss_b])

                def A8(s):
                    x2, x2_b, ss, ss_b = x2s[s]
                    rstd_ops(ss[:, 5:6], ss[:, 6:7], ss[:, 7:8], ss_b, 0.25 / D)

                def A9(s):
                    n = 4 * b + s
                    c = tctx[s]
                    x2, x2_b, ss, ss_b = x2s[s]
                    tg, tg_b = c["tg"], c["tg_b"]
                    k.op("dve", lambda e: e.scalar_tensor_tensor(out=tg[:], in0=tg[:], scalar=ss[:, 7:8], in1=wple[:],
                                                                 op0=ALU.mult, op1=ALU.mult),
                         reads=[tg_b, ss_b, p3_b], writes=[tg_b])
                    xo, xo_b = xo_ring.next()
                    k.op("dve", lambda e: e.scalar_tensor_tensor(out=xo[:], in0=tg[:], scalar=0.5, in1=x2[:],
                                                                 op0=ALU.mult, op1=ALU.add),
                         reads=[tg_b, x2_b], writes=[xo_b])
                    k.dma("pool", out_d[n * 128:(n + 1) * 128, :], xo[:], reads=[xo_b], owner=xo_b)

                def U(fn, s):
                    return lambda: fn(s)

                A1(0)
                if DEFER_MODE == 3:
                    units = []
                    for j in range(8):
                        u = []
                        s = j - 3
                        if 0 <= s < 4:
                            u += [U(A8, s), U(A9, s)]
                        s = j - 2
                        if 0 <= s < 4:
                            u += [U(A6, s), U(A7, s)]
                        s = j - 1
                        if 0 <= s < 4:
                            u += [U(A4, s), U(A5, s)]
                        s = j
                        if 0 <= s < 4:
                            u += [U(A2, s), U(A3, s)]
                        if 0 <= j + 1 < 4:
                            u += [U(A1, j + 1)]
                        units.append(u)
                    return units
                units = []
                for s in range(4):
                    if DEFER_MODE == 2:
                        units.append([U(A2, s), U(A3, s)])
                        units.append([U(A4, s), U(A5, s)])
                        units.append([U(A6, s)])
                        units.append([U(A7, s)])
                        units.append([U(A8, s)])
                        units.append([])
                        units.append([])
                    else:
                        for fn in (A2, A3, A4, A5, A6, A7, A8):
                            units.append([U(fn, s)])
                    if s < 3:
                        units.append([U(A9, s), U(A1, s + 1)])
                    else:
                        units.append([U(A9, s)])
                return units

            wug_prefetch()
            h2_next = load_h2(0)
            for b in range(NB):
                h2, h2_b = h2_next
                for g in range(8):
                    wug, wug_b = wug_pend.popleft()
                    for j in range(4):
                        c = 4 * g + j
                        for seg in range(2):
                            res = []
                            for gv in range(2):
                                bank, bank_b = k.bank()
                                k.pe_begin(reads=[wug_b, h2_b], writes=[bank_b])
                                ins = None
                                for kk in range(8):
                                    ins = nc.tensor.matmul(bank[:, 0:258], lhsT=wug[:, gv, kk, j * 128:(j + 1) * 128],
                                                           rhs=h2[:, kk, seg * 256:seg * 256 + 258], start=(kk == 0), stop=(kk == 7))
                                k.pe_end(ins, reads=[wug_b, h2_b], writes=[bank_b])
                                res.append((bank, bank_b))
                            yg, yg_b = yg_ring.next()
                            yv, yv_b = yv_ring.next()
                            conv3(res[0][0], res[0][1], c, yg, yg_b)
                            conv3p(res[1][0], res[1][1], 32 + c, yv, yv_b)
                            gl, gl_b = gl_ring.next()
                            k.op("act", lambda e: e.activation(out=gl[:], in_=yg[:], func=AF.Gelu_apprx_tanh),
                                 reads=[yg_b], writes=[gl_b])
                            k.op("pool", lambda e: e.tensor_tensor(out=gT[:, c, seg * 256:(seg + 1) * 256], in0=gl[:], in1=yv[:],
                                                                   op=ALU.mult), reads=[gl_b, yv_b], writes=[gT_bs[c // 4]])
                        if DEFER_MODE in (0, 2):
                            emit_deferred()
                    if DEFER_MODE == 1:
                        for _ in range(4):
                            emit_deferred()
                    if DEFER_MODE == 3:
                        emit_deferred()
                    wug_prefetch()
                    if g == 5:
                        wd_prefetch()
                    if b == 0 and g == 1:
                        for kk in range(8):
                            k.dma("pool", wg[:, kk, :], w_pg_d[kk * 128:(kk + 1) * 128, :], writes=[pw_b], owner=pw_b)
                        for kk in range(2):
                            k.dma("pool", wp[:, kk, :], w_pp_d[kk * 128:(kk + 1) * 128, :], writes=[pw_b], owner=pw_b)
                while deferred:
                    emit_deferred()
                if b + 1 < NB:
                    h2_next = load_h2(b + 1)
                dbanks = [[k.bank() for half in range(2)] for s in range(4)]
                allb = [dbanks[s][half][1] for s in range(4) for half in range(2)]
                for piece in range(8):
                    wd_prefetch()
                    wd, wd_b = wd_pend.popleft()
                    k.pe_begin(reads=[wd_b, gT_bs[piece]], writes=allb)
                    ins = None
                    for s in range(4):
                        for half in range(2):
                            for i in range(4):
                                ins = nc.tensor.matmul(dbanks[s][half][0], lhsT=gT[:, piece * 4 + i, s * 128:(s + 1) * 128],
                                                       rhs=wd[:, i, half * 512:(half + 1) * 512],
                                                       start=(piece == 0 and i == 0), stop=(piece == 7 and i == 3),
                                                       skip_group_check=True)
                    k.pe_end(ins, reads=[wd_b, gT_bs[piece]], writes=allb)
                    if piece < 6:
                        wd_prefetch()
                x2s = []
                pts = []
                x1s = []
                for s in range(4):
                    n = 4 * b + s
                    x1, x1_b = x1_ring.next()
                    k.dma("sp", x1[:], s_x1[n], writes=[x1_b], owner=x1_b)
                    pt, pt_b = pt_ring.next()
                    k.dma("sp", pt[:], p_d[n * 128:(n + 1) * 128, :], writes=[pt_b], owner=pt_b)
                    x2, x2_b = x2_ring.next()
                    ss, ss_b = ss_ring.next()
                    x1s.append((x1, x1_b))
                    pts.append((pt, pt_b))
                    x2s.append((x2, x2_b, ss, ss_b))
                for s in range(4):
                    x2, x2_b, ss, ss_b = x2s[s]
                    for half in range(2):
                        bank, bank_b = dbanks[s][half]
                        k.op("act", lambda e: e.activation(out=junk[:, 0:512], in_=bank, func=AF.Square,
                                                           accum_out=ss[:, half:half + 1]), reads=[bank_b], writes=[ss_b])
                for s in range(4):
                    x2, x2_b, ss, ss_b = x2s[s]
                    k.op("pool", lambda e: e.tensor_tensor(out=ss[:, 2:3], in0=ss[:, 0:1], in1=ss[:, 1:2], op=ALU.add),
                         reads=[ss_b], writes=[ss_b])
                    rstd_ops(ss[:, 2:3], ss[:, 3:4], ss[:, 4:5], ss_b, 1.0 / D)
                for s in range(4):
                    x2, x2_b, ss, ss_b = x2s[s]
                    for half in range(2):
                        bank, bank_b = dbanks[s][half]
                        cs = slice(half * 512, (half + 1) * 512)
                        k.op("dve", lambda e: e.scalar_tensor_tensor(out=x2[:, cs], in0=bank, scalar=ss[:, 4:5], in1=wfpost[:, cs],
                                                                     op0=ALU.mult, op1=ALU.mult),
                             reads=[bank_b, ss_b, p3_b], writes=[x2_b])
                for s in range(4):
                    x2, x2_b, ss, ss_b = x2s[s]
                    x1, x1_b = x1s[s]
                    k.op("pool", lambda e: e.tensor_tensor(out=x2[:], in0=x2[:], in1=x1[:], op=ALU.add),
                         reads=[x2_b, x1_b], writes=[x2_b])
                for u in make_tail(b, x2s, pts):
                    if DEFER:
                        deferred.append(u)
                    else:
                        for fn in u:
                            fn()
            while deferred:
                emit_deferred()
            k.end_phase()
    return nc


def make_consts():
    j = np.arange(128)[:, None]
    i = np.arange(128)[None, :]
    tri = np.zeros((128, 4, 128), np.float32)
    tri[:, 0, :] = (j <= i)
    tri[:, 1, :] = (j >= i)
    tri[:, 2, :] = (j > i)
    tri[:, 3, :] = (j < i)
    sel = np.zeros((4, 4, 128), np.float32)
    for h in range(4):
        sel[h, h, :] = 1.0
    return {
        "tri": tri,
        "identb": np.eye(128, dtype=np.float32).astype(ml_dtypes.bfloat16),
        "identf": np.eye(128, dtype=np.float32),
        "sel": sel,
    }


def make_shared(inp):
    f = lambda a: np.ascontiguousarray(np.asarray(a, dtype=np.float32))
    sh = dict(make_consts())
    sh["w_in"] = f(inp["w_in"][0])
    sh["wd_aug"] = f(np.stack([
        np.concatenate([inp["w_gla_decay_f"][0], inp["b_gla_decay_f"][0][None, :]], axis=0),
        np.concatenate([inp["w_gla_decay_b"][0], inp["b_gla_decay_b"][0][None, :]], axis=0)], axis=0))
    sh["mlcw"] = f(np.asarray(inp["ml_conv_w"][0]).T.reshape(8, 128, 3).transpose(1, 0, 2))
    sh["mlcb"] = f(np.asarray(inp["ml_conv_b"][0]).reshape(8, 128).T)
    ib = np.asarray(inp["ml_igate_b"][0])
    fb = np.asarray(inp["ml_fgate_b"][0])
    sh["gbias"] = f(np.stack([ib[0:4], ib[4:8], fb[0:4], fb[4:8]], axis=1))
    sh["ffcw"] = f(np.asarray(inp["ffn_conv_w"][0]).T.reshape(64, 128, 3).transpose(1, 0, 2))
    sh["ffcb"] = f(np.asarray(inp["ffn_conv_b"][0]).reshape(64, 128).T)
    sh["norms"] = f(np.stack([inp["norm_mix_pre"][0], inp["norm_mix_post"][0], inp["gla_norm"][0], inp["ml_norm"][0],
                              inp["norm_ffn_pre"][0], inp["norm_ffn_post"][0], inp["norm_ple_post"][0]], axis=0))
    sh["w_out"] = f(inp["w_out"][0])
    sh["w_up"] = f(inp["w_up"][0])
    sh["w_down"] = f(inp["w_down"][0])
    sh["w_pg"] = f(inp["w_ple_gate"][0])
    sh["w_pp"] = f(inp["w_ple_proj"][0])
    return sh


def kernel(**inputs):
    x = np.asarray(inputs["x"], dtype=np.float32)
    p = np.asarray(inputs["p"], dtype=np.float32)
    B, S, _ = x.shape
    sh = make_shared(inputs)
    nc = build(S)
    in_maps = []
    for b in range(B):
        m = dict(sh)
        m["x"] = np.ascontiguousarray(x[b])
        m["p"] = np.ascontiguousarray(p[0, b])
        in_maps.append(m)
    res = run_bass_kernel_spmd(nc, in_maps, core_ids=list(range(B)))
    return np.stack([np.asarray(r["out"], dtype=np.float32) for r in res.results], axis=0)
```

```python
import math
import numpy as np
import ml_dtypes
from contextlib import ExitStack
import concourse.bass as bass
import concourse.mybir as mybir
from concourse.bass_utils import run_bass_kernel_spmd

F32 = mybir.dt.float32
BF16 = mybir.dt.bfloat16
AF = mybir.ActivationFunctionType
ALU = mybir.AluOpType

D = 1024
DIN = 8240
H = 4
DK = 128
DV = 256
DFF = 4096
PLE = 256
EPS = 1e-6
LN_QS = math.log(DK ** -0.5)
CONV_CH = 1024
DEFER = True
DEFER_MODE = 3


class Buf:
    __slots__ = ("name", "w", "r", "sem", "cnt")

    def __init__(self, name):
        self.name = name
        self.w = None
        self.r = {}
        self.sem = None
        self.cnt = 0


class KB:
    def __init__(self, nc, es):
        self.nc = nc
        self.es = es
        self.eng = {"pe": nc.tensor, "act": nc.scalar, "dve": nc.vector, "pool": nc.gpsimd, "sp": nc.sync}
        self.esem = {}
        for e in ("pe", "act", "dve", "pool"):
            self.esem[e] = es.enter_context(nc.semaphore("es_" + e))
        self.ecnt = {e: 0 for e in self.esem}
        self.waited = {e: {} for e in self.eng}
        self.free_sems = []
        self.all_dma = {}
        self.phase_bufs = []
        self.nsem = 0
        self.ps = es.enter_context(nc.psum_tensor("ps", [128, 8, 512], F32))
        self.pb = [Buf("bank%d" % i) for i in range(8)]
        self.bank_i = 0

    def buf(self, name):
        b = Buf(name)
        self.phase_bufs.append(b)
        return b

    def bank(self):
        i = self.bank_i
        self.bank_i = (i + 1) % 8
        return self.ps[:, i, :], self.pb[i]

    def _wait(self, e, tok):
        key, sem, val = tok
        if e == "pe" and key == "pe":
            return
        w = self.waited[e]
        if w.get(id(sem), 0) >= val:
            return
        self.eng[e].wait_ge(sem, val)
        w[id(sem)] = val

    def _deps(self, e, reads, writes):
        for b in reads:
            if b.w is not None:
                self._wait(e, b.w)
        for b in writes:
            if b.w is not None:
                self._wait(e, b.w)
            for t in b.r.values():
                self._wait(e, t)

    def _mark(self, tok, reads, writes):
        for b in reads:
            b.r[tok[0]] = tok
        for b in writes:
            b.w = tok
            b.r = {}

    def op(self, e, fn, reads=(), writes=()):
        self._deps(e, reads, writes)
        ins = fn(self.eng[e])
        self.ecnt[e] += 1
        ins.then_inc(self.esem[e], 1)
        tok = (e, self.esem[e], self.ecnt[e])
        self._mark(tok, reads, writes)
        return tok

    def pe_begin(self, reads=(), writes=()):
        self._deps("pe", reads, writes)

    def pe_end(self, ins, reads=(), writes=()):
        self.ecnt["pe"] += 1
        ins.then_inc(self.esem["pe"], 1)
        tok = ("pe", self.esem["pe"], self.ecnt["pe"])
        self._mark(tok, reads, writes)

    def dma(self, q, out, in_, reads=(), writes=(), owner=None, **kw):
        skip = ("d", id(owner.sem)) if owner.sem is not None else None
        for b in reads:
            if b.w is not None:
                self._wait(q, b.w)
        for b in writes:
            if b.w is not None and b.w[0] != skip:
                self._wait(q, b.w)
            for t in b.r.values():
                self._wait(q, t)
        if owner.sem is None:
            if self.free_sems:
                owner.sem, owner.cnt = self.free_sems.pop()
            else:
                self.nsem += 1
                owner.sem = self.es.enter_context(self.nc.semaphore("ds%d" % self.nsem))
                owner.cnt = 0
        ins = self.eng[q].dma_start(out=out, in_=in_, **kw)
        owner.cnt += 16
        ins.then_inc(owner.sem, 16)
        self.all_dma[id(owner.sem)] = (owner.sem, owner.cnt)
        tok = (("d", id(owner.sem)), owner.sem, owner.cnt)
        self._mark(tok, reads, writes)

    def barrier(self):
        for e in self.eng:
            for e2 in self.esem:
                if e2 != e and self.ecnt[e2] > 0:
                    self._wait(e, (e2 + "_b", self.esem[e2], self.ecnt[e2]))
            for sem, cnt in self.all_dma.values():
                self._wait(e, ("db", sem, cnt))

    def end_phase(self):
        self.barrier()
        for b in self.phase_bufs:
            if b.sem is not None:
                self.free_sems.append((b.sem, b.cnt))
                b.sem = None
        self.phase_bufs = []
        for b in self.pb:
            b.w = None
            b.r = {}


class Ring:
    def __init__(self, k, es, name, shape, dtype, n):
        self.t = [es.enter_context(k.nc.sbuf_tensor("sr_%s%d_%d" % (name, i, id(es)), shape, dtype)) for i in range(n)]
        self.b = [k.buf("%s%d" % (name, i)) for i in range(n)]
        self.i = 0
        self.n = n

    def next(self):
        i = self.i
        self.i = (i + 1) % self.n
        return self.t[i], self.b[i]


def build(S):
    NT = S // 128
    NB = S // 512
    nc = bass.Bass("TRN2", target_bir_lowering=False)

    def din(name, shape, dt=F32):
        return nc.dram_tensor(name, list(shape), dt, kind="ExternalInput").ap()

    def dscr(name, shape, dt):
        return nc.dram_tensor(name, list(shape), dt, kind="Internal").ap()

    x_d = din("x", [S, D])
    p_d = din("p", [S, PLE])
    w_in = din("w_in", [D, DIN])
    wd_aug = din("wd_aug", [2, 17, 512])
    mlcw_d = din("mlcw", [128, 8, 3])
    mlcb_d = din("mlcb", [128, 8])
    gbias_d = din("gbias", [4, 4])
    ffcw_d = din("ffcw", [128, 64, 3])
    ffcb_d = din("ffcb", [128, 64])
    norms_d = din("norms", [7, D])
    w_out_d = din("w_out", [D, D])
    w_up_d = din("w_up", [D, 2 * DFF])
    w_down_d = din("w_down", [DFF, D])
    w_pg_d = din("w_pg", [D, D])
    w_pp_d = din("w_pp", [PLE, D])
    tri_d = din("tri", [128, 4, 128])
    identb_d = din("identb", [128, 128], BF16)
    identf_d = din("identf", [128, 128])
    sel_d = din("sel", [4, 4, 128])
    out_d = nc.dram_tensor("out", [S, D], F32, kind="ExternalOutput").ap()

    s_hT = dscr("s_hT", [NB, 128, 8, 512], BF16)
    s_mqk = dscr("s_mqk", [1024, S], F32)
    s_gate = dscr("s_gate", [16, S], F32)
    s_qe = [dscr("s_qe%d" % d, [NT, 128, 512], BF16) for d in range(2)]
    s_ke = [dscr("s_ke%d" % d, [NT, 128, 512], BF16) for d in range(2)]
    s_kd = [dscr("s_kd%d" % d, [NT, 128, 512], BF16) for d in range(2)]
    s_v = dscr("s_v", [NT, 128, 1024], BF16)
    s_mv = dscr("s_mv", [NT, 128, 1024], BF16)
    s_GA = dscr("s_GA", [NT, 128, 1024], BF16)
    s_GB = dscr("s_GB", [NT, 128, 1024], BF16)
    s_mq = dscr("s_mq", [NT, 128, 4, 128], BF16)
    s_mk = dscr("s_mk", [NT, 128, 4, 128], BF16)
    s_mktm = dscr("s_mktm", [NT, 128, 512], BF16)
    s_obA = dscr("s_obA", [NT, 128, 1024], F32)
    s_obB = dscr("s_obB", [NT, 128, 1024], F32)
    s_x1 = dscr("s_x1", [NT, 128, 1024], F32)
    s_h2T = dscr("s_h2T", [128, 8, S + 2], BF16)
    s_wup = dscr("s_wup", [D, 2 * DFF], BF16)
    s_wdn = dscr("s_wdn", [DFF, D], BF16)

    with ExitStack() as ges:
        k = KB(nc, ges)

        def GT(name, shape, dt):
            return ges.enter_context(nc.sbuf_tensor("sb_" + name, shape, dt))

        tri = GT("tri", [128, 4, 128], F32)
        identb = GT("identb", [128, 128], BF16)
        identf = GT("identf", [128, 128], F32)
        sel = GT("sel", [4, 4, 128], F32)
        neghalf = GT("neghalf", [128, 8], F32)
        EB = GT("EB", [128, 2, NT, 4], F32)
        w_tm = GT("w_tm", [128, 2, NT, 4], F32)
        thr_tm = GT("thr_tm", [128, 2, NT, 4], F32)
        dec_bc = GT("dec_bc", [128, 2, 4, NT], F32)
        junk = GT("junk", [128, 1024], BF16)
        cst_b = Buf("cst")
        nh_b = Buf("neghalf")
        EB_b = Buf("EB")
        wtm_b = Buf("wtm")
        thr_b = Buf("thr")
        dec_b = Buf("dec")
        k.dma("sp", tri[:], tri_d[:, :, :], writes=[cst_b], owner=cst_b)
        k.dma("sp", identb[:], identb_d[:, :], writes=[cst_b], owner=cst_b)
        k.dma("sp", identf[:], identf_d[:, :], writes=[cst_b], owner=cst_b)
        k.dma("sp", sel[:], sel_d[:, :, :], writes=[cst_b], owner=cst_b)
        k.op("pool", lambda e: e.memset(neghalf[:], -0.5), writes=[nh_b])

        def rstd_ops(ss_ap, ms_ap, rs_ap, stb, scale, n=1):
            k.op("pool", lambda e: e.tensor_scalar(out=ms_ap, in0=ss_ap, scalar1=scale, scalar2=EPS,
                                                   op0=ALU.mult, op1=ALU.add), reads=[stb], writes=[stb])
            k.op("pool", lambda e: e.tensor_tensor(out=rs_ap, in0=ms_ap, in1=neghalf[:, 0:n], op=ALU.pow),
                 reads=[stb, nh_b], writes=[stb])

        def rstd_act(ss_ap, ms_ap, rs_ap, stb, scale, n=1):
            k.op("act", lambda e: e.activation(out=ms_ap, in_=ss_ap, func=AF.Ln, scale=scale, bias=EPS),
                 reads=[stb], writes=[stb])
            k.op("act", lambda e: e.activation(out=rs_ap, in_=ms_ap, func=AF.Exp, scale=-0.5),
                 reads=[stb], writes=[stb])

        def transpose8(src, src_b, dst_view, dst_b, evac_eng):
            bank, bank_b = k.bank()
            bankb = bank.bitcast(BF16)
            k.pe_begin(reads=[src_b, cst_b], writes=[bank_b])
            ins = None
            for kk in range(8):
                ins = nc.tensor.transpose(out=bankb[:, kk * 128:(kk + 1) * 128], in_=src[:, kk * 128:(kk + 1) * 128],
                                          identity=identb[:])
            k.pe_end(ins, reads=[src_b, cst_b], writes=[bank_b])
            srcv = bankb.rearrange("p (k t) -> p k t", k=8)
            if evac_eng == "act":
                k.op("act", lambda e: e.activation(out=dst_view, in_=srcv, func=AF.Copy), reads=[bank_b], writes=[dst_b])
            else:
                k.op("dve", lambda e: e.tensor_copy(out=dst_view, in_=srcv), reads=[bank_b], writes=[dst_b])

        with ExitStack() as es:
            def T(name, shape, dt):
                return es.enter_context(nc.sbuf_tensor("sb_" + name + "_%d" % id(es), shape, dt))

            wA = T("wA", [128, 8, 2096], BF16)
            wA_b = k.buf("wA")
            wpre = T("wpre", [128, D], F32)
            wdt = T("wdt", [17, 2, 512], F32)
            pa_b = k.buf("pa_c")
            k.dma("sp", wpre[:], norms_d[0:1, :].partition_broadcast(128), writes=[pa_b], owner=pa_b)
            for d in range(2):
                k.dma("sp", wdt[:, d, :], wd_aug[d, :, :], writes=[pa_b], owner=pa_b)
            wA_r = [k.buf("wA_r%d" % i) for i in range(3)]
            for ri, (d0, d1, s0) in ((1, (1024, 2080, 3072)), (0, (0, 1024, 0)), (2, (2080, 2096, 6176))):
                for kk in range(8):
                    rows = slice(kk * 128, (kk + 1) * 128)
                    k.dma("pool", wA[:, kk, d0:d1], w_in[rows, s0:s0 + (d1 - d0)], writes=[wA_r[ri]], owner=wA_r[ri])

            def wA_buf(c0):
                return wA_r[0] if c0 < 1024 else (wA_r[1] if c0 < 2080 else wA_r[2])
            x_ring = Ring(k, es, "xa", [128, D], F32, 4)
            st_ring = Ring(k, es, "sta", [128, 4], F32, 4)
            hb_ring = Ring(k, es, "hba", [128, D], BF16, 2)
            hT_ring = Ring(k, es, "hTa", [128, 8, 512], BF16, 2)
            qraw_ring = Ring(k, es, "qraw", [128, 4, 512], F32, 2)
            kraw_ring = Ring(k, es, "kraw", [128, 4, 512], F32, 2)
            mstg_ring = Ring(k, es, "mstg", [128, 512], F32, 3)
            gstg_ring = Ring(k, es, "gstg", [16, 512], F32, 2)
            lrT_ring = [Ring(k, es, "lrT%d" % d, [17, 512], F32, 2) for d in range(2)]
            for d in range(2):
                for i in range(2):
                    t_, b_ = lrT_ring[d].t[i], lrT_ring[d].b[i]
                    k.op("pool", lambda e: e.memset(t_[:], 1.0), writes=[b_])
            sp_ring = [Ring(k, es, "sp%d" % d, [128, 512], F32, 5) for d in range(2)]
            eb_ring = [Ring(k, es, "eb%d" % d, [128, 512], F32, 2) for d in range(2)]
            enb_ring = [Ring(k, es, "enb%d" % d, [128, 512], F32, 2) for d in range(2)]
            ekd_ring = [Ring(k, es, "ekd%d" % d, [128, 512], F32, 2) for d in range(2)]
            qst_ring = Ring(k, es, "qst", [128, 512], BF16, 4)
            kst_ring = Ring(k, es, "kst", [128, 512], BF16, 4)
            kdst_ring = Ring(k, es, "kdst", [128, 512], BF16, 4)

            def make_hT(b):
                hT, hT_b = hT_ring.next()
                for s in range(4):
                    t = 4 * b + s
                    xs, xs_b = x_ring.next()
                    k.dma("sp", xs[:], x_d[t * 128:(t + 1) * 128, :], writes=[xs_b], owner=xs_b)
                    st, st_b = st_ring.next()
                    k.op("act", lambda e: e.activation(out=junk[:], in_=xs[:], func=AF.Square, accum_out=st[:, 0:1]),
                         reads=[xs_b], writes=[st_b])
                    rstd_ops(st[:, 0:1], st[:, 1:2], st[:, 2:3], st_b, 1.0 / D)
                    hb, hb_b = hb_ring.next()
                    k.op("dve", lambda e: e.scalar_tensor_tensor(out=hb[:], in0=xs[:], scalar=st[:, 2:3], in1=wpre[:],
                                                                 op0=ALU.mult, op1=ALU.mult),
                         reads=[xs_b, st_b, pa_b], writes=[hb_b])
                    transpose8(hb, hb_b, hT[:, :, s * 128:(s + 1) * 128], hT_b, "dve")
                k.dma("pool", s_hT[b], hT[:], reads=[hT_b], owner=hT_b)
                return hT, hT_b

            hT_next = make_hT(0)
            for b in range(NB):
                hT, hT_b = hT_next

                def fm_group(c0, M):
                    bank, bank_b = k.bank()
                    k.pe_begin(reads=[wA_buf(c0), hT_b], writes=[bank_b])
                    ins = None
                    for kk in range(8):
                        ins = nc.tensor.matmul(bank[0:M, :], lhsT=wA[:, kk, c0:c0 + M], rhs=hT[:, kk, :],
                                               start=(kk == 0), stop=(kk == 7))
                    k.pe_end(ins, reads=[wA_buf(c0), hT_b], writes=[bank_b])
                    return bank, bank_b

                lrT = []
                for d in range(2):
                    bank, bank_b = fm_group(1024 + d * 16, 16)
                    lt, lt_b = lrT_ring[d].next()
                    k.op("dve", lambda e: e.tensor_copy(out=lt[0:16, :], in_=bank[0:16, :]), reads=[bank_b], writes=[lt_b])
                    lrT.append((lt, lt_b))
                spts = []
                for s in range(4):
                    ts = slice(s * 128, (s + 1) * 128)
                    row = []
                    for d in range(2):
                        lt, lt_b = lrT[d]
                        bank, bank_b = k.bank()
                        k.pe_begin(reads=[lt_b, pa_b], writes=[bank_b])
                        ins = nc.tensor.matmul(bank, lhsT=lt[0:17, ts], rhs=wdt[0:17, d, :], start=True, stop=True)
                        k.pe_end(ins, reads=[lt_b, pa_b], writes=[bank_b])
                        spt, spt_b = sp_ring[d].next()
                        k.op("act", lambda e: e.activation(out=spt[:], in_=bank, func=AF.Exp, scale=-1.0),
                             reads=[bank_b], writes=[spt_b])
                        k.op("act", lambda e: e.activation(out=spt[:], in_=spt[:], func=AF.Ln, bias=1.0),
                             reads=[spt_b], writes=[spt_b])
                        row.append((spt, spt_b))
                    spts.append(row)
                qraw, qraw_b = qraw_ring.next()
                kraw, kraw_b = kraw_ring.next()
                for h in range(4):
                    bank, bank_b = fm_group(h * 128, 128)
                    k.op("act", lambda e: e.activation(out=qraw[:, h, :], in_=bank, func=AF.Copy),
                         reads=[bank_b], writes=[qraw_b])
                for h in range(4):
                    bank, bank_b = fm_group(512 + h * 128, 128)
                    k.op("dve", lambda e: e.tensor_copy(out=kraw[:, h, :], in_=bank), reads=[bank_b], writes=[kraw_b])
                for c in range(8):
                    bank, bank_b = fm_group(1056 + c * 128, 128)
                    ms_, ms_b = mstg_ring.next()
                    if c % 2 == 0:
                        k.op("act", lambda e: e.activation(out=ms_[:], in_=bank, func=AF.Copy), reads=[bank_b], writes=[ms_b])
                    else:
                        k.op("dve", lambda e: e.tensor_copy(out=ms_[:], in_=bank), reads=[bank_b], writes=[ms_b])
                    k.dma("pool", s_mqk[c * 128:(c + 1) * 128, b * 512:(b + 1) * 512], ms_[:], reads=[ms_b], owner=ms_b)
                bank, bank_b = fm_group(2080, 16)
                gs_, gs_b = gstg_ring.next()
                k.op("dve", lambda e: e.tensor_copy(out=gs_[:], in_=bank[0:16, :]), reads=[bank_b], writes=[gs_b])
                k.dma("pool", s_gate[:, b * 512:(b + 1) * 512], gs_[:], reads=[gs_b], owner=gs_b)

                if b + 1 < NB:
                    hT_next = make_hT(b + 1)

                for s in range(4):
                    t = 4 * b + s
                    ts = slice(s * 128, (s + 1) * 128)
                    bk, bk_b = k.bank()
                    k.pe_begin(reads=[wA_r[0], hT_b], writes=[bk_b])
                    ins = None
                    for kk in range(8):
                        ins = nc.tensor.matmul(bk, lhsT=hT[:, kk, ts], rhs=wA[:, kk, 512:1024], start=(kk == 0), stop=(kk == 7))
                    k.pe_end(ins, reads=[wA_r[0], hT_b], writes=[bk_b])
                    pb2 = []
                    for d in range(2):
                        spt, spt_b = spts[s][d]
                        bank2, bank2_b = k.bank()
                        k.pe_begin(reads=[spt_b, cst_b], writes=[bank2_b])
                        for h in range(4):
                            ins = nc.tensor.matmul(bank2[:, h * 128:(h + 1) * 128], lhsT=spt[:, h * 128:(h + 1) * 128],
                                                   rhs=tri[:, d, :], start=True, stop=True)
                        k.pe_end(ins, reads=[spt_b, cst_b], writes=[bank2_b])
                        bank3, bank3_b = k.bank()
                        k.pe_begin(reads=[spt_b, cst_b], writes=[bank3_b])
                        ins = nc.tensor.matmul(bank3, lhsT=tri[:, 2 + d, :], rhs=spt[:], start=True, stop=True)
                        k.pe_end(ins, reads=[spt_b, cst_b], writes=[bank3_b])
                        pb2.append((bank2, bank2_b, bank3, bank3_b))
                    for d in range(2):
                        bank2, bank2_b, bank3, bank3_b = pb2[d]
                        eb, eb_b = eb_ring[d].next()
                        enb, enb_b = enb_ring[d].next()
                        k.op("act", lambda e: e.activation(out=eb[:], in_=bank2, func=AF.Exp, scale=-1.0 / 16.0),
                             reads=[bank2_b], writes=[eb_b])
                        k.op("act", lambda e: e.activation(out=enb[:], in_=bank2, func=AF.Exp, scale=1.0 / 16.0),
                             reads=[bank2_b], writes=[enb_b])
                        ekd, ekd_b = ekd_ring[d].next()
                        k.op("act", lambda e: e.activation(out=ekd[:], in_=bank3, func=AF.Exp, scale=-1.0 / 16.0),
                             reads=[bank3_b], writes=[ekd_b])
                        col = 127 if d == 0 else 0
                        ebv = eb[:].rearrange("p (h t) -> p h t", h=4)
                        enbv = enb[:].rearrange("p (h t) -> p h t", h=4)
                        k.op("pool", lambda e: e.tensor_copy(out=EB[:, d, t, :], in_=ebv[:, :, col]),
                             reads=[eb_b], writes=[EB_b])
                        qst, qst_b = qst_ring.next()
                        k.op("dve", lambda e: e.scalar_tensor_tensor(
                            out=qst[:].rearrange("p (h t) -> p h t", h=4), in0=qraw[:, :, ts], scalar=DK ** -0.5,
                            in1=ebv, op0=ALU.mult, op1=ALU.mult), reads=[qraw_b, eb_b], writes=[qst_b])
                        k.dma("pool", s_qe[d][t], qst[:], reads=[qst_b], owner=qst_b)
                        kst, kst_b = kst_ring.next()
                        k.op("pool", lambda e: e.tensor_tensor(
                            out=kst[:].rearrange("p (h t) -> p h t", h=4), in0=kraw[:, :, ts], in1=enbv, op=ALU.mult),
                            reads=[kraw_b, enb_b], writes=[kst_b])
                        k.dma("pool", s_ke[d][t], kst[:], reads=[kst_b], owner=kst_b)
                        kdst, kdst_b = kdst_ring.next()
                        k.op("dve", lambda e: e.tensor_tensor(out=kdst[:], in0=bk, in1=ekd[:], op=ALU.mult),
                             reads=[bk_b, ekd_b], writes=[kdst_b])
                        k.dma("pool", s_kd[d][t], kdst[:], reads=[kdst_b], owner=kdst_b)
            k.end_phase()

        with ExitStack() as es:
            def T(name, shape, dt):
                return es.enter_context(nc.sbuf_tensor("sb_" + name + "_%d" % id(es), shape, dt))

            wB = T("wB", [128, 8, 6144], BF16)
            wB_b = k.buf("wB")
            gnorm = T("gnorm", [128, D], F32)
            mnorm = T("mnorm", [128, D], F32)
            pb_b = k.buf("pb_c")
            k.dma("sp", gnorm[:], norms_d[2:3, :].partition_broadcast(128), writes=[pb_b], owner=pb_b)
            k.dma("sp", mnorm[:], norms_d[3:4, :].partition_broadcast(128), writes=[pb_b], owner=pb_b)
            wB_bs = [k.buf("wB%d" % i) for i in range(3)]
            for i, (c0, s0) in enumerate(((0, 1024), (2048, 4128), (4096, 6192))):
                for kk in range(8):
                    rows = slice(kk * 128, (kk + 1) * 128)
                    k.dma("pool", wB[:, kk, c0:c0 + 2048], w_in[rows, s0:s0 + 2048], writes=[wB_bs[i]], owner=wB_bs[i])
            hT_ring = Ring(k, es, "hTb", [128, 8, 512], BF16, 2)
            vst_ring = Ring(k, es, "vst", [128, D], BF16, 2)
            mvst_ring = Ring(k, es, "mvst", [128, D], BF16, 2)
            gst_ring = Ring(k, es, "gst", [128, D], BF16, 3)
            t1_ring = Ring(k, es, "t1", [128, D], F32, 2)
            t2_ring = Ring(k, es, "t2", [128, D], F32, 2)

            CH = min(CONV_CH, S)
            NCH = S // CH
            TPC = CH // 128
            cw = T("cw", [128, 8, 3], F32)
            cb = T("cb", [128, 8], F32)
            cv_b = k.buf("cv_c")
            k.dma("sp", cw[:], mlcw_d[:, :, :], writes=[cv_b], owner=cv_b)
            k.dma("sp", cb[:], mlcb_d[:, :], writes=[cv_b], owner=cv_b)
            pad_ring = Ring(k, es, "pad", [128, CH + 2], F32, 2)
            y_ring = Ring(k, es, "ycv", [128, CH], F32, 2)
            qo_ring = Ring(k, es, "qo", [128, CH], BF16, 2)
            mkc = T("mkc", [128, 4, CH], BF16)
            mkc_b = k.buf("mkc")
            mkst_ring = Ring(k, es, "mkst", [128, 2, 512], BF16, 2)

            def conv_unit(u):
                tc, c = u // 8, u % 8
                rows = slice(c * 128, (c + 1) * 128)
                pd, pd_b = pad_ring.next()
                k.dma("sp", pd[:, 1:CH + 1], s_mqk[rows, tc * CH:(tc + 1) * CH], writes=[pd_b], owner=pd_b)
                if tc == 0:
                    k.op("pool", lambda e: e.memset(pd[:, 0:1], 0.0), writes=[pd_b])
                else:
                    k.dma("sp", pd[:, 0:1], s_mqk[rows, tc * CH - 1:tc * CH], writes=[pd_b], owner=pd_b, allow_slow_non_contiguous=True)
                if tc == NCH - 1:
                    k.op("pool", lambda e: e.memset(pd[:, CH + 1:CH + 2], 0.0), writes=[pd_b])
                else:
                    k.dma("sp", pd[:, CH + 1:CH + 2], s_mqk[rows, (tc + 1) * CH:(tc + 1) * CH + 1], writes=[pd_b], owner=pd_b, allow_slow_non_contiguous=True)
                y, y_b = y_ring.next()
                k.op("dve", lambda e: e.tensor_scalar(out=y[:], in0=pd[:, 1:CH + 1], scalar1=cw[:, c, 1:2], scalar2=cb[:, c:c + 1],
                                                      op0=ALU.mult, op1=ALU.add), reads=[pd_b, cv_b], writes=[y_b])
                k.op("dve", lambda e: e.scalar_tensor_tensor(out=y[:], in0=pd[:, 0:CH], scalar=cw[:, c, 0:1], in1=y[:],
                                                             op0=ALU.mult, op1=ALU.add), reads=[pd_b, cv_b, y_b], writes=[y_b])
                k.op("dve", lambda e: e.scalar_tensor_tensor(out=y[:], in0=pd[:, 2:CH + 2], scalar=cw[:, c, 2:3], in1=y[:],
                                                             op0=ALU.mult, op1=ALU.add), reads=[pd_b, cv_b, y_b], writes=[y_b])
                n0 = tc * TPC
                if c < 4:
                    qo, qo_b = qo_ring.next()
                    k.op("act", lambda e: e.activation(out=qo[:], in_=y[:], func=AF.Silu), reads=[y_b], writes=[qo_b])
                    k.dma("pool", s_mq[n0:n0 + TPC, :, c, :].rearrange("n p t -> p n t"),
                          qo[:].rearrange("p (n t) -> p n t", t=128), reads=[qo_b], owner=qo_b)
                else:
                    h = c - 4
                    k.op("act", lambda e: e.activation(out=mkc[:, h, :], in_=y[:], func=AF.Silu), reads=[y_b], writes=[mkc_b])
                    k.dma("pool", s_mk[n0:n0 + TPC, :, h, :].rearrange("n p t -> p n t"),
                          mkc[:, h, :].rearrange("p (n t) -> p n t", t=128), reads=[mkc_b], owner=mkc_b)
                if c == 7:
                    for n2 in range(0, TPC, 2):
                        nn = min(2, TPC - n2)
                        bank, bank_b = k.bank()
                        bankb = bank.bitcast(BF16)
                        k.pe_begin(reads=[mkc_b, cst_b], writes=[bank_b])
                        ins = None
                        for j in range(nn):
                            for h in range(4):
                                ins = nc.tensor.transpose(out=bankb[:, (j * 4 + h) * 128:(j * 4 + h + 1) * 128],
                                                          in_=mkc[:, h, (n2 + j) * 128:(n2 + j + 1) * 128], identity=identb[:])
                        k.pe_end(ins, reads=[mkc_b, cst_b], writes=[bank_b])
                        ms_, ms_b = mkst_ring.next()
                        k.op("act", lambda e: e.activation(out=ms_[:, 0:nn, :],
                                                           in_=bankb[:, 0:nn * 512].rearrange("p (j c) -> p j c", c=512), func=AF.Copy),
                             reads=[bank_b], writes=[ms_b])
                        k.dma("pool", s_mktm[n0 + n2:n0 + n2 + nn].rearrange("n p c -> p n c"), ms_[:, 0:nn, :],
                              reads=[ms_b], owner=ms_b)

            NUNIT = NCH * 8
            units_done = [0]

            for b in range(NB):
                hT, hT_b = hT_ring.next()
                k.dma("sp", hT[:], s_hT[b], writes=[hT_b], owner=hT_b)
                for s in range(4):
                    t = 4 * b + s
                    ts = slice(s * 128, (s + 1) * 128)
                    target = ((t + 1) * NUNIT + NT - 1) // NT
                    while units_done[0] < min(target, NUNIT):
                        conv_unit(units_done[0])
                        units_done[0] += 1

                    def tm_group(c0):
                        res = []
                        for half in range(2):
                            bank, bank_b = k.bank()
                            wbb = wB_bs[c0 // 2048]
                            k.pe_begin(reads=[wbb, hT_b], writes=[bank_b])
                            ins = None
                            for kk in range(8):
                                ins = nc.tensor.matmul(bank, lhsT=hT[:, kk, ts],
                                                       rhs=wB[:, kk, c0 + half * 512:c0 + (half + 1) * 512],
                                                       start=(kk == 0), stop=(kk == 7))
                            k.pe_end(ins, reads=[wbb, hT_b], writes=[bank_b])
                            res.append((bank, bank_b))
                        return res

                    def act_evac(banks, dst, dst_b, func, scale=1.0):
                        for half in range(2):
                            bank, bank_b = banks[half]
                            k.op("act", lambda e: e.activation(out=dst[:, half * 512:(half + 1) * 512], in_=bank,
                                                               func=func, scale=scale), reads=[bank_b], writes=[dst_b])

                    banks = tm_group(0)
                    vst, vst_b = vst_ring.next()
                    act_evac(banks, vst, vst_b, AF.Copy)
                    k.dma("pool", s_v[t], vst[:], reads=[vst_b], owner=vst_b)
                    banks = tm_group(2048)
                    mvst, mvst_b = mvst_ring.next()
                    for half in range(2):
                        bank, bank_b = banks[half]
                        k.op("dve", lambda e: e.tensor_copy(out=mvst[:, half * 512:(half + 1) * 512], in_=bank),
                             reads=[bank_b], writes=[mvst_b])
                    k.dma("pool", s_mv[t], mvst[:], reads=[mvst_b], owner=mvst_b)
                    banks = tm_group(1024)
                    t1, t1_b = t1_ring.next()
                    act_evac(banks, t1, t1_b, AF.Silu)
                    banks = tm_group(4096)
                    t2, t2_b = t2_ring.next()
                    act_evac(banks, t2, t2_b, AF.Tanh, 0.5)
                    k.op("dve", lambda e: e.scalar_tensor_tensor(out=t1[:], in0=t2[:], scalar=1.0, in1=t1[:],
                                                                 op0=ALU.add, op1=ALU.mult), reads=[t1_b, t2_b], writes=[t1_b])
                    gst, gst_b = gst_ring.next()
                    k.op("dve", lambda e: e.scalar_tensor_tensor(out=gst[:], in0=t1[:], scalar=0.5, in1=gnorm[:],
                                                                 op0=ALU.mult, op1=ALU.mult), reads=[t1_b, pb_b], writes=[gst_b])
                    k.dma("pool", s_GA[t], gst[:], reads=[gst_b], owner=gst_b)
                    banks = tm_group(3072)
                    t1, t1_b = t1_ring.next()
                    act_evac(banks, t1, t1_b, AF.Tanh, 0.5)
                    banks = tm_group(5120)
                    t2, t2_b = t2_ring.next()
                    act_evac(banks, t2, t2_b, AF.Tanh, 0.5)
                    k.op("pool", lambda e: e.tensor_scalar(out=t1[:], in0=t1[:], scalar1=1.0, scalar2=0.25,
                                                           op0=ALU.add, op1=ALU.mult), reads=[t1_b], writes=[t1_b])
                    k.op("dve", lambda e: e.scalar_tensor_tensor(out=t1[:], in0=t2[:], scalar=1.0, in1=t1[:],
                                                                 op0=ALU.add, op1=ALU.mult), reads=[t1_b, t2_b], writes=[t1_b])
                    gst, gst_b = gst_ring.next()
                    k.op("pool", lambda e: e.tensor_tensor(out=gst[:], in0=t1[:], in1=mnorm[:], op=ALU.mult),
                         reads=[t1_b, pb_b], writes=[gst_b])
                    k.dma("pool", s_GB[t], gst[:], reads=[gst_b], owner=gst_b)
            k.end_phase()

        with ExitStack() as es:
            def T(name, shape, dt):
                return es.enter_context(nc.sbuf_tensor("sb_" + name + "_%d" % id(es), shape, dt))

            wc_b = Buf("wcast")
            for kk in range(8):
                k.dma("pool", s_wup[kk * 128:(kk + 1) * 128, :], w_up_d[kk * 128:(kk + 1) * 128, :], owner=wc_b)
            for kk in range(32):
                k.dma("pool", s_wdn[kk * 128:(kk + 1) * 128, :], w_down_d[kk * 128:(kk + 1) * 128, :], owner=wc_b)
            G = T("G", [4, 4, S], F32)
            gb = T("gb", [4, 4], F32)
            ones4 = T("ones4", [4, S], F32)
            Ssp = [T("Ssp%d" % d, [4, S], F32) for d in range(2)]
            uu = [T("uu%d" % d, [4, S], F32) for d in range(2)]
            Mg = [T("Mg%d" % d, [4, S], F32) for d in range(2)]
            dec = [T("dec%d" % d, [4, NT], F32) for d in range(2)]
            Gk_b = [k.buf("Gk%d" % i) for i in range(4)]
            gb_b = k.buf("gb")
            on_b = k.buf("ones4")
            g_bd = [k.buf("gwork%d" % d) for d in range(2)]
            for i in range(4):
                k.dma("sp", G[:, i, :], s_gate[4 * i:4 * i + 4, :], writes=[Gk_b[i]], owner=Gk_b[i])
            k.dma("sp", gb[:], gbias_d[:, :], writes=[gb_b], owner=gb_b)
            k.op("pool", lambda e: e.memset(ones4[:], 1.0), writes=[on_b])

            def rvd(d, ap):
                return ap[:, ::-1] if d == 1 else ap

            ngb = T("ngb", [4, 4], F32)
            ngb_b = k.buf("ngb")
            k.op("dve", lambda e: e.tensor_scalar(out=ngb[:], in0=gb[:], scalar1=-1.0, scalar2=None, op0=ALU.mult),
                 reads=[gb_b], writes=[ngb_b])
            for d in range(2):
                fpre = G[:, 2 + d, :]
                k.op("act", lambda e: e.activation(out=fpre, in_=fpre, func=AF.Exp, scale=-1.0, bias=ngb[:, 2 + d:3 + d]),
                     reads=[Gk_b[2 + d], ngb_b], writes=[Gk_b[2 + d]])
            for d in range(2):
                fpre = G[:, 2 + d, :]
                k.op("act", lambda e: e.activation(out=fpre, in_=fpre, func=AF.Ln, bias=1.0),
                     reads=[Gk_b[2 + d]], writes=[Gk_b[2 + d]])
            for d in range(2):
                fpre = G[:, 2 + d, :]
                k.op("dve", lambda e: e.tensor_tensor_scan(out=rvd(d, Ssp[d][:]), data0=rvd(d, ones4[:]), data1=rvd(d, fpre),
                                                           initial=0.0, op0=ALU.mult, op1=ALU.add),
                     reads=[Gk_b[2 + d], on_b], writes=[g_bd[d]])
            for d in range(2):
                ipre = G[:, d, :]
                k.op("dve", lambda e: e.scalar_tensor_tensor(out=uu[d][:], in0=ipre, scalar=gb[:, d:d + 1], in1=Ssp[d][:],
                                                             op0=ALU.add, op1=ALU.add),
                     reads=[Gk_b[d], gb_b, g_bd[d]], writes=[g_bd[d]])
            for d in range(2):
                k.op("dve", lambda e: e.tensor_tensor_scan(out=rvd(d, Mg[d][:]), data0=rvd(d, ones4[:]), data1=rvd(d, uu[d][:]),
                                                           initial=-1e30, op0=ALU.mult, op1=ALU.max),
                     reads=[g_bd[d], on_b], writes=[g_bd[d]])
            views = []
            for d in range(2):
                endc = 127 if d == 0 else 0
                Mgv = Mg[d][:].rearrange("p (n t) -> p n t", t=128)
                Mnb = Mgv[:, :, endc:endc + 1].to_broadcast([4, NT, 128])
                uv = uu[d][:].rearrange("p (n t) -> p n t", t=128)
                sv = Ssp[d][:].rearrange("p (n t) -> p n t", t=128)
                views.append((Mgv, Mnb, uv, sv, endc))
            for d in range(2):
                Mgv, Mnb, uv, sv, endc = views[d]
                k.op("dve", lambda e: e.tensor_tensor(out=uv, in0=uv, in1=Mnb, op=ALU.subtract), reads=[g_bd[d]], writes=[g_bd[d]])
            for d in range(2):
                k.op("act", lambda e: e.activation(out=uu[d][:], in_=uu[d][:], func=AF.Exp, bias=LN_QS),
                     reads=[g_bd[d]], writes=[g_bd[d]])
            for d in range(2):
                Mgv, Mnb, uv, sv, endc = views[d]
                k.op("dve", lambda e: e.tensor_tensor(out=sv, in0=sv, in1=Mnb, op=ALU.subtract), reads=[g_bd[d]], writes=[g_bd[d]])
            for d in range(2):
                k.op("act", lambda e: e.activation(out=Ssp[d][:], in_=Ssp[d][:], func=AF.Exp), reads=[g_bd[d]], writes=[g_bd[d]])
            for d in range(2):
                Mgv, Mnb, uv, sv, endc = views[d]
                k.op("dve", lambda e: e.memset(dec[d][:], 0.0), reads=[g_bd[d]], writes=[g_bd[d]])
                if NT > 1:
                    Mn2 = Mgv[:, :, endc]
                    if d == 0:
                        k.op("dve", lambda e: e.tensor_tensor(out=dec[d][:, 1:NT], in0=Mn2[:, 0:NT - 1], in1=Mn2[:, 1:NT],
                                                              op=ALU.subtract), reads=[g_bd[d]], writes=[g_bd[d]])
                        k.op("act", lambda e: e.activation(out=dec[d][:, 1:NT], in_=dec[d][:, 1:NT], func=AF.Exp),
                             reads=[g_bd[d]], writes=[g_bd[d]])
                    else:
                        k.op("dve", lambda e: e.tensor_tensor(out=dec[d][:, 0:NT - 1], in0=Mn2[:, 1:NT], in1=Mn2[:, 0:NT - 1],
                                                              op=ALU.subtract), reads=[g_bd[d]], writes=[g_bd[d]])
                        k.op("act", lambda e: e.activation(out=dec[d][:, 0:NT - 1], in_=dec[d][:, 0:NT - 1], func=AF.Exp),
                             reads=[g_bd[d]], writes=[g_bd[d]])
            for d in range(2):
                for src, dst, dst_b in ((uu[d], w_tm, wtm_b), (Ssp[d], thr_tm, thr_b)):
                    bank, bank_b = k.bank()
                    k.pe_begin(reads=[g_bd[d], cst_b], writes=[bank_b])
                    ins = None
                    for n in range(NT):
                        ins = nc.tensor.matmul(bank[:, n * 4:(n + 1) * 4], lhsT=src[:, n * 128:(n + 1) * 128],
                                               rhs=identf[0:4, 0:4], start=True, stop=True)
                    k.pe_end(ins, reads=[g_bd[d], cst_b], writes=[bank_b])
                    k.op("dve", lambda e: e.tensor_copy(out=dst[:, d, :, :],
                                                        in_=bank[:, 0:NT * 4].rearrange("p (n h) -> p n h", h=4)),
                         reads=[bank_b], writes=[dst_b])
                bank, bank_b = k.bank()
                k.pe_begin(reads=[g_bd[d], cst_b], writes=[bank_b])
                for h in range(4):
                    ins = nc.tensor.matmul(bank[:, h * NT:(h + 1) * NT], lhsT=sel[:, h, :], rhs=dec[d][:], start=True, stop=True)
                k.pe_end(ins, reads=[g_bd[d], cst_b], writes=[bank_b])
                k.op("dve", lambda e: e.tensor_copy(out=dec_bc[:, d, :, :],
                                                    in_=bank[:, 0:4 * NT].rearrange("p (h n) -> p h n", h=4)),
                     reads=[bank_b], writes=[dec_b])

            k.end_phase()

        with ExitStack() as es:
            def T(name, shape, dt):
                return es.enter_context(nc.sbuf_tensor("sb_" + name + "_%d" % id(es), shape, dt))

            Sst = T("Sst", [128, 4, 256], F32)
            Sbf = [T("Sbf%d" % i, [128, 4, 256], BF16) for i in range(2)]
            Cst = T("Cst", [128, 4, 257], F32)
            Cbf = [T("Cbf%d" % i, [128, 4, 257], BF16) for i in range(2)]
            S_b = [k.buf("S%d" % h) for h in range(4)]
            Sbf_b = [k.buf("Sbf%d" % i) for i in range(2)]
            C_b = [k.buf("C%d" % h) for h in range(4)]
            Cbf_b = [k.buf("Cbf%d" % i) for i in range(2)]
            NL = 3
            ld = {}
            for nm, shp in (("qe", [128, 512]), ("ke", [128, 512]), ("kd", [128, 512]), ("v", [128, 1024]),
                            ("mq", [128, 512]), ("mk", [128, 512]), ("mktm", [128, 512]), ("mv", [128, 1024])):
                ld[nm] = [T("ld_%s%d" % (nm, i), shp, BF16) for i in range(NL)]
            ld_b = [k.buf("ld%d" % i) for i in range(NL)]
            ld_i = [0]
            vw_ring = Ring(k, es, "vw", [128, 4, 257], BF16, 3)
            at_ring = Ring(k, es, "at", [128, 512], BF16, 4)
            qd_ring = Ring(k, es, "qd", [128, 128], BF16, 3)
            dn_ring = Ring(k, es, "dn", [128, 16], F32, 4)
            oA_ring = Ring(k, es, "oA", [128, D], F32, 2)
            hB_ring = Ring(k, es, "hB", [128, D], F32, 2)
            obA_t = [k.buf("obA_t%d" % n) for n in range(NT)]
            obB_t = [k.buf("obB_t%d" % n) for n in range(NT)]
            for d in (1, 0):
                order = list(range(NT)) if d == 0 else list(range(NT - 1, -1, -1))
                maskb = tri[:, d:d + 1, :].to_broadcast([128, 4, 128])
                first = True
                def issue_loads(si2):
                    n2 = order[si2]
                    li = ld_i[0]
                    ld_i[0] = (li + 1) % NL
                    lb2 = ld_b[li]
                    L2 = {nm: ld[nm][li] for nm in ld}
                    k.dma("sp", L2["qe"][:], s_qe[d][n2], writes=[lb2], owner=lb2)
                    k.dma("sp", L2["ke"][:], s_ke[d][n2], writes=[lb2], owner=lb2)
                    k.dma("sp", L2["kd"][:], s_kd[d][n2], writes=[lb2], owner=lb2)
                    k.dma("sp", L2["v"][:], s_v[n2], writes=[lb2], owner=lb2)
                    k.dma("sp", L2["mq"][:], s_mq[n2].rearrange("p h t -> p (h t)"), writes=[lb2], owner=lb2)
                    k.dma("sp", L2["mk"][:], s_mk[n2].rearrange("p h t -> p (h t)"), writes=[lb2], owner=lb2)
                    k.dma("sp", L2["mktm"][:], s_mktm[n2], writes=[lb2], owner=lb2)
                    k.dma("sp", L2["mv"][:], s_mv[n2], writes=[lb2], owner=lb2)
                    vw2, vw2_b = vw_ring.next()
                    k.op("pool", lambda e: e.tensor_tensor(
                        out=vw2[:, :, 0:256], in0=L2["mv"][:].rearrange("p (h c) -> p h c", h=4),
                        in1=w_tm[:, d, n2, :].unsqueeze(2).to_broadcast([128, 4, 256]), op=ALU.mult),
                        reads=[lb2, wtm_b], writes=[vw2_b])
                    k.op("pool", lambda e: e.tensor_copy(out=vw2[:, :, 256], in_=w_tm[:, d, n2, :]), reads=[wtm_b], writes=[vw2_b])
                    return (lb2, L2, vw2, vw2_b)

                pending = issue_loads(0)
                for si, n in enumerate(order):
                    nxt = order[si + 1] if si + 1 < NT else None
                    cur = si % 2
                    prv = 1 - cur
                    lb, L, vw, vw_b = pending
                    if nxt is not None:
                        pending = issue_loads(si + 1)
                    oA, oA_b = oA_ring.next()
                    hB, hB_b = hB_ring.next()
                    s1, s1_b = k.bank()
                    k.pe_begin(reads=[lb], writes=[s1_b])
                    for h in range(4):
                        hk = slice(h * 128, (h + 1) * 128)
                        ins = nc.tensor.matmul(s1[:, hk], lhsT=L["ke"][:, hk], rhs=L["qe"][:, hk], start=True, stop=True)
                    k.pe_end(ins, reads=[lb], writes=[s1_b])
                    s2, s2_b = k.bank()
                    k.pe_begin(reads=[lb], writes=[s2_b])
                    for h in range(4):
                        hk = slice(h * 128, (h + 1) * 128)
                        ins = nc.tensor.matmul(s2[:, hk], lhsT=L["mk"][:, hk], rhs=L["mq"][:, hk], start=True, stop=True)
                    k.pe_end(ins, reads=[lb], writes=[s2_b])
                    ub = []
                    for g2 in range(2):
                        bu, bu_b = k.bank()
                        k.pe_begin(reads=[lb], writes=[bu_b])
                        for hh in range(2):
                            h = 2 * g2 + hh
                            ins = nc.tensor.matmul(bu[:, hh * 256:(hh + 1) * 256], lhsT=L["kd"][:, h * 128:(h + 1) * 128],
                                                   rhs=L["v"][:, h * 256:(h + 1) * 256], start=True, stop=True)
                        k.pe_end(ins, reads=[lb], writes=[bu_b])
                        ub.append((bu, bu_b))
                    u2 = []
                    for h in range(4):
                        bu2, bu2_b = k.bank()
                        k.pe_begin(reads=[lb, vw_b], writes=[bu2_b])
                        ins = nc.tensor.matmul(bu2[:, 0:257], lhsT=L["mktm"][:, h * 128:(h + 1) * 128], rhs=vw[:, h, :],
                                               start=True, stop=True)
                        k.pe_end(ins, reads=[lb, vw_b], writes=[bu2_b])
                        u2.append((bu2, bu2_b))
                    at1, at1_b = at_ring.next()
                    k.op("dve", lambda e: e.tensor_tensor(out=at1[:].rearrange("p (h t) -> p h t", h=4),
                                                          in0=s1.rearrange("p (h t) -> p h t", h=4), in1=maskb, op=ALU.mult),
                         reads=[s1_b, cst_b], writes=[at1_b])
                    at2, at2_b = at_ring.next()
                    k.op("dve", lambda e: e.tensor_tensor(out=at2[:].rearrange("p (h t) -> p h t", h=4),
                                                          in0=s2.rearrange("p (h t) -> p h t", h=4), in1=maskb, op=ALU.mult),
                         reads=[s2_b, cst_b], writes=[at2_b])
                    for h in range(4):
                        bu, bu_b = ub[h // 2]
                        src = bu[:, (h % 2) * 256:(h % 2 + 1) * 256]
                        if first:
                            k.op("dve", lambda e: e.tensor_copy(out=Sst[:, h, :], in_=src), reads=[bu_b], writes=[S_b[h]])
                        else:
                            k.op("dve", lambda e: e.scalar_tensor_tensor(out=Sst[:, h, :], in0=Sst[:, h, :],
                                                                         scalar=EB[:, d, n, h:h + 1], in1=src,
                                                                         op0=ALU.mult, op1=ALU.add),
                                 reads=[bu_b, S_b[h], EB_b], writes=[S_b[h]])
                    for h in range(4):
                        bu2, bu2_b = u2[h]
                        if first:
                            k.op("dve", lambda e: e.tensor_copy(out=Cst[:, h, :], in_=bu2[:, 0:257]),
                                 reads=[bu2_b], writes=[C_b[h]])
                        else:
                            k.op("dve", lambda e: e.scalar_tensor_tensor(out=Cst[:, h, :], in0=Cst[:, h, :],
                                                                         scalar=dec_bc[:, d, h, n:n + 1], in1=bu2[:, 0:257],
                                                                         op0=ALU.mult, op1=ALU.add),
                                 reads=[bu2_b, C_b[h], dec_b], writes=[C_b[h]])
                    ob = []
                    for g2 in range(2):
                        bo, bo_b = k.bank()
                        rd = [at1_b, lb] + ([] if first else [Sbf_b[prv]])
                        k.pe_begin(reads=rd, writes=[bo_b])
                        for hh in range(2):
                            h = 2 * g2 + hh
                            hk = slice(h * 128, (h + 1) * 128)
                            dst = bo[:, hh * 256:(hh + 1) * 256]
                            ins = nc.tensor.matmul(dst, lhsT=at1[:, hk], rhs=L["v"][:, h * 256:(h + 1) * 256],
                                                   start=True, stop=first)
                            if not first:
                                ins = nc.tensor.matmul(dst, lhsT=L["qe"][:, hk], rhs=Sbf[prv][:, h, :], start=False, stop=True)
                        k.pe_end(ins, reads=rd, writes=[bo_b])
                        ob.append((bo, bo_b))
                    nb = []
                    for h in range(4):
                        hk = slice(h * 128, (h + 1) * 128)
                        bn, bn_b = k.bank()
                        rd = [at2_b, vw_b, lb] + ([] if first else [Cbf_b[prv]])
                        k.pe_begin(reads=rd, writes=[bn_b])
                        ins = nc.tensor.matmul(bn[:, 0:257], lhsT=at2[:, hk], rhs=vw[:, h, :], start=True, stop=first)
                        if not first:
                            ins = nc.tensor.matmul(bn[:, 0:257], lhsT=L["mq"][:, hk], rhs=Cbf[prv][:, h, :], start=False, stop=True)
                        k.pe_end(ins, reads=rd, writes=[bn_b])
                        nb.append((bn, bn_b))
                    if nxt is not None:
                        k.op("act", lambda e: e.activation(out=Sbf[cur][:], in_=Sst[:], func=AF.Copy), reads=S_b, writes=[Sbf_b[cur]])
                        k.op("pool", lambda e: e.tensor_tensor(
                            out=Cbf[cur][:], in0=Cst[:], in1=dec_bc[:, d, :, nxt:nxt + 1].to_broadcast([128, 4, 257]),
                            op=ALU.mult), reads=C_b + [dec_b], writes=[Cbf_b[cur]])
                    for g2 in range(2):
                        bo, bo_b = ob[g2]
                        k.op("act", lambda e: e.activation(out=oA[:, g2 * 512:(g2 + 1) * 512], in_=bo, func=AF.Copy),
                             reads=[bo_b], writes=[oA_b])
                    dn, dn_b = dn_ring.next()
                    for h in range(4):
                        bn, bn_b = nb[h]
                        k.op("act", lambda e: e.activation(out=dn[:, h:h + 1], in_=bn[:, 256:257], func=AF.Copy),
                             reads=[bn_b], writes=[dn_b])
                    k.op("dve", lambda e: e.scalar_tensor_tensor(out=dn[:, 4:8], in0=dn[:, 0:4], scalar=-1.0, in1=dn[:, 0:4],
                                                                 op0=ALU.mult, op1=ALU.max), reads=[dn_b], writes=[dn_b])
                    k.op("dve", lambda e: e.tensor_tensor(out=dn[:, 8:12], in0=dn[:, 4:8], in1=thr_tm[:, d, n, :], op=ALU.max),
                         reads=[dn_b, thr_b], writes=[dn_b])
                    k.op("dve", lambda e: e.reciprocal(out=dn[:, 12:16], in_=dn[:, 8:12]), reads=[dn_b], writes=[dn_b])
                    for h in range(4):
                        bn, bn_b = nb[h]
                        k.op("act", lambda e: e.activation(out=hB[:, h * 256:(h + 1) * 256], in_=bn[:, 0:256], func=AF.Copy,
                                                           scale=dn[:, 12 + h:13 + h]),
                             reads=[bn_b, dn_b], writes=[hB_b])
                    if d == 1:
                        k.dma("pool", s_obA[n], oA[:], reads=[oA_b], writes=[obA_t[n]], owner=oA_b)
                        k.dma("pool", s_obB[n], hB[:], reads=[hB_b], writes=[obB_t[n]], owner=hB_b)
                    else:
                        k.dma("pool", s_obA[n], oA[:], reads=[oA_b], writes=[obA_t[n]], owner=oA_b, accum_op=ALU.add)
                        k.dma("pool", s_obB[n], hB[:], reads=[hB_b], writes=[obB_t[n]], owner=hB_b, accum_op=ALU.add)
                    first = False
            k.end_phase()

        with ExitStack() as es:
            def T(name, shape, dt):
                return es.enter_context(nc.sbuf_tensor("sb_" + name + "_%d" % id(es), shape, dt))

            wout = T("wout", [128, 8, D], BF16)
            wpost = T("wpost", [128, D], F32)
            wffn = T("wffn", [128, D], F32)
            zero2 = T("zero2", [128, 8, 1], BF16)
            p2_b = k.buf("p2_c")
            for kk in range(8):
                k.dma("pool", wout[:, kk, :], w_out_d[kk * 128:(kk + 1) * 128, :], writes=[p2_b], owner=p2_b)
            k.dma("sp", wpost[:], norms_d[1:2, :].partition_broadcast(128), writes=[p2_b], owner=p2_b)
            k.dma("sp", wffn[:], norms_d[4:5, :].partition_broadcast(128), writes=[p2_b], owner=p2_b)
            z_b = k.buf("zero2")
            k.op("pool", lambda e: e.memset(zero2[:], 0.0), writes=[z_b])
            k.dma("pool", s_h2T[:, :, 0:1], zero2[:], reads=[z_b], owner=z_b, allow_slow_non_contiguous=True)
            k.dma("pool", s_h2T[:, :, S + 1:S + 2], zero2[:], reads=[z_b], owner=z_b, allow_slow_non_contiguous=True)

            RD = 11
            A_ring = Ring(k, es, "cA", [128, D], F32, RD)
            B_ring = Ring(k, es, "cB", [128, D], F32, RD)
            GA_ring = Ring(k, es, "cGA", [128, D], BF16, 6)
            GB_ring = Ring(k, es, "cGB", [128, D], BF16, 6)
            X_ring = Ring(k, es, "cX", [128, D], F32, 4)
            ss_ring = Ring(k, es, "css", [128, 32], F32, RD)
            ybf_ring = Ring(k, es, "cybf", [128, D], BF16, 3)
            yT_ring = Ring(k, es, "cyT", [128, 8, 128], BF16, 3)
            h2_ring = Ring(k, es, "ch2", [128, D], BF16, 3)
            h2T_ring = Ring(k, es, "ch2T", [128, 8, 128], BF16, 3)
            ctxs = {}

            def f0(n):
                c = {"n": n}
                c["A"], c["A_b"] = A_ring.next()
                c["B"], c["B_b"] = B_ring.next()
                c["GA"], c["GA_b"] = GA_ring.next()
                c["GB"], c["GB_b"] = GB_ring.next()
                c["ss"], c["ss_b"] = ss_ring.next()
                k.dma("sp", c["A"][:], s_obA[n], writes=[c["A_b"]], owner=c["A_b"])
                k.dma("sp", c["B"][:], s_obB[n], writes=[c["B_b"]], owner=c["B_b"])
                k.dma("sp", c["GA"][:], s_GA[n], writes=[c["GA_b"]], owner=c["GA_b"])
                k.dma("sp", c["GB"][:], s_GB[n], writes=[c["GB_b"]], owner=c["GB_b"])
                ctxs[n] = c

            def f1(n):
                c = ctxs[n]
                A, A_b, B, B_b, ss, ss_b = c["A"], c["A_b"], c["B"], c["B_b"], c["ss"], c["ss_b"]
                for h in range(4):
                    k.op("act", lambda e: e.activation(out=junk[:, 0:256], in_=A[:, h * 256:(h + 1) * 256], func=AF.Square,
                                                       accum_out=ss[:, h:h + 1]), reads=[A_b], writes=[ss_b])
                for h in range(4):
                    k.op("act", lambda e: e.activation(out=junk[:, 0:256], in_=B[:, h * 256:(h + 1) * 256], func=AF.Square,
                                                       accum_out=ss[:, 4 + h:5 + h]), reads=[B_b], writes=[ss_b])

            def f2(n):
                c = ctxs[n]
                A, A_b, B, B_b, ss, ss_b = c["A"], c["A_b"], c["B"], c["B_b"], c["ss"], c["ss_b"]
                GA, GA_b, GBt, GB_b = c["GA"], c["GA_b"], c["GB"], c["GB_b"]
                rstd_act(ss[:, 0:8], ss[:, 8:16], ss[:, 16:24], ss_b, 1.0 / DV, n=8)
                k.op("pool", lambda e: e.tensor_tensor(out=A[:], in0=A[:], in1=GA[:], op=ALU.mult),
                     reads=[A_b, GA_b], writes=[A_b])
                k.op("pool", lambda e: e.tensor_tensor(out=B[:], in0=B[:], in1=GBt[:], op=ALU.mult),
                     reads=[B_b, GB_b], writes=[B_b])

            def f3(n):
                c = ctxs[n]
                A, A_b, B, B_b, ss, ss_b = c["A"], c["A_b"], c["B"], c["B_b"], c["ss"], c["ss_b"]
                c["ybf"], c["ybf_b"] = ybf_ring.next()
                ybf = c["ybf"]
                for h in range(4):
                    hs = slice(h * 256, (h + 1) * 256)
                    k.op("dve", lambda e: e.tensor_scalar(out=B[:, hs], in0=B[:, hs], scalar1=ss[:, 20 + h:21 + h],
                                                          scalar2=None, op0=ALU.mult),
                         reads=[B_b, ss_b], writes=[B_b])
                for h in range(4):
                    hs = slice(h * 256, (h + 1) * 256)
                    k.op("dve", lambda e: e.scalar_tensor_tensor(out=ybf[:, hs], in0=A[:, hs], scalar=ss[:, 16 + h:17 + h],
                                                                 in1=B[:, hs], op0=ALU.mult, op1=ALU.add),
                         reads=[A_b, B_b, ss_b], writes=[c["ybf_b"]])

            def f4(n):
                c = ctxs[n]
                c["yT"], c["yT_b"] = yT_ring.next()
                transpose8(c["ybf"], c["ybf_b"], c["yT"][:], c["yT_b"], "act")
                c["X"], c["X_b"] = X_ring.next()
                k.dma("sp", c["X"][:], x_d[n * 128:(n + 1) * 128, :], writes=[c["X_b"]], owner=c["X_b"])

            def f5(n):
                c = ctxs[n]
                ss, ss_b = c["ss"], c["ss_b"]
                yT, yT_b = c["yT"], c["yT_b"]
                banks = []
                for half in range(2):
                    bank, bank_b = k.bank()
                    k.pe_begin(reads=[yT_b, p2_b], writes=[bank_b])
                    ins = None
                    for kk in range(8):
                        ins = nc.tensor.matmul(bank, lhsT=yT[:, kk, :], rhs=wout[:, kk, half * 512:(half + 1) * 512],
                                               start=(kk == 0), stop=(kk == 7))
                    k.pe_end(ins, reads=[yT_b, p2_b], writes=[bank_b])
                    banks.append((bank, bank_b))
                c["banks"] = banks
                for half in range(2):
                    bank, bank_b = banks[half]
                    k.op("act", lambda e: e.activation(out=junk[:, 0:512], in_=bank, func=AF.Square,
                                                       accum_out=ss[:, 24 + half:25 + half]), reads=[bank_b], writes=[ss_b])

            def f6(n):
                c = ctxs[n]
                ss, ss_b = c["ss"], c["ss_b"]
                k.op("pool", lambda e: e.tensor_tensor(out=ss[:, 26:27], in0=ss[:, 24:25], in1=ss[:, 25:26], op=ALU.add),
                     reads=[ss_b], writes=[ss_b])
                rstd_act(ss[:, 26:27], ss[:, 27:28], ss[:, 28:29], ss_b, 1.0 / D)

            def f7(n):
                c = ctxs[n]
                A, A_b, B, B_b, ss, ss_b, X, X_b = c["A"], c["A_b"], c["B"], c["B_b"], c["ss"], c["ss_b"], c["X"], c["X_b"]
                for half in range(2):
                    bank, bank_b = c["banks"][half]
                    cs = slice(half * 512, (half + 1) * 512)
                    k.op("dve", lambda e: e.scalar_tensor_tensor(out=A[:, cs], in0=bank, scalar=ss[:, 28:29], in1=wpost[:, cs],
                                                                 op0=ALU.mult, op1=ALU.mult),
                         reads=[bank_b, ss_b, p2_b], writes=[A_b])
                k.op("pool", lambda e: e.tensor_tensor(out=B[:], in0=A[:], in1=X[:], op=ALU.add),
                     reads=[A_b, X_b], writes=[B_b])
                k.dma("pool", s_x1[n], B[:], reads=[B_b], owner=B_b)
                k.op("act", lambda e: e.activation(out=junk[:], in_=B[:], func=AF.Square, accum_out=ss[:, 29:30]),
                     reads=[B_b], writes=[ss_b])

            def f8(n):
                c = ctxs[n]
                ss, ss_b = c["ss"], c["ss_b"]
                rstd_act(ss[:, 29:30], ss[:, 30:31], ss[:, 31:32], ss_b, 1.0 / D)

            def f9(n):
                c = ctxs[n]
                B, B_b, ss, ss_b = c["B"], c["B_b"], c["ss"], c["ss_b"]
                h2, h2_b = h2_ring.next()
                k.op("dve", lambda e: e.scalar_tensor_tensor(out=h2[:], in0=B[:], scalar=ss[:, 31:32], in1=wffn[:],
                                                             op0=ALU.mult, op1=ALU.mult),
                     reads=[B_b, ss_b, p2_b], writes=[h2_b])
                h2T, h2T_b = h2T_ring.next()
                transpose8(h2, h2_b, h2T[:], h2T_b, "act")
                k.dma("pool", s_h2T[:, :, 1 + n * 128:1 + (n + 1) * 128], h2T[:], reads=[h2T_b], owner=h2T_b)
                del ctxs[n]

            stages = [f0, f1, f2, f3, f4, f5, f6, f7, f8, f9]
            NSTG = len(stages)
            GS = 1
            LAG = 1
            groups = [list(range(g0, min(NT, g0 + GS))) for g0 in range(0, NT, GS)]
            NG = len(groups)
            for tau in range((NG - 1) * LAG + NSTG):
                for g in range(NG):
                    st = tau - g * LAG
                    if 0 <= st < NSTG:
                        for t in groups[g]:
                            stages[st](t)
            k.end_phase()

        with ExitStack() as es:
            def T(name, shape, dt):
                return es.enter_context(nc.sbuf_tensor("sb_" + name + "_%d" % id(es), shape, dt))

            wg = T("wg", [128, 8, D], BF16)
            wp = T("wp", [128, 2, D], BF16)
            fcw = T("fcw", [128, 64, 3], F32)
            fcb = T("fcb", [128, 64], F32)
            wfpost = T("wfpost", [128, D], F32)
            wple = T("wple", [128, D], F32)
            p3_b = k.buf("p3_c")
            pw_b = k.buf("p3_w")
            k.dma("sp", fcw[:], ffcw_d[:, :, :], writes=[p3_b], owner=p3_b)
            k.dma("sp", fcb[:], ffcb_d[:, :], writes=[p3_b], owner=p3_b)
            k.dma("sp", wfpost[:], norms_d[5:6, :].partition_broadcast(128), writes=[p3_b], owner=p3_b)
            k.dma("sp", wple[:], norms_d[6:7, :].partition_broadcast(128), writes=[p3_b], owner=p3_b)
            h2_ring = Ring(k, es, "h2b", [128, 8, 514], BF16, 1)
            gT = T("gT", [128, 32, 512], BF16)
            gT_bs = [k.buf("gT%d" % i) for i in range(8)]
            wug_ring = Ring(k, es, "wug", [128, 2, 8, 512], BF16, 2)
            wd_ring = Ring(k, es, "wdn", [128, 4, D], BF16, 2)
            yg_ring = Ring(k, es, "yg", [128, 256], F32, 3)
            yv_ring = Ring(k, es, "yv", [128, 256], F32, 3)
            gl_ring = Ring(k, es, "gl", [128, 256], F32, 2)
            x1_ring = Ring(k, es, "x1f", [128, D], F32, 4)
            x2_ring = Ring(k, es, "x2f", [128, D], F32, 4)
            x2b_ring = Ring(k, es, "x2b", [128, D], BF16, 2)
            x2T_ring = Ring(k, es, "x2T", [128, 8, 128], BF16, 2)
            pt_ring = Ring(k, es, "ptf", [128, PLE], F32, 4)
            ptb_ring = Ring(k, es, "ptb", [128, PLE], BF16, 2)
            pT_ring = Ring(k, es, "pTf", [128, 2, 128], BF16, 2)
            tg_ring = Ring(k, es, "tgf", [128, D], F32, 2)
            ss_ring = Ring(k, es, "ssf", [128, 16], F32, 4)
            xo_ring = Ring(k, es, "xo", [128, D], F32, 1)

            def conv3(bank, bank_b, c, dst, dst_b):
                k.op("act", lambda e: e.activation(out=dst[:], in_=bank[:, 1:257], func=AF.Identity, scale=fcw[:, c, 1:2],
                                                   bias=fcb[:, c:c + 1]),
                     reads=[bank_b, p3_b], writes=[dst_b])
                k.op("dve", lambda e: e.scalar_tensor_tensor(out=dst[:], in0=bank[:, 0:256], scalar=fcw[:, c, 0:1], in1=dst[:],
                                                             op0=ALU.mult, op1=ALU.add), reads=[bank_b, p3_b, dst_b], writes=[dst_b])
                k.op("dve", lambda e: e.scalar_tensor_tensor(out=dst[:], in0=bank[:, 2:258], scalar=fcw[:, c, 2:3], in1=dst[:],
                                                             op0=ALU.mult, op1=ALU.add), reads=[bank_b, p3_b, dst_b], writes=[dst_b])

            print("phase3 sbuf remaining", nc.sbuf_bytes_remaining)
            raw_ring = Ring(k, es, "rawv", [128, 256], F32, 2)
            tp_ring = Ring(k, es, "tpv", [128, 256], F32, 2)

            def conv3p(bank, bank_b, c, dst, dst_b):
                k.op("act", lambda e: e.activation(out=dst[:], in_=bank[:, 1:257], func=AF.Identity, scale=fcw[:, c, 1:2],
                                                   bias=fcb[:, c:c + 1]),
                     reads=[bank_b, p3_b], writes=[dst_b])
                raw, raw_b = raw_ring.next()
                k.op("act", lambda e: e.activation(out=raw[:], in_=bank[:, 0:256], func=AF.Copy),
                     reads=[bank_b], writes=[raw_b])
                k.op("dve", lambda e: e.scalar_tensor_tensor(out=dst[:], in0=bank[:, 2:258], scalar=fcw[:, c, 2:3], in1=dst[:],
                                                             op0=ALU.mult, op1=ALU.add), reads=[bank_b, p3_b, dst_b], writes=[dst_b])
                tp, tp_b = tp_ring.next()
                k.op("pool", lambda e: e.tensor_scalar(out=tp[:], in0=raw[:], scalar1=fcw[:, c, 0:1], scalar2=0.0,
                                                       op0=ALU.mult, op1=ALU.add), reads=[raw_b, p3_b], writes=[tp_b])
                k.op("pool", lambda e: e.tensor_tensor(out=dst[:], in0=dst[:], in1=tp[:], op=ALU.add),
                     reads=[dst_b, tp_b], writes=[dst_b])

            eS_ring = Ring(k, es, "eSf", [128, D], F32, 2)
            from collections import deque
            wug_jobs = deque((bb, gg) for bb in range(NB) for gg in range(8))
            wug_pend = deque()

            def wug_prefetch():
                while len(wug_pend) < 2 and wug_jobs:
                    bb, gg = wug_jobs.popleft()
                    wug, wug_b = wug_ring.next()
                    for gv in range(2):
                        c0 = gv * DFF + gg * 512
                        k.dma("sp", wug[:, gv, :, :], s_wup[:, c0:c0 + 512].rearrange("(kk p) c -> p kk c", p=128),
                              writes=[wug_b], owner=wug_b)
                    wug_pend.append((wug, wug_b))

            wd_jobs = deque((bb, pp) for bb in range(NB) for pp in range(8))
            wd_pend = deque()

            def wd_prefetch():
                while len(wd_pend) < 2 and wd_jobs:
                    bb, piece = wd_jobs.popleft()
                    wd, wd_b = wd_ring.next()
                    k.dma("sp", wd[:], s_wdn[piece * 512:(piece + 1) * 512, :].rearrange("(i p) c -> p i c", p=128),
                          writes=[wd_b], owner=wd_b)
                    wd_pend.append((wd, wd_b))

            def load_h2(bb):
                h2, h2_b = h2_ring.next()
                k.dma("sp", h2[:], s_h2T[:, :, bb * 512:bb * 512 + 514], writes=[h2_b], owner=h2_b)
                return h2, h2_b

            deferred = deque()

            def emit_deferred():
                if deferred:
                    for fn in deferred.popleft():
                        fn()

            def make_tail(b, x2s, pts):
                tctx = [dict() for _ in range(4)]

                def A1(s):
                    c = tctx[s]
                    x2, x2_b, ss, ss_b = x2s[s]
                    pt, pt_b = pts[s]
                    x2b, x2b_b = x2b_ring.next()
                    k.op("act", lambda e: e.activation(out=x2b[:], in_=x2[:], func=AF.Copy), reads=[x2_b], writes=[x2b_b])
                    ptb, ptb_b = ptb_ring.next()
                    k.op("pool", lambda e: e.tensor_copy(out=ptb[:], in_=pt[:]), reads=[pt_b], writes=[ptb_b])
                    c["x2b"], c["x2b_b"], c["ptb"], c["ptb_b"] = x2b, x2b_b, ptb, ptb_b

                def A2(s):
                    c = tctx[s]
                    x2b, x2b_b, ptb, ptb_b = c["x2b"], c["x2b_b"], c["ptb"], c["ptb_b"]
                    bank, bank_b = k.bank()
                    bankb = bank.bitcast(BF16)
                    k.pe_begin(reads=[x2b_b, cst_b], writes=[bank_b])
                    ins = None
                    for kk in range(8):
                        ins = nc.tensor.transpose(out=bankb[:, kk * 128:(kk + 1) * 128], in_=x2b[:, kk * 128:(kk + 1) * 128],
                                                  identity=identb[:])
                    k.pe_end(ins, reads=[x2b_b, cst_b], writes=[bank_b])
                    c["bk1"] = (bankb, bank_b)
                    bank, bank_b = k.bank()
                    bankb = bank.bitcast(BF16)
                    k.pe_begin(reads=[ptb_b, cst_b], writes=[bank_b])
                    for kk in range(2):
                        ins = nc.tensor.transpose(out=bankb[:, kk * 128:(kk + 1) * 128], in_=ptb[:, kk * 128:(kk + 1) * 128],
                                                  identity=identb[:])
                    k.pe_end(ins, reads=[ptb_b, cst_b], writes=[bank_b])
                    c["bk2"] = (bankb, bank_b)

                def A3(s):
                    c = tctx[s]
                    x2T, x2T_b = x2T_ring.next()
                    bankb, bank_b = c["bk1"]
                    k.op("act", lambda e: e.activation(out=x2T[:], in_=bankb.rearrange("p (k t) -> p k t", k=8), func=AF.Copy),
                         reads=[bank_b], writes=[x2T_b])
                    pT, pT_b = pT_ring.next()
                    bankb, bank_b = c["bk2"]
                    k.op("act", lambda e: e.activation(out=pT[:], in_=bankb[:, 0:256].rearrange("p (k t) -> p k t", k=2),
                                                       func=AF.Copy), reads=[bank_b], writes=[pT_b])
                    c["x2T"], c["x2T_b"], c["pT"], c["pT_b"] = x2T, x2T_b, pT, pT_b

                def A4(s):
                    c = tctx[s]
                    x2T, x2T_b, pT, pT_b = c["x2T"], c["x2T_b"], c["pT"], c["pT_b"]
                    gbanks = []
                    ebanks = []
                    for half in range(2):
                        cs = slice(half * 512, (half + 1) * 512)
                        bank, bank_b = k.bank()
                        k.pe_begin(reads=[x2T_b, pw_b], writes=[bank_b])
                        ins = None
                        for kk in range(8):
                            ins = nc.tensor.matmul(bank, lhsT=x2T[:, kk, :], rhs=wg[:, kk, cs], start=(kk == 0), stop=(kk == 7))
                        k.pe_end(ins, reads=[x2T_b, pw_b], writes=[bank_b])
                        gbanks.append((bank, bank_b))
                        bank, bank_b = k.bank()
                        k.pe_begin(reads=[pT_b, pw_b], writes=[bank_b])
                        for kk in range(2):
                            ins = nc.tensor.matmul(bank, lhsT=pT[:, kk, :], rhs=wp[:, kk, cs], start=(kk == 0), stop=(kk == 1))
                        k.pe_end(ins, reads=[pT_b, pw_b], writes=[bank_b])
                        ebanks.append((bank, bank_b))
                    c["gbanks"], c["ebanks"] = gbanks, ebanks

                def A5(s):
                    c = tctx[s]
                    tg, tg_b = tg_ring.next()
                    eS, eS_b = eS_ring.next()
                    for half in range(2):
                        cs = slice(half * 512, (half + 1) * 512)
                        bank, bank_b = c["gbanks"][half]
                        k.op("act", lambda e: e.activation(out=tg[:, cs], in_=bank, func=AF.Tanh, scale=0.5),
                             reads=[bank_b], writes=[tg_b])
                        bank, bank_b = c["ebanks"][half]
                        k.op("act", lambda e: e.activation(out=eS[:, cs], in_=bank, func=AF.Copy),
                             reads=[bank_b], writes=[eS_b])
                    c["tg"], c["tg_b"], c["eS"], c["eS_b"] = tg, tg_b, eS, eS_b

                def A6(s):
                    c = tctx[s]
                    tg, tg_b, eS, eS_b = c["tg"], c["tg_b"], c["eS"], c["eS_b"]
                    k.op("dve", lambda e: e.scalar_tensor_tensor(out=tg[:], in0=tg[:], scalar=1.0, in1=eS[:],
                                                                 op0=ALU.add, op1=ALU.mult),
                         reads=[eS_b, tg_b], writes=[tg_b])

                def A7(s):
                    c = tctx[s]
                    x2, x2_b, ss, ss_b = x2s[s]
                    tg, tg_b = c["tg"], c["tg_b"]
                    k.op("act", lambda e: e.activation(out=junk[:], in_=tg[:], func=AF.Square, accum_out=ss[:, 5:6]),
                         reads=[tg_b], writes=[ss_b])

                def A8(s):
                    x2, x2_b, ss, ss_b = x2s[s]
                    rstd_ops(ss[:, 5:6], ss[:, 6:7], ss[:, 7:8], ss_b, 0.25 / D)

                def A9(s):
                    n = 4 * b + s
                    c = tctx[s]
                    x2, x2_b, ss, ss_b = x2s[s]
                    tg, tg_b = c["tg"], c["tg_b"]
                    k.op("dve", lambda e: e.scalar_tensor_tensor(out=tg[:], in0=tg[:], scalar=ss[:, 7:8], in1=wple[:],
                                                                 op0=ALU.mult, op1=ALU.mult),
                         reads=[tg_b, ss_b, p3_b], writes=[tg_b])
                    xo, xo_b = xo_ring.next()
                    k.op("dve", lambda e: e.scalar_tensor_tensor(out=xo[:], in0=tg[:], scalar=0.5, in1=x2[:],
                                                                 op0=ALU.mult, op1=ALU.add),
                         reads=[tg_b, x2_b], writes=[xo_b])
                    k.dma("pool", out_d[n * 128:(n + 1) * 128, :], xo[:], reads=[xo_b], owner=xo_b)

                def U(fn, s):
                    return lambda: fn(s)

                A1(0)
                if DEFER_MODE == 3:
                    units = []
                    for j in range(8):
                        u = []
                        s = j - 3
                        if 0 <= s < 4:
                            u += [U(A8, s), U(A9, s)]
                        s = j - 2
                        if 0 <= s < 4:
                            u += [U(A6, s), U(A7, s)]
                        s = j - 1
                        if 0 <= s < 4:
                            u += [U(A4, s), U(A5, s)]
                        s = j
                        if 0 <= s < 4:
                            u += [U(A2, s), U(A3, s)]
                        if 0 <= j + 1 < 4:
                            u += [U(A1, j + 1)]
                        units.append(u)
                    return units
                units = []
                for s in range(4):
                    if DEFER_MODE == 2:
                        units.append([U(A2, s), U(A3, s)])
                        units.append([U(A4, s), U(A5, s)])
                        units.append([U(A6, s)])
                        units.append([U(A7, s)])
                        units.append([U(A8, s)])
                        units.append([])
                        units.append([])
                    else:
                        for fn in (A2, A3, A4, A5, A6, A7, A8):
                            units.append([U(fn, s)])
                    if s < 3:
                        units.append([U(A9, s), U(A1, s + 1)])
                    else:
                        units.append([U(A9, s)])
                return units

            wug_prefetch()
            h2_next = load_h2(0)
            for b in range(NB):
                h2, h2_b = h2_next
                for g in range(8):
                    wug, wug_b = wug_pend.popleft()
                    for j in range(4):
                        c = 4 * g + j
                        for seg in range(2):
                            res = []
                            for gv in range(2):
                                bank, bank_b = k.bank()
                                k.pe_begin(reads=[wug_b, h2_b], writes=[bank_b])
                                ins = None
                                for kk in range(8):
                                    ins = nc.tensor.matmul(bank[:, 0:258], lhsT=wug[:, gv, kk, j * 128:(j + 1) * 128],
                                                           rhs=h2[:, kk, seg * 256:seg * 256 + 258], start=(kk == 0), stop=(kk == 7))
                                k.pe_end(ins, reads=[wug_b, h2_b], writes=[bank_b])
                                res.append((bank, bank_b))
                            yg, yg_b = yg_ring.next()
                            yv, yv_b = yv_ring.next()
                            conv3(res[0][0], res[0][1], c, yg, yg_b)
                            conv3p(res[1][0], res[1][1], 32 + c, yv, yv_b)
                            gl, gl_b = gl_ring.next()
                            k.op("act", lambda e: e.activation(out=gl[:], in_=yg[:], func=AF.Gelu_apprx_tanh),
                                 reads=[yg_b], writes=[gl_b])
                            k.op("pool", lambda e: e.tensor_tensor(out=gT[:, c, seg * 256:(seg + 1) * 256], in0=gl[:], in1=yv[:],
                                                                   op=ALU.mult), reads=[gl_b, yv_b], writes=[gT_bs[c // 4]])
                        if DEFER_MODE in (0, 2):
                            emit_deferred()
                    if DEFER_MODE == 1:
                        for _ in range(4):
                            emit_deferred()
                    if DEFER_MODE == 3:
                        emit_deferred()
                    wug_prefetch()
                    if g == 5:
                        wd_prefetch()
                    if b == 0 and g == 1:
                        for kk in range(8):
                            k.dma("pool", wg[:, kk, :], w_pg_d[kk * 128:(kk + 1) * 128, :], writes=[pw_b], owner=pw_b)
                        for kk in range(2):
                            k.dma("pool", wp[:, kk, :], w_pp_d[kk * 128:(kk + 1) * 128, :], writes=[pw_b], owner=pw_b)
                while deferred:
                    emit_deferred()
                if b + 1 < NB:
                    h2_next = load_h2(b + 1)
                dbanks = [[k.bank() for half in range(2)] for s in range(4)]
                allb = [dbanks[s][half][1] for s in range(4) for half in range(2)]
                for piece in range(8):
                    wd_prefetch()
                    wd, wd_b = wd_pend.popleft()
                    k.pe_begin(reads=[wd_b, gT_bs[piece]], writes=allb)
                    ins = None
                    for s in range(4):
                        for half in range(2):
                            for i in range(4):
                                ins = nc.tensor.matmul(dbanks[s][half][0], lhsT=gT[:, piece * 4 + i, s * 128:(s + 1) * 128],
                                                       rhs=wd[:, i, half * 512:(half + 1) * 512],
                                                       start=(piece == 0 and i == 0), stop=(piece == 7 and i == 3),
                                                       skip_group_check=True)
                    k.pe_end(ins, reads=[wd_b, gT_bs[piece]], writes=allb)
                    if piece < 6:
                        wd_prefetch()
                x2s = []
                pts = []
                x1s = []
                for s in range(4):
                    n = 4 * b + s
                    x1, x1_b = x1_ring.next()
                    k.dma("sp", x1[:], s_x1[n], writes=[x1_b], owner=x1_b)
                    pt, pt_b = pt_ring.next()
                    k.dma("sp", pt[:], p_d[n * 128:(n + 1) * 128, :], writes=[pt_b], owner=pt_b)
                    x2, x2_b = x2_ring.next()
                    ss, ss_b = ss_ring.next()
                    x1s.append((x1, x1_b))
                    pts.append((pt, pt_b))
                    x2s.append((x2, x2_b, ss, ss_b))
                for s in range(4):
                    x2, x2_b, ss, ss_b = x2s[s]
                    for half in range(2):
                        bank, bank_b = dbanks[s][half]
                        k.op("act", lambda e: e.activation(out=junk[:, 0:512], in_=bank, func=AF.Square,
                                                           accum_out=ss[:, half:half + 1]), reads=[bank_b], writes=[ss_b])
                for s in range(4):
                    x2, x2_b, ss, ss_b = x2s[s]
                    k.op("pool", lambda e: e.tensor_tensor(out=ss[:, 2:3], in0=ss[:, 0:1], in1=ss[:, 1:2], op=ALU.add),
                         reads=[ss_b], writes=[ss_b])
                    rstd_ops(ss[:, 2:3], ss[:, 3:4], ss[:, 4:5], ss_b, 1.0 / D)
                for s in range(4):
                    x2, x2_b, ss, ss_b = x2s[s]
                    for half in range(2):
                        bank, bank_b = dbanks[s][half]
                        cs = slice(half * 512, (half + 1) * 512)
                        k.op("dve", lambda e: e.scalar_tensor_tensor(out=x2[:, cs], in0=bank, scalar=ss[:, 4:5], in1=wfpost[:, cs],
                                                                     op0=ALU.mult, op1=ALU.mult),
                             reads=[bank_b, ss_b, p3_b], writes=[x2_b])
                for s in range(4):
                    x2, x2_b, ss, ss_b = x2s[s]
                    x1, x1_b = x1s[s]
                    k.op("pool", lambda e: e.tensor_tensor(out=x2[:], in0=x2[:], in1=x1[:], op=ALU.add),
                         reads=[x2_b, x1_b], writes=[x2_b])
                for u in make_tail(b, x2s, pts):
                    if DEFER:
                        deferred.append(u)
                    else:
                        for fn in u:
                            fn()
            while deferred:
                emit_deferred()
            k.end_phase()
    return nc


def make_consts():
    j = np.arange(128)[:, None]
    i = np.arange(128)[None, :]
    tri = np.zeros((128, 4, 128), np.float32)
    tri[:, 0, :] = (j <= i)
    tri[:, 1, :] = (j >= i)
    tri[:, 2, :] = (j > i)
    tri[:, 3, :] = (j < i)
    sel = np.zeros((4, 4, 128), np.float32)
    for h in range(4):
        sel[h, h, :] = 1.0
    return {
        "tri": tri,
        "identb": np.eye(128, dtype=np.float32).astype(ml_dtypes.bfloat16),
        "identf": np.eye(128, dtype=np.float32),
        "sel": sel,
    }


def make_shared(inp):
    f = lambda a: np.ascontiguousarray(np.asarray(a, dtype=np.float32))
    sh = dict(make_consts())
    sh["w_in"] = f(inp["w_in"][0])
    sh["wd_aug"] = f(np.stack([
        np.concatenate([inp["w_gla_decay_f"][0], inp["b_gla_decay_f"][0][None, :]], axis=0),
        np.concatenate([inp["w_gla_decay_b"][0], inp["b_gla_decay_b"][0][None, :]], axis=0)], axis=0))
    sh["mlcw"] = f(np.asarray(inp["ml_conv_w"][0]).T.reshape(8, 128, 3).transpose(1, 0, 2))
    sh["mlcb"] = f(np.asarray(inp["ml_conv_b"][0]).reshape(8, 128).T)
    ib = np.asarray(inp["ml_igate_b"][0])
    fb = np.asarray(inp["ml_fgate_b"][0])
    sh["gbias"] = f(np.stack([ib[0:4], ib[4:8], fb[0:4], fb[4:8]], axis=1))
    sh["ffcw"] = f(np.asarray(inp["ffn_conv_w"][0]).T.reshape(64, 128, 3).transpose(1, 0, 2))
    sh["ffcb"] = f(np.asarray(inp["ffn_conv_b"][0]).reshape(64, 128).T)
    sh["norms"] = f(np.stack([inp["norm_mix_pre"][0], inp["norm_mix_post"][0], inp["gla_norm"][0], inp["ml_norm"][0],
                              inp["norm_ffn_pre"][0], inp["norm_ffn_post"][0], inp["norm_ple_post"][0]], axis=0))
    sh["w_out"] = f(inp["w_out"][0])
    sh["w_up"] = f(inp["w_up"][0])
    sh["w_down"] = f(inp["w_down"][0])
    sh["w_pg"] = f(inp["w_ple_gate"][0])
    sh["w_pp"] = f(inp["w_ple_proj"][0])
    return sh


def kernel(**inputs):
    x = np.asarray(inputs["x"], dtype=np.float32)
    p = np.asarray(inputs["p"], dtype=np.float32)
    B, S, _ = x.shape
    sh = make_shared(inputs)
    nc = build(S)
    in_maps = []
    for b in range(B):
        m = dict(sh)
        m["x"] = np.ascontiguousarray(x[b])
        m["p"] = np.ascontiguousarray(p[0, b])
        in_maps.append(m)
    res = run_bass_kernel_spmd(nc, in_maps, core_ids=list(range(B)))
    return np.stack([np.asarray(r["out"], dtype=np.float32) for r in res.results], axis=0)
```

```python
import math
import numpy as np
import ml_dtypes
from contextlib import ExitStack
import concourse.bass as bass
import concourse.mybir as mybir
from concourse.bass_utils import run_bass_kernel_spmd

F32 = mybir.dt.float32
BF16 = mybir.dt.bfloat16
AF = mybir.ActivationFunctionType
ALU = mybir.AluOpType

D = 1024
DIN = 8240
H = 4
DK = 128
DV = 256
DFF = 4096
PLE = 256
EPS = 1e-6
LN_QS = math.log(DK ** -0.5)
CONV_CH = 1024
DEFER = True
DEFER_MODE = 3


class Buf:
    __slots__ = ("name", "w", "r", "sem", "cnt")

    def __init__(self, name):
        self.name = name
        self.w = None
        self.r = {}
        self.sem = None
        self.cnt = 0


class KB:
    def __init__(self, nc, es):
        self.nc = nc
        self.es = es
        self.eng = {"pe": nc.tensor, "act": nc.scalar, "dve": nc.vector, "pool": nc.gpsimd, "sp": nc.sync}
        self.esem = {}
        for e in ("pe", "act", "dve", "pool"):
            self.esem[e] = es.enter_context(nc.semaphore("es_" + e))
        self.ecnt = {e: 0 for e in self.esem}
        self.waited = {e: {} for e in self.eng}
        self.free_sems = []
        self.all_dma = {}
        self.phase_bufs = []
        self.nsem = 0
        self.ps = es.enter_context(nc.psum_tensor("ps", [128, 8, 512], F32))
        self.pb = [Buf("bank%d" % i) for i in range(8)]
        self.bank_i = 0

    def buf(self, name):
        b = Buf(name)
        self.phase_bufs.append(b)
        return b

    def bank(self):
        i = self.bank_i
        self.bank_i = (i + 1) % 8
        return self.ps[:, i, :], self.pb[i]

    def _wait(self, e, tok):
        key, sem, val = tok
        if e == "pe" and key == "pe":
            return
        w = self.waited[e]
        if w.get(id(sem), 0) >= val:
            return
        self.eng[e].wait_ge(sem, val)
        w[id(sem)] = val

    def _deps(self, e, reads, writes):
        for b in reads:
            if b.w is not None:
                self._wait(e, b.w)
        for b in writes:
            if b.w is not None:
                self._wait(e, b.w)
            for t in b.r.values():
                self._wait(e, t)

    def _mark(self, tok, reads, writes):
        for b in reads:
            b.r[tok[0]] = tok
        for b in writes:
            b.w = tok
            b.r = {}

    def op(self, e, fn, reads=(), writes=()):
        self._deps(e, reads, writes)
        ins = fn(self.eng[e])
        self.ecnt[e] += 1
        ins.then_inc(self.esem[e], 1)
        tok = (e, self.esem[e], self.ecnt[e])
        self._mark(tok, reads, writes)
        return tok

    def pe_begin(self, reads=(), writes=()):
        self._deps("pe", reads, writes)

    def pe_end(self, ins, reads=(), writes=()):
        self.ecnt["pe"] += 1
        ins.then_inc(self.esem["pe"], 1)
        tok = ("pe", self.esem["pe"], self.ecnt["pe"])
        self._mark(tok, reads, writes)

    def dma(self, q, out, in_, reads=(), writes=(), owner=None, **kw):
        skip = ("d", id(owner.sem)) if owner.sem is not None else None
        for b in reads:
            if b.w is not None:
                self._wait(q, b.w)
        for b in writes:
            if b.w is not None and b.w[0] != skip:
                self._wait(q, b.w)
            for t in b.r.values():
                self._wait(q, t)
        if owner.sem is None:
            if self.free_sems:
                owner.sem, owner.cnt = self.free_sems.pop()
            else:
                self.nsem += 1
                owner.sem = self.es.enter_context(self.nc.semaphore("ds%d" % self.nsem))
                owner.cnt = 0
        ins = self.eng[q].dma_start(out=out, in_=in_, **kw)
        owner.cnt += 16
        ins.then_inc(owner.sem, 16)
        self.all_dma[id(owner.sem)] = (owner.sem, owner.cnt)
        tok = (("d", id(owner.sem)), owner.sem, owner.cnt)
        self._mark(tok, reads, writes)

    def barrier(self):
        for e in self.eng:
            for e2 in self.esem:
                if e2 != e and self.ecnt[e2] > 0:
                    self._wait(e, (e2 + "_b", self.esem[e2], self.ecnt[e2]))
            for sem, cnt in self.all_dma.values():
                self._wait(e, ("db", sem, cnt))

    def end_phase(self):
        self.barrier()
        for b in self.phase_bufs:
            if b.sem is not None:
                self.free_sems.append((b.sem, b.cnt))
                b.sem = None
        self.phase_bufs = []
        for b in self.pb:
            b.w = None
            b.r = {}


class Ring:
    def __init__(self, k, es, name, shape, dtype, n):
        self.t = [es.enter_context(k.nc.sbuf_tensor("sr_%s%d_%d" % (name, i, id(es)), shape, dtype)) for i in range(n)]
        self.b = [k.buf("%s%d" % (name, i)) for i in range(n)]
        self.i = 0
        self.n = n

    def next(self):
        i = self.i
        self.i = (i + 1) % self.n
        return self.t[i], self.b[i]


def build(S):
    NT = S // 128
    NB = S // 512
    nc = bass.Bass("TRN2", target_bir_lowering=False)

    def din(name, shape, dt=F32):
        return nc.dram_tensor(name, list(shape), dt, kind="ExternalInput").ap()

    def dscr(name, shape, dt):
        return nc.dram_tensor(name, list(shape), dt, kind="Internal").ap()

    x_d = din("x", [S, D])
    p_d = din("p", [S, PLE])
    w_in = din("w_in", [D, DIN])
    wd_aug = din("wd_aug", [2, 17, 512])
    mlcw_d = din("mlcw", [128, 8, 3])
    mlcb_d = din("mlcb", [128, 8])
    gbias_d = din("gbias", [4, 4])
    ffcw_d = din("ffcw", [128, 64, 3])
    ffcb_d = din("ffcb", [128, 64])
    norms_d = din("norms", [7, D])
    w_out_d = din("w_out", [D, D])
    w_up_d = din("w_up", [D, 2 * DFF])
    w_down_d = din("w_down", [DFF, D])
    w_pg_d = din("w_pg", [D, D])
    w_pp_d = din("w_pp", [PLE, D])
    tri_d = din("tri", [128, 4, 128])
    identb_d = din("identb", [128, 128], BF16)
    identf_d = din("identf", [128, 128])
    sel_d = din("sel", [4, 4, 128])
    out_d = nc.dram_tensor("out", [S, D], F32, kind="ExternalOutput").ap()

    s_hT = dscr("s_hT", [NB, 128, 8, 512], BF16)
    s_mqk = dscr("s_mqk", [1024, S], F32)
    s_gate = dscr("s_gate", [16, S], F32)
    s_qe = [dscr("s_qe%d" % d, [NT, 128, 512], BF16) for d in range(2)]
    s_ke = [dscr("s_ke%d" % d, [NT, 128, 512], BF16) for d in range(2)]
    s_kd = [dscr("s_kd%d" % d, [NT, 128, 512], BF16) for d in range(2)]
    s_v = dscr("s_v", [NT, 128, 1024], BF16)
    s_mv = dscr("s_mv", [NT, 128, 1024], BF16)
    s_GA = dscr("s_GA", [NT, 128, 1024], BF16)
    s_GB = dscr("s_GB", [NT, 128, 1024], BF16)
    s_mq = dscr("s_mq", [NT, 128, 4, 128], BF16)
    s_mk = dscr("s_mk", [NT, 128, 4, 128], BF16)
    s_mktm = dscr("s_mktm", [NT, 128, 512], BF16)
    s_obA = dscr("s_obA", [NT, 128, 1024], F32)
    s_obB = dscr("s_obB", [NT, 128, 1024], F32)
    s_x1 = dscr("s_x1", [NT, 128, 1024], F32)
    s_h2T = dscr("s_h2T", [128, 8, S + 2], BF16)
    s_wup = dscr("s_wup", [D, 2 * DFF], BF16)
    s_wdn = dscr("s_wdn", [DFF, D], BF16)

    with ExitStack() as ges:
        k = KB(nc, ges)

        def GT(name, shape, dt):
            return ges.enter_context(nc.sbuf_tensor("sb_" + name, shape, dt))

        tri = GT("tri", [128, 4, 128], F32)
        identb = GT("identb", [128, 128], BF16)
        identf = GT("identf", [128, 128], F32)
        sel = GT("sel", [4, 4, 128], F32)
        neghalf = GT("neghalf", [128, 8], F32)
        EB = GT("EB", [128, 2, NT, 4], F32)
        w_tm = GT("w_tm", [128, 2, NT, 4], F32)
        thr_tm = GT("thr_tm", [128, 2, NT, 4], F32)
        dec_bc = GT("dec_bc", [128, 2, 4, NT], F32)
        junk = GT("junk", [128, 1024], BF16)
        cst_b = Buf("cst")
        nh_b = Buf("neghalf")
        EB_b = Buf("EB")
        wtm_b = Buf("wtm")
        thr_b = Buf("thr")
        dec_b = Buf("dec")
        k.dma("sp", tri[:], tri_d[:, :, :], writes=[cst_b], owner=cst_b)
        k.dma("sp", identb[:], identb_d[:, :], writes=[cst_b], owner=cst_b)
        k.dma("sp", identf[:], identf_d[:, :], writes=[cst_b], owner=cst_b)
        k.dma("sp", sel[:], sel_d[:, :, :], writes=[cst_b], owner=cst_b)
        k.op("pool", lambda e: e.memset(neghalf[:], -0.5), writes=[nh_b])

        def rstd_ops(ss_ap, ms_ap, rs_ap, stb, scale, n=1):
            k.op("pool", lambda e: e.tensor_scalar(out=ms_ap, in0=ss_ap, scalar1=scale, scalar2=EPS,
                                                   op0=ALU.mult, op1=ALU.add), reads=[stb], writes=[stb])
            k.op("pool", lambda e: e.tensor_tensor(out=rs_ap, in0=ms_ap, in1=neghalf[:, 0:n], op=ALU.pow),
                 reads=[stb, nh_b], writes=[stb])

        def rstd_act(ss_ap, ms_ap, rs_ap, stb, scale, n=1):
            k.op("act", lambda e: e.activation(out=ms_ap, in_=ss_ap, func=AF.Ln, scale=scale, bias=EPS),
                 reads=[stb], writes=[stb])
            k.op("act", lambda e: e.activation(out=rs_ap, in_=ms_ap, func=AF.Exp, scale=-0.5),
                 reads=[stb], writes=[stb])

        def transpose8(src, src_b, dst_view, dst_b, evac_eng):
            bank, bank_b = k.bank()
            bankb = bank.bitcast(BF16)
            k.pe_begin(reads=[src_b, cst_b], writes=[bank_b])
            ins = None
            for kk in range(8):
                ins = nc.tensor.transpose(out=bankb[:, kk * 128:(kk + 1) * 128], in_=src[:, kk * 128:(kk + 1) * 128],
                                          identity=identb[:])
            k.pe_end(ins, reads=[src_b, cst_b], writes=[bank_b])
            srcv = bankb.rearrange("p (k t) -> p k t", k=8)
            if evac_eng == "act":
                k.op("act", lambda e: e.activation(out=dst_view, in_=srcv, func=AF.Copy), reads=[bank_b], writes=[dst_b])
            else:
                k.op("dve", lambda e: e.tensor_copy(out=dst_view, in_=srcv), reads=[bank_b], writes=[dst_b])

        with ExitStack() as es:
            def T(name, shape, dt):
                return es.enter_context(nc.sbuf_tensor("sb_" + name + "_%d" % id(es), shape, dt))

            wA = T("wA", [128, 8, 2096], BF16)
            wA_b = k.buf("wA")
            wpre = T("wpre", [128, D], F32)
            wdt = T("wdt", [17, 2, 512], F32)
            pa_b = k.buf("pa_c")
            k.dma("sp", wpre[:], norms_d[0:1, :].partition_broadcast(128), writes=[pa_b], owner=pa_b)
            for d in range(2):
                k.dma("sp", wdt[:, d, :], wd_aug[d, :, :], writes=[pa_b], owner=pa_b)
            for kk in range(8):
                rows = slice(kk * 128, (kk + 1) * 128)
                k.dma("pool", wA[:, kk, 0:1024], w_in[rows, 0:1024], writes=[wA_b], owner=wA_b)
                k.dma("pool", wA[:, kk, 1024:2080], w_in[rows, 3072:4128], writes=[wA_b], owner=wA_b)
                k.dma("pool", wA[:, kk, 2080:2096], w_in[rows, 6176:6192], writes=[wA_b], owner=wA_b)
            x_ring = Ring(k, es, "xa", [128, D], F32, 4)
            st_ring = Ring(k, es, "sta", [128, 4], F32, 4)
            hb_ring = Ring(k, es, "hba", [128, D], BF16, 2)
            hT_ring = Ring(k, es, "hTa", [128, 8, 512], BF16, 2)
            qraw_ring = Ring(k, es, "qraw", [128, 4, 512], F32, 2)
            kraw_ring = Ring(k, es, "kraw", [128, 4, 512], F32, 2)
            mstg_ring = Ring(k, es, "mstg", [128, 512], F32, 3)
            gstg_ring = Ring(k, es, "gstg", [16, 512], F32, 2)
            lrT_ring = [Ring(k, es, "lrT%d" % d, [17, 512], F32, 2) for d in range(2)]
            for d in range(2):
                for i in range(2):
                    t_, b_ = lrT_ring[d].t[i], lrT_ring[d].b[i]
                    k.op("pool", lambda e: e.memset(t_[:], 1.0), writes=[b_])
            sp_ring = [Ring(k, es, "sp%d" % d, [128, 512], F32, 5) for d in range(2)]
            eb_ring = [Ring(k, es, "eb%d" % d, [128, 512], F32, 2) for d in range(2)]
            enb_ring = [Ring(k, es, "enb%d" % d, [128, 512], F32, 2) for d in range(2)]
            ekd_ring = [Ring(k, es, "ekd%d" % d, [128, 512], F32, 2) for d in range(2)]
            qst_ring = Ring(k, es, "qst", [128, 512], BF16, 4)
            kst_ring = Ring(k, es, "kst", [128, 512], BF16, 4)
            kdst_ring = Ring(k, es, "kdst", [128, 512], BF16, 4)

            def make_hT(b):
                hT, hT_b = hT_ring.next()
                for s in range(4):
                    t = 4 * b + s
                    xs, xs_b = x_ring.next()
                    k.dma("sp", xs[:], x_d[t * 128:(t + 1) * 128, :], writes=[xs_b], owner=xs_b)
                    st, st_b = st_ring.next()
                    k.op("act", lambda e: e.activation(out=junk[:], in_=xs[:], func=AF.Square, accum_out=st[:, 0:1]),
                         reads=[xs_b], writes=[st_b])
                    rstd_ops(st[:, 0:1], st[:, 1:2], st[:, 2:3], st_b, 1.0 / D)
                    hb, hb_b = hb_ring.next()
                    k.op("dve", lambda e: e.scalar_tensor_tensor(out=hb[:], in0=xs[:], scalar=st[:, 2:3], in1=wpre[:],
                                                                 op0=ALU.mult, op1=ALU.mult),
                         reads=[xs_b, st_b, pa_b], writes=[hb_b])
                    transpose8(hb, hb_b, hT[:, :, s * 128:(s + 1) * 128], hT_b, "dve")
                k.dma("pool", s_hT[b], hT[:], reads=[hT_b], owner=hT_b)
                return hT, hT_b

            hT_next = make_hT(0)
            for b in range(NB):
                hT, hT_b = hT_next

                def fm_group(c0, M):
                    bank, bank_b = k.bank()
                    k.pe_begin(reads=[wA_b, hT_b], writes=[bank_b])
                    ins = None
                    for kk in range(8):
                        ins = nc.tensor.matmul(bank[0:M, :], lhsT=wA[:, kk, c0:c0 + M], rhs=hT[:, kk, :],
                                               start=(kk == 0), stop=(kk == 7))
                    k.pe_end(ins, reads=[wA_b, hT_b], writes=[bank_b])
                    return bank, bank_b

                lrT = []
                for d in range(2):
                    bank, bank_b = fm_group(1024 + d * 16, 16)
                    lt, lt_b = lrT_ring[d].next()
                    k.op("dve", lambda e: e.tensor_copy(out=lt[0:16, :], in_=bank[0:16, :]), reads=[bank_b], writes=[lt_b])
                    lrT.append((lt, lt_b))
                spts = []
                for s in range(4):
                    ts = slice(s * 128, (s + 1) * 128)
                    row = []
                    for d in range(2):
                        lt, lt_b = lrT[d]
                        bank, bank_b = k.bank()
                        k.pe_begin(reads=[lt_b, pa_b], writes=[bank_b])
                        ins = nc.tensor.matmul(bank, lhsT=lt[0:17, ts], rhs=wdt[0:17, d, :], start=True, stop=True)
                        k.pe_end(ins, reads=[lt_b, pa_b], writes=[bank_b])
                        spt, spt_b = sp_ring[d].next()
                        k.op("act", lambda e: e.activation(out=spt[:], in_=bank, func=AF.Exp, scale=-1.0),
                             reads=[bank_b], writes=[spt_b])
                        k.op("act", lambda e: e.activation(out=spt[:], in_=spt[:], func=AF.Ln, bias=1.0),
                             reads=[spt_b], writes=[spt_b])
                        row.append((spt, spt_b))
                    spts.append(row)
                qraw, qraw_b = qraw_ring.next()
                kraw, kraw_b = kraw_ring.next()
                for h in range(4):
                    bank, bank_b = fm_group(h * 128, 128)
                    k.op("act", lambda e: e.activation(out=qraw[:, h, :], in_=bank, func=AF.Copy),
                         reads=[bank_b], writes=[qraw_b])
                for h in range(4):
                    bank, bank_b = fm_group(512 + h * 128, 128)
                    k.op("dve", lambda e: e.tensor_copy(out=kraw[:, h, :], in_=bank), reads=[bank_b], writes=[kraw_b])
                for c in range(8):
                    bank, bank_b = fm_group(1056 + c * 128, 128)
                    ms_, ms_b = mstg_ring.next()
                    if c % 2 == 0:
                        k.op("act", lambda e: e.activation(out=ms_[:], in_=bank, func=AF.Copy), reads=[bank_b], writes=[ms_b])
                    else:
                        k.op("dve", lambda e: e.tensor_copy(out=ms_[:], in_=bank), reads=[bank_b], writes=[ms_b])
                    k.dma("pool", s_mqk[c * 128:(c + 1) * 128, b * 512:(b + 1) * 512], ms_[:], reads=[ms_b], owner=ms_b)
                bank, bank_b = fm_group(2080, 16)
                gs_, gs_b = gstg_ring.next()
                k.op("dve", lambda e: e.tensor_copy(out=gs_[:], in_=bank[0:16, :]), reads=[bank_b], writes=[gs_b])
                k.dma("pool", s_gate[:, b * 512:(b + 1) * 512], gs_[:], reads=[gs_b], owner=gs_b)

                if b + 1 < NB:
                    hT_next = make_hT(b + 1)

                for s in range(4):
                    t = 4 * b + s
                    ts = slice(s * 128, (s + 1) * 128)
                    bk, bk_b = k.bank()
                    k.pe_begin(reads=[wA_b, hT_b], writes=[bk_b])
                    ins = None
                    for kk in range(8):
                        ins = nc.tensor.matmul(bk, lhsT=hT[:, kk, ts], rhs=wA[:, kk, 512:1024], start=(kk == 0), stop=(kk == 7))
                    k.pe_end(ins, reads=[wA_b, hT_b], writes=[bk_b])
                    pb2 = []
                    for d in range(2):
                        spt, spt_b = spts[s][d]
                        bank2, bank2_b = k.bank()
                        k.pe_begin(reads=[spt_b, cst_b], writes=[bank2_b])
                        for h in range(4):
                            ins = nc.tensor.matmul(bank2[:, h * 128:(h + 1) * 128], lhsT=spt[:, h * 128:(h + 1) * 128],
                                                   rhs=tri[:, d, :], start=True, stop=True)
                        k.pe_end(ins, reads=[spt_b, cst_b], writes=[bank2_b])
                        bank3, bank3_b = k.bank()
                        k.pe_begin(reads=[spt_b, cst_b], writes=[bank3_b])
                        ins = nc.tensor.matmul(bank3, lhsT=tri[:, 2 + d, :], rhs=spt[:], start=True, stop=True)
                        k.pe_end(ins, reads=[spt_b, cst_b], writes=[bank3_b])
                        pb2.append((bank2, bank2_b, bank3, bank3_b))
                    for d in range(2):
                        bank2, bank2_b, bank3, bank3_b = pb2[d]
                        eb, eb_b = eb_ring[d].next()
                        enb, enb_b = enb_ring[d].next()
                        k.op("act", lambda e: e.activation(out=eb[:], in_=bank2, func=AF.Exp, scale=-1.0 / 16.0),
                             reads=[bank2_b], writes=[eb_b])
                        k.op("act", lambda e: e.activation(out=enb[:], in_=bank2, func=AF.Exp, scale=1.0 / 16.0),
                             reads=[bank2_b], writes=[enb_b])
                        ekd, ekd_b = ekd_ring[d].next()
                        k.op("act", lambda e: e.activation(out=ekd[:], in_=bank3, func=AF.Exp, scale=-1.0 / 16.0),
                             reads=[bank3_b], writes=[ekd_b])
                        col = 127 if d == 0 else 0
                        ebv = eb[:].rearrange("p (h t) -> p h t", h=4)
                        enbv = enb[:].rearrange("p (h t) -> p h t", h=4)
                        k.op("pool", lambda e: e.tensor_copy(out=EB[:, d, t, :], in_=ebv[:, :, col]),
                             reads=[eb_b], writes=[EB_b])
                        qst, qst_b = qst_ring.next()
                        k.op("dve", lambda e: e.scalar_tensor_tensor(
                            out=qst[:].rearrange("p (h t) -> p h t", h=4), in0=qraw[:, :, ts], scalar=DK ** -0.5,
                            in1=ebv, op0=ALU.mult, op1=ALU.mult), reads=[qraw_b, eb_b], writes=[qst_b])
                        k.dma("pool", s_qe[d][t], qst[:], reads=[qst_b], owner=qst_b)
                        kst, kst_b = kst_ring.next()
                        k.op("pool", lambda e: e.tensor_tensor(
                            out=kst[:].rearrange("p (h t) -> p h t", h=4), in0=kraw[:, :, ts], in1=enbv, op=ALU.mult),
                            reads=[kraw_b, enb_b], writes=[kst_b])
                        k.dma("pool", s_ke[d][t], kst[:], reads=[kst_b], owner=kst_b)
                        kdst, kdst_b = kdst_ring.next()
                        k.op("dve", lambda e: e.tensor_tensor(out=kdst[:], in0=bk, in1=ekd[:], op=ALU.mult),
                             reads=[bk_b, ekd_b], writes=[kdst_b])
                        k.dma("pool", s_kd[d][t], kdst[:], reads=[kdst_b], owner=kdst_b)
            k.end_phase()

        with ExitStack() as es:
            def T(name, shape, dt):
                return es.enter_context(nc.sbuf_tensor("sb_" + name + "_%d" % id(es), shape, dt))

            wB = T("wB", [128, 8, 6144], BF16)
            wB_b = k.buf("wB")
            gnorm = T("gnorm", [128, D], F32)
            mnorm = T("mnorm", [128, D], F32)
            pb_b = k.buf("pb_c")
            k.dma("sp", gnorm[:], norms_d[2:3, :].partition_broadcast(128), writes=[pb_b], owner=pb_b)
            k.dma("sp", mnorm[:], norms_d[3:4, :].partition_broadcast(128), writes=[pb_b], owner=pb_b)
            wB_bs = [k.buf("wB%d" % i) for i in range(3)]
            for i, (c0, s0) in enumerate(((0, 1024), (2048, 4128), (4096, 6192))):
                for kk in range(8):
                    rows = slice(kk * 128, (kk + 1) * 128)
                    k.dma("pool", wB[:, kk, c0:c0 + 2048], w_in[rows, s0:s0 + 2048], writes=[wB_bs[i]], owner=wB_bs[i])
            hT_ring = Ring(k, es, "hTb", [128, 8, 512], BF16, 2)
            vst_ring = Ring(k, es, "vst", [128, D], BF16, 2)
            mvst_ring = Ring(k, es, "mvst", [128, D], BF16, 2)
            gst_ring = Ring(k, es, "gst", [128, D], BF16, 3)
            t1_ring = Ring(k, es, "t1", [128, D], F32, 2)
            t2_ring = Ring(k, es, "t2", [128, D], F32, 2)

            CH = min(CONV_CH, S)
            NCH = S // CH
            TPC = CH // 128
            cw = T("cw", [128, 8, 3], F32)
            cb = T("cb", [128, 8], F32)
            cv_b = k.buf("cv_c")
            k.dma("sp", cw[:], mlcw_d[:, :, :], writes=[cv_b], owner=cv_b)
            k.dma("sp", cb[:], mlcb_d[:, :], writes=[cv_b], owner=cv_b)
            pad_ring = Ring(k, es, "pad", [128, CH + 2], F32, 2)
            y_ring = Ring(k, es, "ycv", [128, CH], F32, 2)
            qo_ring = Ring(k, es, "qo", [128, CH], BF16, 2)
            mkc = T("mkc", [128, 4, CH], BF16)
            mkc_b = k.buf("mkc")
            mkst_ring = Ring(k, es, "mkst", [128, 2, 512], BF16, 2)

            def conv_unit(u):
                tc, c = u // 8, u % 8
                rows = slice(c * 128, (c + 1) * 128)
                pd, pd_b = pad_ring.next()
                k.dma("sp", pd[:, 1:CH + 1], s_mqk[rows, tc * CH:(tc + 1) * CH], writes=[pd_b], owner=pd_b)
                if tc == 0:
                    k.op("pool", lambda e: e.memset(pd[:, 0:1], 0.0), writes=[pd_b])
                else:
                    k.dma("sp", pd[:, 0:1], s_mqk[rows, tc * CH - 1:tc * CH], writes=[pd_b], owner=pd_b, allow_slow_non_contiguous=True)
                if tc == NCH - 1:
                    k.op("pool", lambda e: e.memset(pd[:, CH + 1:CH + 2], 0.0), writes=[pd_b])
                else:
                    k.dma("sp", pd[:, CH + 1:CH + 2], s_mqk[rows, (tc + 1) * CH:(tc + 1) * CH + 1], writes=[pd_b], owner=pd_b, allow_slow_non_contiguous=True)
                y, y_b = y_ring.next()
                k.op("dve", lambda e: e.tensor_scalar(out=y[:], in0=pd[:, 1:CH + 1], scalar1=cw[:, c, 1:2], scalar2=cb[:, c:c + 1],
                                                      op0=ALU.mult, op1=ALU.add), reads=[pd_b, cv_b], writes=[y_b])
                k.op("dve", lambda e: e.scalar_tensor_tensor(out=y[:], in0=pd[:, 0:CH], scalar=cw[:, c, 0:1], in1=y[:],
                                                             op0=ALU.mult, op1=ALU.add), reads=[pd_b, cv_b, y_b], writes=[y_b])
                k.op("dve", lambda e: e.scalar_tensor_tensor(out=y[:], in0=pd[:, 2:CH + 2], scalar=cw[:, c, 2:3], in1=y[:],
                                                             op0=ALU.mult, op1=ALU.add), reads=[pd_b, cv_b, y_b], writes=[y_b])
                n0 = tc * TPC
                if c < 4:
                    qo, qo_b = qo_ring.next()
                    k.op("act", lambda e: e.activation(out=qo[:], in_=y[:], func=AF.Silu), reads=[y_b], writes=[qo_b])
                    k.dma("pool", s_mq[n0:n0 + TPC, :, c, :].rearrange("n p t -> p n t"),
                          qo[:].rearrange("p (n t) -> p n t", t=128), reads=[qo_b], owner=qo_b)
                else:
                    h = c - 4
                    k.op("act", lambda e: e.activation(out=mkc[:, h, :], in_=y[:], func=AF.Silu), reads=[y_b], writes=[mkc_b])
                    k.dma("pool", s_mk[n0:n0 + TPC, :, h, :].rearrange("n p t -> p n t"),
                          mkc[:, h, :].rearrange("p (n t) -> p n t", t=128), reads=[mkc_b], owner=mkc_b)
                if c == 7:
                    for n2 in range(0, TPC, 2):
                        nn = min(2, TPC - n2)
                        bank, bank_b = k.bank()
                        bankb = bank.bitcast(BF16)
                        k.pe_begin(reads=[mkc_b, cst_b], writes=[bank_b])
                        ins = None
                        for j in range(nn):
                            for h in range(4):
                                ins = nc.tensor.transpose(out=bankb[:, (j * 4 + h) * 128:(j * 4 + h + 1) * 128],
                                                          in_=mkc[:, h, (n2 + j) * 128:(n2 + j + 1) * 128], identity=identb[:])
                        k.pe_end(ins, reads=[mkc_b, cst_b], writes=[bank_b])
                        ms_, ms_b = mkst_ring.next()
                        k.op("act", lambda e: e.activation(out=ms_[:, 0:nn, :],
                                                           in_=bankb[:, 0:nn * 512].rearrange("p (j c) -> p j c", c=512), func=AF.Copy),
                             reads=[bank_b], writes=[ms_b])
                        k.dma("pool", s_mktm[n0 + n2:n0 + n2 + nn].rearrange("n p c -> p n c"), ms_[:, 0:nn, :],
                              reads=[ms_b], owner=ms_b)

            NUNIT = NCH * 8
            units_done = [0]

            for b in range(NB):
                hT, hT_b = hT_ring.next()
                k.dma("sp", hT[:], s_hT[b], writes=[hT_b], owner=hT_b)
                for s in range(4):
                    t = 4 * b + s
                    ts = slice(s * 128, (s + 1) * 128)
                    target = ((t + 1) * NUNIT + NT - 1) // NT
                    while units_done[0] < min(target, NUNIT):
                        conv_unit(units_done[0])
                        units_done[0] += 1

                    def tm_group(c0):
                        res = []
                        for half in range(2):
                            bank, bank_b = k.bank()
                            wbb = wB_bs[c0 // 2048]
                            k.pe_begin(reads=[wbb, hT_b], writes=[bank_b])
                            ins = None
                            for kk in range(8):
                                ins = nc.tensor.matmul(bank, lhsT=hT[:, kk, ts],
                                                       rhs=wB[:, kk, c0 + half * 512:c0 + (half + 1) * 512],
                                                       start=(kk == 0), stop=(kk == 7))
                            k.pe_end(ins, reads=[wbb, hT_b], writes=[bank_b])
                            res.append((bank, bank_b))
                        return res

                    def act_evac(banks, dst, dst_b, func, scale=1.0):
                        for half in range(2):
                            bank, bank_b = banks[half]
                            k.op("act", lambda e: e.activation(out=dst[:, half * 512:(half + 1) * 512], in_=bank,
                                                               func=func, scale=scale), reads=[bank_b], writes=[dst_b])

                    banks = tm_group(0)
                    vst, vst_b = vst_ring.next()
                    act_evac(banks, vst, vst_b, AF.Copy)
                    k.dma("pool", s_v[t], vst[:], reads=[vst_b], owner=vst_b)
                    banks = tm_group(2048)
                    mvst, mvst_b = mvst_ring.next()
                    for half in range(2):
                        bank, bank_b = banks[half]
                        k.op("dve", lambda e: e.tensor_copy(out=mvst[:, half * 512:(half + 1) * 512], in_=bank),
                             reads=[bank_b], writes=[mvst_b])
                    k.dma("pool", s_mv[t], mvst[:], reads=[mvst_b], owner=mvst_b)
                    banks = tm_group(1024)
                    t1, t1_b = t1_ring.next()
                    act_evac(banks, t1, t1_b, AF.Silu)
                    banks = tm_group(4096)
                    t2, t2_b = t2_ring.next()
                    act_evac(banks, t2, t2_b, AF.Tanh, 0.5)
                    k.op("dve", lambda e: e.scalar_tensor_tensor(out=t1[:], in0=t2[:], scalar=1.0, in1=t1[:],
                                                                 op0=ALU.add, op1=ALU.mult), reads=[t1_b, t2_b], writes=[t1_b])
                    gst, gst_b = gst_ring.next()
                    k.op("dve", lambda e: e.scalar_tensor_tensor(out=gst[:], in0=t1[:], scalar=0.5, in1=gnorm[:],
                                                                 op0=ALU.mult, op1=ALU.mult), reads=[t1_b, pb_b], writes=[gst_b])
                    k.dma("pool", s_GA[t], gst[:], reads=[gst_b], owner=gst_b)
                    banks = tm_group(3072)
                    t1, t1_b = t1_ring.next()
                    act_evac(banks, t1, t1_b, AF.Tanh, 0.5)
                    banks = tm_group(5120)
                    t2, t2_b = t2_ring.next()
                    act_evac(banks, t2, t2_b, AF.Tanh, 0.5)
                    k.op("pool", lambda e: e.tensor_scalar(out=t1[:], in0=t1[:], scalar1=1.0, scalar2=0.25,
                                                           op0=ALU.add, op1=ALU.mult), reads=[t1_b], writes=[t1_b])
                    k.op("dve", lambda e: e.scalar_tensor_tensor(out=t1[:], in0=t2[:], scalar=1.0, in1=t1[:],
                                                                 op0=ALU.add, op1=ALU.mult), reads=[t1_b, t2_b], writes=[t1_b])
                    gst, gst_b = gst_ring.next()
                    k.op("pool", lambda e: e.tensor_tensor(out=gst[:], in0=t1[:], in1=mnorm[:], op=ALU.mult),
                         reads=[t1_b, pb_b], writes=[gst_b])
                    k.dma("pool", s_GB[t], gst[:], reads=[gst_b], owner=gst_b)
            k.end_phase()

        with ExitStack() as es:
            def T(name, shape, dt):
                return es.enter_context(nc.sbuf_tensor("sb_" + name + "_%d" % id(es), shape, dt))

            wc_b = Buf("wcast")
            for kk in range(8):
                k.dma("pool", s_wup[kk * 128:(kk + 1) * 128, :], w_up_d[kk * 128:(kk + 1) * 128, :], owner=wc_b)
            for kk in range(32):
                k.dma("pool", s_wdn[kk * 128:(kk + 1) * 128, :], w_down_d[kk * 128:(kk + 1) * 128, :], owner=wc_b)
            G = T("G", [4, 4, S], F32)
            gb = T("gb", [4, 4], F32)
            ones4 = T("ones4", [4, S], F32)
            Ssp = [T("Ssp%d" % d, [4, S], F32) for d in range(2)]
            uu = [T("uu%d" % d, [4, S], F32) for d in range(2)]
            Mg = [T("Mg%d" % d, [4, S], F32) for d in range(2)]
            dec = [T("dec%d" % d, [4, NT], F32) for d in range(2)]
            Gk_b = [k.buf("Gk%d" % i) for i in range(4)]
            gb_b = k.buf("gb")
            on_b = k.buf("ones4")
            g_bd = [k.buf("gwork%d" % d) for d in range(2)]
            for i in range(4):
                k.dma("sp", G[:, i, :], s_gate[4 * i:4 * i + 4, :], writes=[Gk_b[i]], owner=Gk_b[i])
            k.dma("sp", gb[:], gbias_d[:, :], writes=[gb_b], owner=gb_b)
            k.op("pool", lambda e: e.memset(ones4[:], 1.0), writes=[on_b])

            def rvd(d, ap):
                return ap[:, ::-1] if d == 1 else ap

            ngb = T("ngb", [4, 4], F32)
            ngb_b = k.buf("ngb")
            k.op("dve", lambda e: e.tensor_scalar(out=ngb[:], in0=gb[:], scalar1=-1.0, scalar2=None, op0=ALU.mult),
                 reads=[gb_b], writes=[ngb_b])
            for d in range(2):
                fpre = G[:, 2 + d, :]
                k.op("act", lambda e: e.activation(out=fpre, in_=fpre, func=AF.Exp, scale=-1.0, bias=ngb[:, 2 + d:3 + d]),
                     reads=[Gk_b[2 + d], ngb_b], writes=[Gk_b[2 + d]])
            for d in range(2):
                fpre = G[:, 2 + d, :]
                k.op("act", lambda e: e.activation(out=fpre, in_=fpre, func=AF.Ln, bias=1.0),
                     reads=[Gk_b[2 + d]], writes=[Gk_b[2 + d]])
            for d in range(2):
                fpre = G[:, 2 + d, :]
                k.op("dve", lambda e: e.tensor_tensor_scan(out=rvd(d, Ssp[d][:]), data0=rvd(d, ones4[:]), data1=rvd(d, fpre),
                                                           initial=0.0, op0=ALU.mult, op1=ALU.add),
                     reads=[Gk_b[2 + d], on_b], writes=[g_bd[d]])
            for d in range(2):
                ipre = G[:, d, :]
                k.op("dve", lambda e: e.scalar_tensor_tensor(out=uu[d][:], in0=ipre, scalar=gb[:, d:d + 1], in1=Ssp[d][:],
                                                             op0=ALU.add, op1=ALU.add),
                     reads=[Gk_b[d], gb_b, g_bd[d]], writes=[g_bd[d]])
            for d in range(2):
                k.op("dve", lambda e: e.tensor_tensor_scan(out=rvd(d, Mg[d][:]), data0=rvd(d, ones4[:]), data1=rvd(d, uu[d][:]),
                                                           initial=-1e30, op0=ALU.mult, op1=ALU.max),
                     reads=[g_bd[d], on_b], writes=[g_bd[d]])
            views = []
            for d in range(2):
                endc = 127 if d == 0 else 0
                Mgv = Mg[d][:].rearrange("p (n t) -> p n t", t=128)
                Mnb = Mgv[:, :, endc:endc + 1].to_broadcast([4, NT, 128])
                uv = uu[d][:].rearrange("p (n t) -> p n t", t=128)
                sv = Ssp[d][:].rearrange("p (n t) -> p n t", t=128)
                views.append((Mgv, Mnb, uv, sv, endc))
            for d in range(2):
                Mgv, Mnb, uv, sv, endc = views[d]
                k.op("dve", lambda e: e.tensor_tensor(out=uv, in0=uv, in1=Mnb, op=ALU.subtract), reads=[g_bd[d]], writes=[g_bd[d]])
            for d in range(2):
                k.op("act", lambda e: e.activation(out=uu[d][:], in_=uu[d][:], func=AF.Exp, bias=LN_QS),
                     reads=[g_bd[d]], writes=[g_bd[d]])
            for d in range(2):
                Mgv, Mnb, uv, sv, endc = views[d]
                k.op("dve", lambda e: e.tensor_tensor(out=sv, in0=sv, in1=Mnb, op=ALU.subtract), reads=[g_bd[d]], writes=[g_bd[d]])
            for d in range(2):
                k.op("act", lambda e: e.activation(out=Ssp[d][:], in_=Ssp[d][:], func=AF.Exp), reads=[g_bd[d]], writes=[g_bd[d]])
            for d in range(2):
                Mgv, Mnb, uv, sv, endc = views[d]
                k.op("dve", lambda e: e.memset(dec[d][:], 0.0), reads=[g_bd[d]], writes=[g_bd[d]])
                if NT > 1:
                    Mn2 = Mgv[:, :, endc]
                    if d == 0:
                        k.op("dve", lambda e: e.tensor_tensor(out=dec[d][:, 1:NT], in0=Mn2[:, 0:NT - 1], in1=Mn2[:, 1:NT],
                                                              op=ALU.subtract), reads=[g_bd[d]], writes=[g_bd[d]])
                        k.op("act", lambda e: e.activation(out=dec[d][:, 1:NT], in_=dec[d][:, 1:NT], func=AF.Exp),
                             reads=[g_bd[d]], writes=[g_bd[d]])
                    else:
                        k.op("dve", lambda e: e.tensor_tensor(out=dec[d][:, 0:NT - 1], in0=Mn2[:, 1:NT], in1=Mn2[:, 0:NT - 1],
                                                              op=ALU.subtract), reads=[g_bd[d]], writes=[g_bd[d]])
                        k.op("act", lambda e: e.activation(out=dec[d][:, 0:NT - 1], in_=dec[d][:, 0:NT - 1], func=AF.Exp),
                             reads=[g_bd[d]], writes=[g_bd[d]])
            for d in range(2):
                for src, dst, dst_b in ((uu[d], w_tm, wtm_b), (Ssp[d], thr_tm, thr_b)):
                    bank, bank_b = k.bank()
                    k.pe_begin(reads=[g_bd[d], cst_b], writes=[bank_b])
                    ins = None
                    for n in range(NT):
                        ins = nc.tensor.matmul(bank[:, n * 4:(n + 1) * 4], lhsT=src[:, n * 128:(n + 1) * 128],
                                               rhs=identf[0:4, 0:4], start=True, stop=True)
                    k.pe_end(ins, reads=[g_bd[d], cst_b], writes=[bank_b])
                    k.op("dve", lambda e: e.tensor_copy(out=dst[:, d, :, :],
                                                        in_=bank[:, 0:NT * 4].rearrange("p (n h) -> p n h", h=4)),
                         reads=[bank_b], writes=[dst_b])
                bank, bank_b = k.bank()
                k.pe_begin(reads=[g_bd[d], cst_b], writes=[bank_b])
                for h in range(4):
                    ins = nc.tensor.matmul(bank[:, h * NT:(h + 1) * NT], lhsT=sel[:, h, :], rhs=dec[d][:], start=True, stop=True)
                k.pe_end(ins, reads=[g_bd[d], cst_b], writes=[bank_b])
                k.op("dve", lambda e: e.tensor_copy(out=dec_bc[:, d, :, :],
                                                    in_=bank[:, 0:4 * NT].rearrange("p (h n) -> p h n", h=4)),
                     reads=[bank_b], writes=[dec_b])

            k.end_phase()

        with ExitStack() as es:
            def T(name, shape, dt):
                return es.enter_context(nc.sbuf_tensor("sb_" + name + "_%d" % id(es), shape, dt))

            Sst = T("Sst", [128, 4, 256], F32)
            Sbf = [T("Sbf%d" % i, [128, 4, 256], BF16) for i in range(2)]
            Cst = T("Cst", [128, 4, 257], F32)
            Cbf = [T("Cbf%d" % i, [128, 4, 257], BF16) for i in range(2)]
            S_b = [k.buf("S%d" % h) for h in range(4)]
            Sbf_b = [k.buf("Sbf%d" % i) for i in range(2)]
            C_b = [k.buf("C%d" % h) for h in range(4)]
            Cbf_b = [k.buf("Cbf%d" % i) for i in range(2)]
            NL = 3
            ld = {}
            for nm, shp in (("qe", [128, 512]), ("ke", [128, 512]), ("kd", [128, 512]), ("v", [128, 1024]),
                            ("mq", [128, 512]), ("mk", [128, 512]), ("mktm", [128, 512]), ("mv", [128, 1024])):
                ld[nm] = [T("ld_%s%d" % (nm, i), shp, BF16) for i in range(NL)]
            ld_b = [k.buf("ld%d" % i) for i in range(NL)]
            ld_i = [0]
            vw_ring = Ring(k, es, "vw", [128, 4, 257], BF16, 3)
            at_ring = Ring(k, es, "at", [128, 512], BF16, 4)
            qd_ring = Ring(k, es, "qd", [128, 128], BF16, 3)
            dn_ring = Ring(k, es, "dn", [128, 16], F32, 4)
            oA_ring = Ring(k, es, "oA", [128, D], F32, 2)
            hB_ring = Ring(k, es, "hB", [128, D], F32, 2)
            obA_t = [k.buf("obA_t%d" % n) for n in range(NT)]
            obB_t = [k.buf("obB_t%d" % n) for n in range(NT)]
            for d in (1, 0):
                order = list(range(NT)) if d == 0 else list(range(NT - 1, -1, -1))
                maskb = tri[:, d:d + 1, :].to_broadcast([128, 4, 128])
                first = True
                def issue_loads(si2):
                    n2 = order[si2]
                    li = ld_i[0]
                    ld_i[0] = (li + 1) % NL
                    lb2 = ld_b[li]
                    L2 = {nm: ld[nm][li] for nm in ld}
                    k.dma("sp", L2["qe"][:], s_qe[d][n2], writes=[lb2], owner=lb2)
                    k.dma("sp", L2["ke"][:], s_ke[d][n2], writes=[lb2], owner=lb2)
                    k.dma("sp", L2["kd"][:], s_kd[d][n2], writes=[lb2], owner=lb2)
                    k.dma("sp", L2["v"][:], s_v[n2], writes=[lb2], owner=lb2)
                    k.dma("sp", L2["mq"][:], s_mq[n2].rearrange("p h t -> p (h t)"), writes=[lb2], owner=lb2)
                    k.dma("sp", L2["mk"][:], s_mk[n2].rearrange("p h t -> p (h t)"), writes=[lb2], owner=lb2)
                    k.dma("sp", L2["mktm"][:], s_mktm[n2], writes=[lb2], owner=lb2)
                    k.dma("sp", L2["mv"][:], s_mv[n2], writes=[lb2], owner=lb2)
                    vw2, vw2_b = vw_ring.next()
                    k.op("pool", lambda e: e.tensor_tensor(
                        out=vw2[:, :, 0:256], in0=L2["mv"][:].rearrange("p (h c) -> p h c", h=4),
                        in1=w_tm[:, d, n2, :].unsqueeze(2).to_broadcast([128, 4, 256]), op=ALU.mult),
                        reads=[lb2, wtm_b], writes=[vw2_b])
                    k.op("pool", lambda e: e.tensor_copy(out=vw2[:, :, 256], in_=w_tm[:, d, n2, :]), reads=[wtm_b], writes=[vw2_b])
                    return (lb2, L2, vw2, vw2_b)

                pending = issue_loads(0)
                for si, n in enumerate(order):
                    nxt = order[si + 1] if si + 1 < NT else None
                    cur = si % 2
                    prv = 1 - cur
                    lb, L, vw, vw_b = pending
                    if nxt is not None:
                        pending = issue_loads(si + 1)
                    oA, oA_b = oA_ring.next()
                    hB, hB_b = hB_ring.next()
                    s1, s1_b = k.bank()
                    k.pe_begin(reads=[lb], writes=[s1_b])
                    for h in range(4):
                        hk = slice(h * 128, (h + 1) * 128)
                        ins = nc.tensor.matmul(s1[:, hk], lhsT=L["ke"][:, hk], rhs=L["qe"][:, hk], start=True, stop=True)
                    k.pe_end(ins, reads=[lb], writes=[s1_b])
                    s2, s2_b = k.bank()
                    k.pe_begin(reads=[lb], writes=[s2_b])
                    for h in range(4):
                        hk = slice(h * 128, (h + 1) * 128)
                        ins = nc.tensor.matmul(s2[:, hk], lhsT=L["mk"][:, hk], rhs=L["mq"][:, hk], start=True, stop=True)
                    k.pe_end(ins, reads=[lb], writes=[s2_b])
                    ub = []
                    for g2 in range(2):
                        bu, bu_b = k.bank()
                        k.pe_begin(reads=[lb], writes=[bu_b])
                        for hh in range(2):
                            h = 2 * g2 + hh
                            ins = nc.tensor.matmul(bu[:, hh * 256:(hh + 1) * 256], lhsT=L["kd"][:, h * 128:(h + 1) * 128],
                                                   rhs=L["v"][:, h * 256:(h + 1) * 256], start=True, stop=True)
                        k.pe_end(ins, reads=[lb], writes=[bu_b])
                        ub.append((bu, bu_b))
                    u2 = []
                    for h in range(4):
                        bu2, bu2_b = k.bank()
                        k.pe_begin(reads=[lb, vw_b], writes=[bu2_b])
                        ins = nc.tensor.matmul(bu2[:, 0:257], lhsT=L["mktm"][:, h * 128:(h + 1) * 128], rhs=vw[:, h, :],
                                               start=True, stop=True)
                        k.pe_end(ins, reads=[lb, vw_b], writes=[bu2_b])
                        u2.append((bu2, bu2_b))
                    at1, at1_b = at_ring.next()
                    k.op("dve", lambda e: e.tensor_tensor(out=at1[:].rearrange("p (h t) -> p h t", h=4),
                                                          in0=s1.rearrange("p (h t) -> p h t", h=4), in1=maskb, op=ALU.mult),
                         reads=[s1_b, cst_b], writes=[at1_b])
                    at2, at2_b = at_ring.next()
                    k.op("dve", lambda e: e.tensor_tensor(out=at2[:].rearrange("p (h t) -> p h t", h=4),
                                                          in0=s2.rearrange("p (h t) -> p h t", h=4), in1=maskb, op=ALU.mult),
                         reads=[s2_b, cst_b], writes=[at2_b])
                    for h in range(4):
                        bu, bu_b = ub[h // 2]
                        src = bu[:, (h % 2) * 256:(h % 2 + 1) * 256]
                        if first:
                            k.op("dve", lambda e: e.tensor_copy(out=Sst[:, h, :], in_=src), reads=[bu_b], writes=[S_b[h]])
                        else:
                            k.op("dve", lambda e: e.scalar_tensor_tensor(out=Sst[:, h, :], in0=Sst[:, h, :],
                                                                         scalar=EB[:, d, n, h:h + 1], in1=src,
                                                                         op0=ALU.mult, op1=ALU.add),
                                 reads=[bu_b, S_b[h], EB_b], writes=[S_b[h]])
                    for h in range(4):
                        bu2, bu2_b = u2[h]
                        if first:
                            k.op("dve", lambda e: e.tensor_copy(out=Cst[:, h, :], in_=bu2[:, 0:257]),
                                 reads=[bu2_b], writes=[C_b[h]])
                        else:
                            k.op("dve", lambda e: e.scalar_tensor_tensor(out=Cst[:, h, :], in0=Cst[:, h, :],
                                                                         scalar=dec_bc[:, d, h, n:n + 1], in1=bu2[:, 0:257],
                                                                         op0=ALU.mult, op1=ALU.add),
                                 reads=[bu2_b, C_b[h], dec_b], writes=[C_b[h]])
                    ob = []
                    for g2 in range(2):
                        bo, bo_b = k.bank()
                        rd = [at1_b, lb] + ([] if first else [Sbf_b[prv]])
                        k.pe_begin(reads=rd, writes=[bo_b])
                        for hh in range(2):
                            h = 2 * g2 + hh
                            hk = slice(h * 128, (h + 1) * 128)
                            dst = bo[:, hh * 256:(hh + 1) * 256]
                            ins = nc.tensor.matmul(dst, lhsT=at1[:, hk], rhs=L["v"][:, h * 256:(h + 1) * 256],
                                                   start=True, stop=first)
                            if not first:
                                ins = nc.tensor.matmul(dst, lhsT=L["qe"][:, hk], rhs=Sbf[prv][:, h, :], start=False, stop=True)
                        k.pe_end(ins, reads=rd, writes=[bo_b])
                        ob.append((bo, bo_b))
                    nb = []
                    for h in range(4):
                        hk = slice(h * 128, (h + 1) * 128)
                        bn, bn_b = k.bank()
                        rd = [at2_b, vw_b, lb] + ([] if first else [Cbf_b[prv]])
                        k.pe_begin(reads=rd, writes=[bn_b])
                        ins = nc.tensor.matmul(bn[:, 0:257], lhsT=at2[:, hk], rhs=vw[:, h, :], start=True, stop=first)
                        if not first:
                            ins = nc.tensor.matmul(bn[:, 0:257], lhsT=L["mq"][:, hk], rhs=Cbf[prv][:, h, :], start=False, stop=True)
                        k.pe_end(ins, reads=rd, writes=[bn_b])
                        nb.append((bn, bn_b))
                    if nxt is not None:
                        k.op("act", lambda e: e.activation(out=Sbf[cur][:], in_=Sst[:], func=AF.Copy), reads=S_b, writes=[Sbf_b[cur]])
                        k.op("pool", lambda e: e.tensor_tensor(
                            out=Cbf[cur][:], in0=Cst[:], in1=dec_bc[:, d, :, nxt:nxt + 1].to_broadcast([128, 4, 257]),
                            op=ALU.mult), reads=C_b + [dec_b], writes=[Cbf_b[cur]])
                    for g2 in range(2):
                        bo, bo_b = ob[g2]
                        k.op("act", lambda e: e.activation(out=oA[:, g2 * 512:(g2 + 1) * 512], in_=bo, func=AF.Copy),
                             reads=[bo_b], writes=[oA_b])
                    dn, dn_b = dn_ring.next()
                    for h in range(4):
                        bn, bn_b = nb[h]
                        k.op("act", lambda e: e.activation(out=dn[:, h:h + 1], in_=bn[:, 256:257], func=AF.Copy),
                             reads=[bn_b], writes=[dn_b])
                    k.op("dve", lambda e: e.scalar_tensor_tensor(out=dn[:, 4:8], in0=dn[:, 0:4], scalar=-1.0, in1=dn[:, 0:4],
                                                                 op0=ALU.mult, op1=ALU.max), reads=[dn_b], writes=[dn_b])
                    k.op("dve", lambda e: e.tensor_tensor(out=dn[:, 8:12], in0=dn[:, 4:8], in1=thr_tm[:, d, n, :], op=ALU.max),
                         reads=[dn_b, thr_b], writes=[dn_b])
                    k.op("dve", lambda e: e.reciprocal(out=dn[:, 12:16], in_=dn[:, 8:12]), reads=[dn_b], writes=[dn_b])
                    for h in range(4):
                        bn, bn_b = nb[h]
                        k.op("act", lambda e: e.activation(out=hB[:, h * 256:(h + 1) * 256], in_=bn[:, 0:256], func=AF.Copy,
                                                           scale=dn[:, 12 + h:13 + h]),
                             reads=[bn_b, dn_b], writes=[hB_b])
                    if d == 1:
                        k.dma("pool", s_obA[n], oA[:], reads=[oA_b], writes=[obA_t[n]], owner=oA_b)
                        k.dma("pool", s_obB[n], hB[:], reads=[hB_b], writes=[obB_t[n]], owner=hB_b)
                    else:
                        k.dma("pool", s_obA[n], oA[:], reads=[oA_b], writes=[obA_t[n]], owner=oA_b, accum_op=ALU.add)
                        k.dma("pool", s_obB[n], hB[:], reads=[hB_b], writes=[obB_t[n]], owner=hB_b, accum_op=ALU.add)
                    first = False
            k.end_phase()

        with ExitStack() as es:
            def T(name, shape, dt):
                return es.enter_context(nc.sbuf_tensor("sb_" + name + "_%d" % id(es), shape, dt))

            wout = T("wout", [128, 8, D], BF16)
            wpost = T("wpost", [128, D], F32)
            wffn = T("wffn", [128, D], F32)
            zero2 = T("zero2", [128, 8, 1], BF16)
            p2_b = k.buf("p2_c")
            for kk in range(8):
                k.dma("pool", wout[:, kk, :], w_out_d[kk * 128:(kk + 1) * 128, :], writes=[p2_b], owner=p2_b)
            k.dma("sp", wpost[:], norms_d[1:2, :].partition_broadcast(128), writes=[p2_b], owner=p2_b)
            k.dma("sp", wffn[:], norms_d[4:5, :].partition_broadcast(128), writes=[p2_b], owner=p2_b)
            z_b = k.buf("zero2")
            k.op("pool", lambda e: e.memset(zero2[:], 0.0), writes=[z_b])
            k.dma("pool", s_h2T[:, :, 0:1], zero2[:], reads=[z_b], owner=z_b, allow_slow_non_contiguous=True)
            k.dma("pool", s_h2T[:, :, S + 1:S + 2], zero2[:], reads=[z_b], owner=z_b, allow_slow_non_contiguous=True)

            RD = 11
            A_ring = Ring(k, es, "cA", [128, D], F32, RD)
            B_ring = Ring(k, es, "cB", [128, D], F32, RD)
            GA_ring = Ring(k, es, "cGA", [128, D], BF16, 6)
            GB_ring = Ring(k, es, "cGB", [128, D], BF16, 6)
            X_ring = Ring(k, es, "cX", [128, D], F32, 4)
            ss_ring = Ring(k, es, "css", [128, 32], F32, RD)
            ybf_ring = Ring(k, es, "cybf", [128, D], BF16, 3)
            yT_ring = Ring(k, es, "cyT", [128, 8, 128], BF16, 3)
            h2_ring = Ring(k, es, "ch2", [128, D], BF16, 3)
            h2T_ring = Ring(k, es, "ch2T", [128, 8, 128], BF16, 3)
            ctxs = {}

            def f0(n):
                c = {"n": n}
                c["A"], c["A_b"] = A_ring.next()
                c["B"], c["B_b"] = B_ring.next()
                c["GA"], c["GA_b"] = GA_ring.next()
                c["GB"], c["GB_b"] = GB_ring.next()
                c["ss"], c["ss_b"] = ss_ring.next()
                k.dma("sp", c["A"][:], s_obA[n], writes=[c["A_b"]], owner=c["A_b"])
                k.dma("sp", c["B"][:], s_obB[n], writes=[c["B_b"]], owner=c["B_b"])
                k.dma("sp", c["GA"][:], s_GA[n], writes=[c["GA_b"]], owner=c["GA_b"])
                k.dma("sp", c["GB"][:], s_GB[n], writes=[c["GB_b"]], owner=c["GB_b"])
                ctxs[n] = c

            def f1(n):
                c = ctxs[n]
                A, A_b, B, B_b, ss, ss_b = c["A"], c["A_b"], c["B"], c["B_b"], c["ss"], c["ss_b"]
                for h in range(4):
                    k.op("act", lambda e: e.activation(out=junk[:, 0:256], in_=A[:, h * 256:(h + 1) * 256], func=AF.Square,
                                                       accum_out=ss[:, h:h + 1]), reads=[A_b], writes=[ss_b])
                for h in range(4):
                    k.op("act", lambda e: e.activation(out=junk[:, 0:256], in_=B[:, h * 256:(h + 1) * 256], func=AF.Square,
                                                       accum_out=ss[:, 4 + h:5 + h]), reads=[B_b], writes=[ss_b])

            def f2(n):
                c = ctxs[n]
                A, A_b, B, B_b, ss, ss_b = c["A"], c["A_b"], c["B"], c["B_b"], c["ss"], c["ss_b"]
                GA, GA_b, GBt, GB_b = c["GA"], c["GA_b"], c["GB"], c["GB_b"]
                rstd_act(ss[:, 0:8], ss[:, 8:16], ss[:, 16:24], ss_b, 1.0 / DV, n=8)
                k.op("pool", lambda e: e.tensor_tensor(out=A[:], in0=A[:], in1=GA[:], op=ALU.mult),
                     reads=[A_b, GA_b], writes=[A_b])
                k.op("pool", lambda e: e.tensor_tensor(out=B[:], in0=B[:], in1=GBt[:], op=ALU.mult),
                     reads=[B_b, GB_b], writes=[B_b])

            def f3(n):
                c = ctxs[n]
                A, A_b, B, B_b, ss, ss_b = c["A"], c["A_b"], c["B"], c["B_b"], c["ss"], c["ss_b"]
                c["ybf"], c["ybf_b"] = ybf_ring.next()
                ybf = c["ybf"]
                for h in range(4):
                    hs = slice(h * 256, (h + 1) * 256)
                    k.op("dve", lambda e: e.tensor_scalar(out=B[:, hs], in0=B[:, hs], scalar1=ss[:, 20 + h:21 + h],
                                                          scalar2=None, op0=ALU.mult),
                         reads=[B_b, ss_b], writes=[B_b])
                for h in range(4):
                    hs = slice(h * 256, (h + 1) * 256)
                    k.op("dve", lambda e: e.scalar_tensor_tensor(out=ybf[:, hs], in0=A[:, hs], scalar=ss[:, 16 + h:17 + h],
                                                                 in1=B[:, hs], op0=ALU.mult, op1=ALU.add),
                         reads=[A_b, B_b, ss_b], writes=[c["ybf_b"]])

            def f4(n):
                c = ctxs[n]
                c["yT"], c["yT_b"] = yT_ring.next()
                transpose8(c["ybf"], c["ybf_b"], c["yT"][:], c["yT_b"], "act")
                c["X"], c["X_b"] = X_ring.next()
                k.dma("sp", c["X"][:], x_d[n * 128:(n + 1) * 128, :], writes=[c["X_b"]], owner=c["X_b"])

            def f5(n):
                c = ctxs[n]
                ss, ss_b = c["ss"], c["ss_b"]
                yT, yT_b = c["yT"], c["yT_b"]
                banks = []
                for half in range(2):
                    bank, bank_b = k.bank()
                    k.pe_begin(reads=[yT_b, p2_b], writes=[bank_b])
                    ins = None
                    for kk in range(8):
                        ins = nc.tensor.matmul(bank, lhsT=yT[:, kk, :], rhs=wout[:, kk, half * 512:(half + 1) * 512],
                                               start=(kk == 0), stop=(kk == 7))
                    k.pe_end(ins, reads=[yT_b, p2_b], writes=[bank_b])
                    banks.append((bank, bank_b))
                c["banks"] = banks
                for half in range(2):
                    bank, bank_b = banks[half]
                    k.op("act", lambda e: e.activation(out=junk[:, 0:512], in_=bank, func=AF.Square,
                                                       accum_out=ss[:, 24 + half:25 + half]), reads=[bank_b], writes=[ss_b])

            def f6(n):
                c = ctxs[n]
                ss, ss_b = c["ss"], c["ss_b"]
                k.op("pool", lambda e: e.tensor_tensor(out=ss[:, 26:27], in0=ss[:, 24:25], in1=ss[:, 25:26], op=ALU.add),
                     reads=[ss_b], writes=[ss_b])
                rstd_act(ss[:, 26:27], ss[:, 27:28], ss[:, 28:29], ss_b, 1.0 / D)

            def f7(n):
                c = ctxs[n]
                A, A_b, B, B_b, ss, ss_b, X, X_b = c["A"], c["A_b"], c["B"], c["B_b"], c["ss"], c["ss_b"], c["X"], c["X_b"]
                for half in range(2):
                    bank, bank_b = c["banks"][half]
                    cs = slice(half * 512, (half + 1) * 512)
                    k.op("dve", lambda e: e.scalar_tensor_tensor(out=A[:, cs], in0=bank, scalar=ss[:, 28:29], in1=wpost[:, cs],
                                                                 op0=ALU.mult, op1=ALU.mult),
                         reads=[bank_b, ss_b, p2_b], writes=[A_b])
                k.op("pool", lambda e: e.tensor_tensor(out=B[:], in0=A[:], in1=X[:], op=ALU.add),
                     reads=[A_b, X_b], writes=[B_b])
                k.dma("pool", s_x1[n], B[:], reads=[B_b], owner=B_b)
                k.op("act", lambda e: e.activation(out=junk[:], in_=B[:], func=AF.Square, accum_out=ss[:, 29:30]),
                     reads=[B_b], writes=[ss_b])

            def f8(n):
                c = ctxs[n]
                ss, ss_b = c["ss"], c["ss_b"]
                rstd_act(ss[:, 29:30], ss[:, 30:31], ss[:, 31:32], ss_b, 1.0 / D)

            def f9(n):
                c = ctxs[n]
                B, B_b, ss, ss_b = c["B"], c["B_b"], c["ss"], c["ss_b"]
                h2, h2_b = h2_ring.next()
                k.op("dve", lambda e: e.scalar_tensor_tensor(out=h2[:], in0=B[:], scalar=ss[:, 31:32], in1=wffn[:],
                                                             op0=ALU.mult, op1=ALU.mult),
                     reads=[B_b, ss_b, p2_b], writes=[h2_b])
                h2T, h2T_b = h2T_ring.next()
                transpose8(h2, h2_b, h2T[:], h2T_b, "act")
                k.dma("pool", s_h2T[:, :, 1 + n * 128:1 + (n + 1) * 128], h2T[:], reads=[h2T_b], owner=h2T_b)
                del ctxs[n]

            stages = [f0, f1, f2, f3, f4, f5, f6, f7, f8, f9]
            NSTG = len(stages)
            GS = 1
            LAG = 1
            groups = [list(range(g0, min(NT, g0 + GS))) for g0 in range(0, NT, GS)]
            NG = len(groups)
            for tau in range((NG - 1) * LAG + NSTG):
                for g in range(NG):
                    st = tau - g * LAG
                    if 0 <= st < NSTG:
                        for t in groups[g]:
                            stages[st](t)
            k.end_phase()

        with ExitStack() as es:
            def T(name, shape, dt):
                return es.enter_context(nc.sbuf_tensor("sb_" + name + "_%d" % id(es), shape, dt))

            wg = T("wg", [128, 8, D], BF16)
            wp = T("wp", [128, 2, D], BF16)
            fcw = T("fcw", [128, 64, 3], F32)
            fcb = T("fcb", [128, 64], F32)
            wfpost = T("wfpost", [128, D], F32)
            wple = T("wple", [128, D], F32)
            p3_b = k.buf("p3_c")
            pw_b = k.buf("p3_w")
            k.dma("sp", fcw[:], ffcw_d[:, :, :], writes=[p3_b], owner=p3_b)
            k.dma("sp", fcb[:], ffcb_d[:, :], writes=[p3_b], owner=p3_b)
            k.dma("sp", wfpost[:], norms_d[5:6, :].partition_broadcast(128), writes=[p3_b], owner=p3_b)
            k.dma("sp", wple[:], norms_d[6:7, :].partition_broadcast(128), writes=[p3_b], owner=p3_b)
            h2_ring = Ring(k, es, "h2b", [128, 8, 514], BF16, 1)
            gT = T("gT", [128, 32, 512], BF16)
            gT_bs = [k.buf("gT%d" % i) for i in range(8)]
            wug_ring = Ring(k, es, "wug", [128, 2, 8, 512], BF16, 2)
            wd_ring = Ring(k, es, "wdn", [128, 4, D], BF16, 2)
            yg_ring = Ring(k, es, "yg", [128, 256], F32, 3)
            yv_ring = Ring(k, es, "yv", [128, 256], F32, 3)
            gl_ring = Ring(k, es, "gl", [128, 256], F32, 2)
            x1_ring = Ring(k, es, "x1f", [128, D], F32, 4)
            x2_ring = Ring(k, es, "x2f", [128, D], F32, 4)
            x2b_ring = Ring(k, es, "x2b", [128, D], BF16, 2)
            x2T_ring = Ring(k, es, "x2T", [128, 8, 128], BF16, 2)
            pt_ring = Ring(k, es, "ptf", [128, PLE], F32, 4)
            ptb_ring = Ring(k, es, "ptb", [128, PLE], BF16, 2)
            pT_ring = Ring(k, es, "pTf", [128, 2, 128], BF16, 2)
            tg_ring = Ring(k, es, "tgf", [128, D], F32, 2)
            ss_ring = Ring(k, es, "ssf", [128, 16], F32, 4)
            xo_ring = Ring(k, es, "xo", [128, D], F32, 1)

            def conv3(bank, bank_b, c, dst, dst_b):
                k.op("act", lambda e: e.activation(out=dst[:], in_=bank[:, 1:257], func=AF.Identity, scale=fcw[:, c, 1:2],
                                                   bias=fcb[:, c:c + 1]),
                     reads=[bank_b, p3_b], writes=[dst_b])
                k.op("dve", lambda e: e.scalar_tensor_tensor(out=dst[:], in0=bank[:, 0:256], scalar=fcw[:, c, 0:1], in1=dst[:],
                                                             op0=ALU.mult, op1=ALU.add), reads=[bank_b, p3_b, dst_b], writes=[dst_b])
                k.op("dve", lambda e: e.scalar_tensor_tensor(out=dst[:], in0=bank[:, 2:258], scalar=fcw[:, c, 2:3], in1=dst[:],
                                                             op0=ALU.mult, op1=ALU.add), reads=[bank_b, p3_b, dst_b], writes=[dst_b])

            print("phase3 sbuf remaining", nc.sbuf_bytes_remaining)
            raw_ring = Ring(k, es, "rawv", [128, 256], F32, 2)
            tp_ring = Ring(k, es, "tpv", [128, 256], F32, 2)

            def conv3p(bank, bank_b, c, dst, dst_b):
                k.op("act", lambda e: e.activation(out=dst[:], in_=bank[:, 1:257], func=AF.Identity, scale=fcw[:, c, 1:2],
                                                   bias=fcb[:, c:c + 1]),
                     reads=[bank_b, p3_b], writes=[dst_b])
                raw, raw_b = raw_ring.next()
                k.op("act", lambda e: e.activation(out=raw[:], in_=bank[:, 0:256], func=AF.Copy),
                     reads=[bank_b], writes=[raw_b])
                k.op("dve", lambda e: e.scalar_tensor_tensor(out=dst[:], in0=bank[:, 2:258], scalar=fcw[:, c, 2:3], in1=dst[:],
                                                             op0=ALU.mult, op1=ALU.add), reads=[bank_b, p3_b, dst_b], writes=[dst_b])
                tp, tp_b = tp_ring.next()
                k.op("pool", lambda e: e.tensor_scalar(out=tp[:], in0=raw[:], scalar1=fcw[:, c, 0:1], scalar2=0.0,
                                                       op0=ALU.mult, op1=ALU.add), reads=[raw_b, p3_b], writes=[tp_b])
                k.op("pool", lambda e: e.tensor_tensor(out=dst[:], in0=dst[:], in1=tp[:], op=ALU.add),
                     reads=[dst_b, tp_b], writes=[dst_b])

            eS_ring = Ring(k, es, "eSf", [128, D], F32, 2)
            from collections import deque
            wug_jobs = deque((bb, gg) for bb in range(NB) for gg in range(8))
            wug_pend = deque()

            def wug_prefetch():
                while len(wug_pend) < 2 and wug_jobs:
                    bb, gg = wug_jobs.popleft()
                    wug, wug_b = wug_ring.next()
                    for gv in range(2):
                        c0 = gv * DFF + gg * 512
                        k.dma("sp", wug[:, gv, :, :], s_wup[:, c0:c0 + 512].rearrange("(kk p) c -> p kk c", p=128),
                              writes=[wug_b], owner=wug_b)
                    wug_pend.append((wug, wug_b))

            wd_jobs = deque((bb, pp) for bb in range(NB) for pp in range(8))
            wd_pend = deque()

            def wd_prefetch():
                while len(wd_pend) < 2 and wd_jobs:
                    bb, piece = wd_jobs.popleft()
                    wd, wd_b = wd_ring.next()
                    k.dma("sp", wd[:], s_wdn[piece * 512:(piece + 1) * 512, :].rearrange("(i p) c -> p i c", p=128),
                          writes=[wd_b], owner=wd_b)
                    wd_pend.append((wd, wd_b))

            def load_h2(bb):
                h2, h2_b = h2_ring.next()
                k.dma("sp", h2[:], s_h2T[:, :, bb * 512:bb * 512 + 514], writes=[h2_b], owner=h2_b)
                return h2, h2_b

            deferred = deque()

            def emit_deferred():
                if deferred:
                    for fn in deferred.popleft():
                        fn()

            def make_tail(b, x2s, pts):
                tctx = [dict() for _ in range(4)]

                def A1(s):
                    c = tctx[s]
                    x2, x2_b, ss, ss_b = x2s[s]
                    pt, pt_b = pts[s]
                    x2b, x2b_b = x2b_ring.next()
                    k.op("act", lambda e: e.activation(out=x2b[:], in_=x2[:], func=AF.Copy), reads=[x2_b], writes=[x2b_b])
                    ptb, ptb_b = ptb_ring.next()
                    k.op("pool", lambda e: e.tensor_copy(out=ptb[:], in_=pt[:]), reads=[pt_b], writes=[ptb_b])
                    c["x2b"], c["x2b_b"], c["ptb"], c["ptb_b"] = x2b, x2b_b, ptb, ptb_b

                def A2(s):
                    c = tctx[s]
                    x2b, x2b_b, ptb, ptb_b = c["x2b"], c["x2b_b"], c["ptb"], c["ptb_b"]
                    bank, bank_b = k.bank()
                    bankb = bank.bitcast(BF16)
                    k.pe_begin(reads=[x2b_b, cst_b], writes=[bank_b])
                    ins = None
                    for kk in range(8):
                        ins = nc.tensor.transpose(out=bankb[:, kk * 128:(kk + 1) * 128], in_=x2b[:, kk * 128:(kk + 1) * 128],
                                                  identity=identb[:])
                    k.pe_end(ins, reads=[x2b_b, cst_b], writes=[bank_b])
                    c["bk1"] = (bankb, bank_b)
                    bank, bank_b = k.bank()
                    bankb = bank.bitcast(BF16)
                    k.pe_begin(reads=[ptb_b, cst_b], writes=[bank_b])
                    for kk in range(2):
                        ins = nc.tensor.transpose(out=bankb[:, kk * 128:(kk + 1) * 128], in_=ptb[:, kk * 128:(kk + 1) * 128],
                                                  identity=identb[:])
                    k.pe_end(ins, reads=[ptb_b, cst_b], writes=[bank_b])
                    c["bk2"] = (bankb, bank_b)

                def A3(s):
                    c = tctx[s]
                    x2T, x2T_b = x2T_ring.next()
                    bankb, bank_b = c["bk1"]
                    k.op("act", lambda e: e.activation(out=x2T[:], in_=bankb.rearrange("p (k t) -> p k t", k=8), func=AF.Copy),
                         reads=[bank_b], writes=[x2T_b])
                    pT, pT_b = pT_ring.next()
                    bankb, bank_b = c["bk2"]
                    k.op("act", lambda e: e.activation(out=pT[:], in_=bankb[:, 0:256].rearrange("p (k t) -> p k t", k=2),
                                                       func=AF.Copy), reads=[bank_b], writes=[pT_b])
                    c["x2T"], c["x2T_b"], c["pT"], c["pT_b"] = x2T, x2T_b, pT, pT_b

                def A4(s):
                    c = tctx[s]
                    x2T, x2T_b, pT, pT_b = c["x2T"], c["x2T_b"], c["pT"], c["pT_b"]
                    gbanks = []
                    ebanks = []
                    for half in range(2):
                        cs = slice(half * 512, (half + 1) * 512)
                        bank, bank_b = k.bank()
                        k.pe_begin(reads=[x2T_b, pw_b], writes=[bank_b])
                        ins = None
                        for kk in range(8):
                            ins = nc.tensor.matmul(bank, lhsT=x2T[:, kk, :], rhs=wg[:, kk, cs], start=(kk == 0), stop=(kk == 7))
                        k.pe_end(ins, reads=[x2T_b, pw_b], writes=[bank_b])
                        gbanks.append((bank, bank_b))
                        bank, bank_b = k.bank()
                        k.pe_begin(reads=[pT_b, pw_b], writes=[bank_b])
                        for kk in range(2):
                            ins = nc.tensor.matmul(bank, lhsT=pT[:, kk, :], rhs=wp[:, kk, cs], start=(kk == 0), stop=(kk == 1))
                        k.pe_end(ins, reads=[pT_b, pw_b], writes=[bank_b])
                        ebanks.append((bank, bank_b))
                    c["gbanks"], c["ebanks"] = gbanks, ebanks

                def A5(s):
                    c = tctx[s]
                    tg, tg_b = tg_ring.next()
                    eS, eS_b = eS_ring.next()
                    for half in range(2):
                        cs = slice(half * 512, (half + 1) * 512)
                        bank, bank_b = c["gbanks"][half]
                        k.op("act", lambda e: e.activation(out=tg[:, cs], in_=bank, func=AF.Tanh, scale=0.5),
                             reads=[bank_b], writes=[tg_b])
                        bank, bank_b = c["ebanks"][half]
                        k.op("act", lambda e: e.activation(out=eS[:, cs], in_=bank, func=AF.Copy),
                             reads=[bank_b], writes=[eS_b])
                    c["tg"], c["tg_b"], c["eS"], c["eS_b"] = tg, tg_b, eS, eS_b

                def A6(s):
                    c = tctx[s]
                    tg, tg_b, eS, eS_b = c["tg"], c["tg_b"], c["eS"], c["eS_b"]
                    k.op("dve", lambda e: e.scalar_tensor_tensor(out=tg[:], in0=tg[:], scalar=1.0, in1=eS[:],
                                                                 op0=ALU.add, op1=ALU.mult),
                         reads=[eS_b, tg_b], writes=[tg_b])

                def A7(s):
                    c = tctx[s]
                    x2, x2_b, ss, ss_b = x2s[s]
                    tg, tg_b = c["tg"], c["tg_b"]
                    k.op("act", lambda e: e.activation(out=junk[:], in_=tg[:], func=AF.Square, accum_out=ss[:, 5:6]),
                         reads=[tg_b], writes=[ss_b])

                def A8(s):
                    x2, x2_b, ss, ss_b = x2s[s]
                    rstd_ops(ss[:, 5:6], ss[:, 6:7], ss[:, 7:8], ss_b, 0.25 / D)

                def A9(s):
                    n = 4 * b + s
                    c = tctx[s]
                    x2, x2_b, ss, ss_b = x2s[s]
                    tg, tg_b = c["tg"], c["tg_b"]
                    k.op("dve", lambda e: e.scalar_tensor_tensor(out=tg[:], in0=tg[:], scalar=ss[:, 7:8], in1=wple[:],
                                                                 op0=ALU.mult, op1=ALU.mult),
                         reads=[tg_b, ss_b, p3_b], writes=[tg_b])
                    xo, xo_b = xo_ring.next()
                    k.op("dve", lambda e: e.scalar_tensor_tensor(out=xo[:], in0=tg[:], scalar=0.5, in1=x2[:],
                                                                 op0=ALU.mult, op1=ALU.add),
                         reads=[tg_b, x2_b], writes=[xo_b])
                    k.dma("pool", out_d[n * 128:(n + 1) * 128, :], xo[:], reads=[xo_b], owner=xo_b)

                def U(fn, s):
                    return lambda: fn(s)

                A1(0)
                if DEFER_MODE == 3:
                    units = []
                    for j in range(8):
                        u = []
                        s = j - 3
                        if 0 <= s < 4:
                            u += [U(A8, s), U(A9, s)]
                        s = j - 2
                        if 0 <= s < 4:
                            u += [U(A6, s), U(A7, s)]
                        s = j - 1
                        if 0 <= s < 4:
                            u += [U(A4, s), U(A5, s)]
                        s = j
                        if 0 <= s < 4:
                            u += [U(A2, s), U(A3, s)]
                        if 0 <= j + 1 < 4:
                            u += [U(A1, j + 1)]
                        units.append(u)
                    return units
                units = []
                for s in range(4):
                    if DEFER_MODE == 2:
                        units.append([U(A2, s), U(A3, s)])
                        units.append([U(A4, s), U(A5, s)])
                        units.append([U(A6, s)])
                        units.append([U(A7, s)])
                        units.append([U(A8, s)])
                        units.append([])
                        units.append([])
                    else:
                        for fn in (A2, A3, A4, A5, A6, A7, A8):
                            units.append([U(fn, s)])
                    if s < 3:
                        units.append([U(A9, s), U(A1, s + 1)])
                    else:
                        units.append([U(A9, s)])
                return units

            wug_prefetch()
            h2_next = load_h2(0)
            for b in range(NB):
                h2, h2_b = h2_next
                for g in range(8):
                    wug, wug_b = wug_pend.popleft()
                    for j in range(4):
                        c = 4 * g + j
                        for seg in range(2):
                            res = []
                            for gv in range(2):
                                bank, bank_b = k.bank()
                                k.pe_begin(reads=[wug_b, h2_b], writes=[bank_b])
                                ins = None
                                for kk in range(8):
                                    ins = nc.tensor.matmul(bank[:, 0:258], lhsT=wug[:, gv, kk, j * 128:(j + 1) * 128],
                                                           rhs=h2[:, kk, seg * 256:seg * 256 + 258], start=(kk == 0), stop=(kk == 7))
                                k.pe_end(ins, reads=[wug_b, h2_b], writes=[bank_b])
                                res.append((bank, bank_b))
                            yg, yg_b = yg_ring.next()
                            yv, yv_b = yv_ring.next()
                            conv3(res[0][0], res[0][1], c, yg, yg_b)
                            conv3p(res[1][0], res[1][1], 32 + c, yv, yv_b)
                            gl, gl_b = gl_ring.next()
                            k.op("act", lambda e: e.activation(out=gl[:], in_=yg[:], func=AF.Gelu_apprx_tanh),
                                 reads=[yg_b], writes=[gl_b])
                            k.op("pool", lambda e: e.tensor_tensor(out=gT[:, c, seg * 256:(seg + 1) * 256], in0=gl[:], in1=yv[:],
                                                                   op=ALU.mult), reads=[gl_b, yv_b], writes=[gT_bs[c // 4]])
                        if DEFER_MODE in (0, 2):
                            emit_deferred()
                    if DEFER_MODE == 1:
                        for _ in range(4):
                            emit_deferred()
                    if DEFER_MODE == 3:
                        emit_deferred()
                    wug_prefetch()
                    if g == 5:
                        wd_prefetch()
                    if b == 0 and g == 1:
                        for kk in range(8):
                            k.dma("pool", wg[:, kk, :], w_pg_d[kk * 128:(kk + 1) * 128, :], writes=[pw_b], owner=pw_b)
                        for kk in range(2):
                            k.dma("pool", wp[:, kk, :], w_pp_d[kk * 128:(kk + 1) * 128, :], writes=[pw_b], owner=pw_b)
                while deferred:
                    emit_deferred()
                if b + 1 < NB:
                    h2_next = load_h2(b + 1)
                dbanks = [[k.bank() for half in range(2)] for s in range(4)]
                allb = [dbanks[s][half][1] for s in range(4) for half in range(2)]
                for piece in range(8):
                    wd_prefetch()
                    wd, wd_b = wd_pend.popleft()
                    k.pe_begin(reads=[wd_b, gT_bs[piece]], writes=allb)
                    ins = None
                    for s in range(4):
                        for half in range(2):
                            for i in range(4):
                                ins = nc.tensor.matmul(dbanks[s][half][0], lhsT=gT[:, piece * 4 + i, s * 128:(s + 1) * 128],
                                                       rhs=wd[:, i, half * 512:(half + 1) * 512],
                                                       start=(piece == 0 and i == 0), stop=(piece == 7 and i == 3),
                                                       skip_group_check=True)
                    k.pe_end(ins, reads=[wd_b, gT_bs[piece]], writes=allb)
                    if piece < 6:
                        wd_prefetch()
                x2s = []
                pts = []
                x1s = []
                for s in range(4):
                    n = 4 * b + s
                    x1, x1_b = x1_ring.next()
                    k.dma("sp", x1[:], s_x1[n], writes=[x1_b], owner=x1_b)
                    pt, pt_b = pt_ring.next()
                    k.dma("sp", pt[:], p_d[n * 128:(n + 1) * 128, :], writes=[pt_b], owner=pt_b)
                    x2, x2_b = x2_ring.next()
                    ss, ss_b = ss_ring.next()
                    x1s.append((x1, x1_b))
                    pts.append((pt, pt_b))
                    x2s.append((x2, x2_b, ss, ss_b))
                for s in range(4):
                    x2, x2_b, ss, ss_b = x2s[s]
                    for half in range(2):
                        bank, bank_b = dbanks[s][half]
                        k.op("act", lambda e: e.activation(out=junk[:, 0:512], in_=bank, func=AF.Square,
                                                           accum_out=ss[:, half:half + 1]), reads=[bank_b], writes=[ss_b])
                for s in range(4):
                    x2, x2_b, ss, ss_b = x2s[s]
                    k.op("pool", lambda e: e.tensor_tensor(out=ss[:, 2:3], in0=ss[:, 0:1], in1=ss[:, 1:2], op=ALU.add),
                         reads=[ss_b], writes=[ss_b])
                    rstd_ops(ss[:, 2:3], ss[:, 3:4], ss[:, 4:5], ss_b, 1.0 / D)
                for s in range(4):
                    x2, x2_b, ss, ss_b = x2s[s]
                    for half in range(2):
                        bank, bank_b = dbanks[s][half]
                        cs = slice(half * 512, (half + 1) * 512)
                        k.op("dve", lambda e: e.scalar_tensor_tensor(out=x2[:, cs], in0=bank, scalar=ss[:, 4:5], in1=wfpost[:, cs],
                                                                     op0=ALU.mult, op1=ALU.mult),
                             reads=[bank_b, ss_b, p3_b], writes=[x2_b])
                for s in range(4):
                    x2, x2_b, ss, ss_b = x2s[s]
                    x1, x1_b = x1s[s]
                    k.op("pool", lambda e: e.tensor_tensor(out=x2[:], in0=x2[:], in1=x1[:], op=ALU.add),
                         reads=[x2_b, x1_b], writes=[x2_b])
                for u in make_tail(b, x2s, pts):
                    if DEFER:
                        deferred.append(u)
                    else:
                        for fn in u:
                            fn()
            while deferred:
                emit_deferred()
            k.end_phase()
    return nc


def make_consts():
    j = np.arange(128)[:, None]
    i = np.arange(128)[None, :]
    tri = np.zeros((128, 4, 128), np.float32)
    tri[:, 0, :] = (j <= i)
    tri[:, 1, :] = (j >= i)
    tri[:, 2, :] = (j > i)
    tri[:, 3, :] = (j < i)
    sel = np.zeros((4, 4, 128), np.float32)
    for h in range(4):
        sel[h, h, :] = 1.0
    return {
        "tri": tri,
        "identb": np.eye(128, dtype=np.float32).astype(ml_dtypes.bfloat16),
        "identf": np.eye(128, dtype=np.float32),
        "sel": sel,
    }


def make_shared(inp):
    f = lambda a: np.ascontiguousarray(np.asarray(a, dtype=np.float32))
    sh = dict(make_consts())
    sh["w_in"] = f(inp["w_in"][0])
    sh["wd_aug"] = f(np.stack([
        np.concatenate([inp["w_gla_decay_f"][0], inp["b_gla_decay_f"][0][None, :]], axis=0),
        np.concatenate([inp["w_gla_decay_b"][0], inp["b_gla_decay_b"][0][None, :]], axis=0)], axis=0))
    sh["mlcw"] = f(np.asarray(inp["ml_conv_w"][0]).T.reshape(8, 128, 3).transpose(1, 0, 2))
    sh["mlcb"] = f(np.asarray(inp["ml_conv_b"][0]).reshape(8, 128).T)
    ib = np.asarray(inp["ml_igate_b"][0])
    fb = np.asarray(inp["ml_fgate_b"][0])
    sh["gbias"] = f(np.stack([ib[0:4], ib[4:8], fb[0:4], fb[4:8]], axis=1))
    sh["ffcw"] = f(np.asarray(inp["ffn_conv_w"][0]).T.reshape(64, 128, 3).transpose(1, 0, 2))
    sh["ffcb"] = f(np.asarray(inp["ffn_conv_b"][0]).reshape(64, 128).T)
    sh["norms"] = f(np.stack([inp["norm_mix_pre"][0], inp["norm_mix_post"][0], inp["gla_norm"][0], inp["ml_norm"][0],
                              inp["norm_ffn_pre"][0], inp["norm_ffn_post"][0], inp["norm_ple_post"][0]], axis=0))
    sh["w_out"] = f(inp["w_out"][0])
    sh["w_up"] = f(inp["w_up"][0])
    sh["w_down"] = f(inp["w_down"][0])
    sh["w_pg"] = f(inp["w_ple_gate"][0])
    sh["w_pp"] = f(inp["w_ple_proj"][0])
    return sh


def kernel(**inputs):
    x = np.asarray(inputs["x"], dtype=np.float32)
    p = np.asarray(inputs["p"], dtype=np.float32)
    B, S, _ = x.shape
    sh = make_shared(inputs)
    nc = build(S)
    in_maps = []
    for b in range(B):
        m = dict(sh)
        m["x"] = np.ascontiguousarray(x[b])
        m["p"] = np.ascontiguousarray(p[0, b])
        in_maps.append(m)
    res = run_bass_kernel_spmd(nc, in_maps, core_ids=list(range(B)))
    return np.stack([np.asarray(r["out"], dtype=np.float32) for r in res.results], axis=0)
```
